# Optimizing a Trainium2 kernel written in Bass

```python
import math
import jax, jax.numpy as jnp
from jax import lax
import numpy as np

D_MODEL = 2048
BATCH = 1
SEQ = 8192
DEPTH = 4

HEAD_DIM = 64
S5_CH = D_MODEL // 4
S5_GROUP = 16
S5_NG = S5_CH // S5_GROUP
S5_P = 64
SB_HEADS = D_MODEL // 4 // HEAD_DIM
SB_W = SB_HEADS * HEAD_DIM
NSA_HEADS = D_MODEL // 2 // HEAD_DIM
NSA_KV = 4
NSA_REP = NSA_HEADS // NSA_KV
NSA_W = NSA_HEADS * HEAD_DIM
NSA_KV_W = NSA_KV * HEAD_DIM
CMP_LEN = 32
CMP_STRIDE = 16
CMP_HID = 128
SEL_BLOCK = 64
SEL_TOPN = 16
WINDOW = 512
Q_BLOCK = 128
FORCE_SCORE = 1e4
N_BUCKETS = 32
MAX_DIST = 1024
N_MEM = 256
XA_HEADS = 4
XA_HEAD_DIM = 128
XA_W = XA_HEADS * XA_HEAD_DIM
D_FF = 256 * ((8 * D_MODEL // 3 + 255) // 256)
CONV_W = 3
ALPHA = (2.0 * DEPTH) ** 0.25
BETA = (8.0 * DEPTH) ** -0.25
LN_EPS = 1e-5
IN_SIZES = (S5_CH, SB_W, SB_W, SB_W, NSA_W,
            NSA_KV_W, NSA_KV_W, NSA_KV_W, NSA_KV_W, NSA_KV_W, NSA_KV_W, 3 * NSA_HEADS)
N_IN = sum(IN_SIZES)
D_MIX = S5_CH + SB_W + NSA_W

kernel_name = "hybrid_s5_stickbreak_nsa_trunk"


def layer_norm(x, g, b):
    xf = x.astype(jnp.float32)
    mu = jnp.mean(xf, axis=-1, keepdims=True)
    var = jnp.mean(jnp.square(xf - mu), axis=-1, keepdims=True)
    return ((xf - mu) * lax.rsqrt(var + LN_EPS) * g + b).astype(x.dtype)


def masked_softmax(logits, mask):
    l = jnp.where(mask, logits.astype(jnp.float32), -1e30)
    m = jnp.max(l, axis=-1, keepdims=True)
    e = jnp.where(mask, jnp.exp(l - m), 0.0)
    return e / jnp.maximum(jnp.sum(e, axis=-1, keepdims=True), 1e-30)


def t5_bucket(dist):
    n = jnp.maximum(dist, 0)
    max_exact = N_BUCKETS // 2
    nf = jnp.maximum(n, 1).astype(jnp.float32)
    large = max_exact + (jnp.log(nf / max_exact) / math.log(MAX_DIST / max_exact)
                         * (N_BUCKETS - max_exact)).astype(jnp.int32)
    large = jnp.minimum(large, N_BUCKETS - 1)
    return jnp.where(n < max_exact, n, large)


def _complex_affine(e1, e2):
    a1r, a1i, b1r, b1i = e1
    a2r, a2i, b2r, b2i = e2
    return (a2r * a1r - a2i * a1i,
            a2r * a1i + a2i * a1r,
            a2r * b1r - a2i * b1i + b2r,
            a2r * b1i + a2i * b1r + b2i)


def s5_mixer(u, lam_re, lam_im, log_dt, b_re, b_im, c_re, c_im, d, w_glu, b_glu):
    bsz, t_len, _ = u.shape
    uf = u.astype(jnp.float32)
    ug = uf.reshape(bsz, t_len, S5_NG, S5_GROUP)
    lr = lam_re.astype(jnp.float32)
    li = lam_im.astype(jnp.float32)
    delta = jnp.exp(log_dt.astype(jnp.float32))[:, None]
    mag = jnp.exp(lr * delta)
    ar = mag * jnp.cos(li * delta)
    ai = mag * jnp.sin(li * delta)
    den = lr * lr + li * li
    cr = ((ar - 1.0) * lr + ai * li) / den
    ci = (ai * lr - (ar - 1.0) * li) / den
    br = cr[..., None] * b_re - ci[..., None] * b_im
    bi = cr[..., None] * b_im + ci[..., None] * b_re
    xr_in = jnp.einsum('btgc,gpc->btgp', ug, br)
    xi_in = jnp.einsum('btgc,gpc->btgp', ug, bi)
    a_r = jnp.broadcast_to(ar, xr_in.shape)
    a_i = jnp.broadcast_to(ai, xr_in.shape)
    _, _, sr, si = lax.associative_scan(_complex_affine, (a_r, a_i, xr_in, xi_in), axis=1)
    y = jnp.einsum('btgp,gcp->btgc', sr, c_re) - jnp.einsum('btgp,gcp->btgc', si, c_im)
    y = y.reshape(bsz, t_len, S5_CH) + d * uf
    z = jax.nn.gelu(y)
    out = z * jax.nn.sigmoid(z @ w_glu + b_glu)
    return out.astype(u.dtype)


def stick_breaking(q, k, v):
    bsz, t_len, n_h, hd = q.shape
    scale = hd ** -0.5
    kpos = jnp.arange(t_len)

    def block(q0):
        qb = lax.dynamic_slice_in_dim(q, q0, Q_BLOCK, axis=1)
        z = jnp.einsum('bqhd,bkhd->bhqk', qb, k).astype(jnp.float32) * scale
        t = q0 + jnp.arange(Q_BLOCK)
        mask = kpos[None, :] < t[:, None]
        log1m = jnp.where(mask, jax.nn.log_sigmoid(-z), 0.0)
        later = lax.cumsum(log1m, axis=3, reverse=True) - log1m
        w = jnp.where(mask, jnp.exp(jax.nn.log_sigmoid(z) + later), 0.0)
        o = jnp.einsum('bhqk,bkhd->bqhd', w.astype(v.dtype), v)
        return o.reshape(bsz, Q_BLOCK, n_h * hd)

    starts = jnp.arange(t_len // Q_BLOCK) * Q_BLOCK
    o = lax.map(block, starts)
    return o.transpose(1, 0, 2, 3).reshape(bsz, t_len, n_h * hd)


def nsa_mixer(q, k_cmp, v_cmp, k_slc, v_slc, k_swa, v_swa, gates,
              cmp_pos, cmp_w1, cmp_w2, rel_bias):
    bsz, t_len, n_h, hd = q.shape
    n_g, n_r = NSA_KV, NSA_REP
    scale = hd ** -0.5
    n_cmp = (t_len - CMP_LEN) // CMP_STRIDE + 1
    n_sel = t_len // SEL_BLOCK
    top_n = min(SEL_TOPN, n_sel)

    cmp_start = jnp.arange(n_cmp) * CMP_STRIDE
    idx = cmp_start[:, None] + jnp.arange(CMP_LEN)[None, :]

    def compress(kv, j):
        blk = kv[:, idx] + cmp_pos[j][None, None, :, None, :]
        blk = blk.transpose(0, 1, 3, 2, 4).reshape(bsz, n_cmp, n_g, CMP_LEN * hd)
        return jax.nn.gelu(blk @ cmp_w1[j]) @ cmp_w2[j]

    kc = compress(k_cmp, 0)
    vc = compress(v_cmp, 1)
    cmp_end = cmp_start + CMP_LEN - 1
    sel_ids = jnp.arange(n_sel)
    overlap = ((cmp_start[:, None] < (sel_ids[None, :] + 1) * SEL_BLOCK)
               & (cmp_end[:, None] >= sel_ids[None, :] * SEL_BLOCK)).astype(jnp.float32)

    kb = k_slc.reshape(bsz, n_sel, SEL_BLOCK, n_g, hd).transpose(0, 3, 1, 2, 4)
    vb = v_slc.reshape(bsz, n_sel, SEL_BLOCK, n_g, hd).transpose(0, 3, 1, 2, 4)
    pad = ((0, 0), (WINDOW, 0), (0, 0), (0, 0))
    kw = jnp.pad(k_swa, pad)
    vw = jnp.pad(v_swa, pad)

    tab = rel_bias.astype(jnp.float32)
    tab_g = tab.reshape(N_BUCKETS, n_g, n_r).transpose(1, 0, 2)
    b_idx = jnp.arange(bsz)[:, None, None, None]
    g_idx = jnp.arange(n_g)[None, :, None, None]

    def head_bias(dist):
        return tab[t5_bucket(dist)].reshape(*dist.shape, n_g, n_r).transpose(2, 3, 0, 1)

    def block(q0):
        t = q0 + jnp.arange(Q_BLOCK)
        qb = lax.dynamic_slice_in_dim(q, q0, Q_BLOCK, axis=1).reshape(bsz, Q_BLOCK, n_g, n_r, hd)
        dist_c = t[:, None] - cmp_end[None, :]
        s_c = jnp.einsum('bqgrd,bngd->bgrqn', qb, kc).astype(jnp.float32) * scale + head_bias(dist_c)
        p_c = masked_softmax(s_c, dist_c >= 0)
        o_c = jnp.einsum('bgrqn,bngd->bqgrd', p_c.astype(vc.dtype), vc)
        imp = jnp.einsum('bgrqn,nj->bgqj', p_c, overlap)
        cur = t // SEL_BLOCK
        forced = ((sel_ids[None, :] == 0) | (sel_ids[None, :] == cur[:, None])
                  | (sel_ids[None, :] == cur[:, None] - 1))
        valid = sel_ids[None, :] <= cur[:, None]
        score = jnp.where(valid, jnp.where(forced, FORCE_SCORE, imp), -1.0)
        _, sel = lax.top_k(score, top_n)
        ks = kb[b_idx, g_idx, sel].reshape(bsz, n_g, Q_BLOCK, top_n * SEL_BLOCK, hd)
        vs = vb[b_idx, g_idx, sel].reshape(bsz, n_g, Q_BLOCK, top_n * SEL_BLOCK, hd)
        pos_s = sel[..., None] * SEL_BLOCK + jnp.arange(SEL_BLOCK)
        dist_s = (t[:, None, None] - pos_s).reshape(bsz, n_g, Q_BLOCK, top_n * SEL_BLOCK)
        bias_s = tab_g[g_idx, t5_bucket(dist_s)].transpose(0, 1, 4, 2, 3)
        s_s = jnp.einsum('bqgrd,bgqmd->bgrqm', qb, ks).astype(jnp.float32) * scale + bias_s
        p_s = masked_softmax(s_s, (dist_s >= 0)[:, :, None])
        o_s = jnp.einsum('bgrqm,bgqmd->bqgrd', p_s.astype(vs.dtype), vs)
        kwb = lax.dynamic_slice_in_dim(kw, q0, WINDOW + Q_BLOCK, axis=1)
        vwb = lax.dynamic_slice_in_dim(vw, q0, WINDOW + Q_BLOCK, axis=1)
        pos_w = q0 - WINDOW + jnp.arange(WINDOW + Q_BLOCK)
        dist_w = t[:, None] - pos_w[None, :]
        mask_w = (dist_w >= 0) & (dist_w < WINDOW) & (pos_w[None, :] >= 0)
        s_w = jnp.einsum('bqgrd,bkgd->bgrqk', qb, kwb).astype(jnp.float32) * scale + head_bias(dist_w)
        p_w = masked_softmax(s_w, mask_w)
        o_w = jnp.einsum('bgrqk,bkgd->bqgrd', p_w.astype(vwb.dtype), vwb)
        gb = lax.dynamic_slice_in_dim(gates, q0, Q_BLOCK, axis=1).reshape(bsz, Q_BLOCK, n_g, n_r, 3)
        o = gb[..., 0:1] * o_c + gb[..., 1:2] * o_s + gb[..., 2:3] * o_w
        return o.reshape(bsz, Q_BLOCK, n_h * hd)

    starts = jnp.arange(t_len // Q_BLOCK) * Q_BLOCK
    o = lax.map(block, starts)
    return o.transpose(1, 0, 2, 3).reshape(bsz, t_len, n_h * hd)


def hybrid_mixer(h, w_in, w_out, lam_re, lam_im, log_dt, b_re, b_im, c_re, c_im, d,
                 w_glu, b_glu, cmp_pos, cmp_w1, cmp_w2, rel_bias):
    bsz, t_len, _ = h.shape
    proj = h @ w_in
    offsets = tuple(int(o) for o in np.cumsum(IN_SIZES)[:-1])
    (u5, sq, sk, sv, nq, kc, vc, ksl, vsl, ksw, vsw, g) = jnp.split(proj, offsets, axis=-1)

    def heads(a, n):
        return a.reshape(bsz, t_len, n, HEAD_DIM)

    y_s5 = s5_mixer(u5, lam_re, lam_im, log_dt, b_re, b_im, c_re, c_im, d, w_glu, b_glu)
    y_sb = stick_breaking(heads(sq, SB_HEADS), heads(sk, SB_HEADS), heads(sv, SB_HEADS))
    gates = jax.nn.sigmoid(g.reshape(bsz, t_len, NSA_HEADS, 3))
    y_nsa = nsa_mixer(heads(nq, NSA_HEADS), heads(kc, NSA_KV), heads(vc, NSA_KV),
                      heads(ksl, NSA_KV), heads(vsl, NSA_KV), heads(ksw, NSA_KV), heads(vsw, NSA_KV),
                      gates, cmp_pos, cmp_w1, cmp_w2, rel_bias)
    return jnp.concatenate([y_s5, y_sb, y_nsa], axis=-1) @ w_out


def memory_cross_attention(h, mem, wq, wkv, wo):
    bsz, t_len, _ = h.shape
    n_mem = mem.shape[1]
    q = (h @ wq).reshape(bsz, t_len, XA_HEADS, XA_HEAD_DIM)
    k, v = jnp.split(mem @ wkv, 2, axis=-1)
    k = k.reshape(bsz, n_mem, XA_HEADS, XA_HEAD_DIM)
    v = v.reshape(bsz, n_mem, XA_HEADS, XA_HEAD_DIM)
    s = jnp.einsum('bqhd,bmhd->bhqm', q, k).astype(jnp.float32) * (XA_HEAD_DIM ** -0.5)
    p = jax.nn.softmax(s, axis=-1)
    o = jnp.einsum('bhqm,bmhd->bqhd', p.astype(v.dtype), v).reshape(bsz, t_len, XA_W)
    return o @ wo


def conv_ffn(h, w_up, conv_w, conv_b, w_down):
    u = h @ w_up
    ch = u.shape[-1]
    u = lax.conv_general_dilated(u, conv_w[:, None, :], window_strides=(1,),
                                 padding=((CONV_W - 1, 0),),
                                 dimension_numbers=('NWC', 'WIO', 'NWC'),
                                 feature_group_count=ch) + conv_b
    a, g = jnp.split(u, 2, axis=-1)
    return (a * jax.nn.gelu(g)) @ w_down


def setup_inputs(seed: int = 0) -> dict:
    key = jax.random.key(seed)
    ks = iter(jax.random.split(key, 32))
    f32 = jnp.float32
    L = DEPTH

    def nrm(shape, scale):
        return scale * jax.random.normal(next(ks), shape, f32)

    lam_im0 = math.pi * jnp.arange(S5_P, dtype=f32)
    return {
        "x": nrm((BATCH, SEQ, D_MODEL), 1.0),
        "mem": nrm((BATCH, N_MEM, D_MODEL), 1.0),
        "w_in": nrm((L, D_MODEL, N_IN), D_MODEL ** -0.5),
        "w_out": nrm((L, D_MIX, D_MODEL), BETA * D_MIX ** -0.5),
        "s5_lambda_re": -0.5 + nrm((L, S5_NG, S5_P), 0.01),
        "s5_lambda_im": lam_im0 + nrm((L, S5_NG, S5_P), 0.01),
        "s5_log_dt": jax.random.uniform(next(ks), (L, S5_NG), f32, math.log(0.001), math.log(0.1)),
        "s5_b_re": nrm((L, S5_NG, S5_P, S5_GROUP), (2 * S5_GROUP) ** -0.5),
        "s5_b_im": nrm((L, S5_NG, S5_P, S5_GROUP), (2 * S5_GROUP) ** -0.5),
        "s5_c_re": nrm((L, S5_NG, S5_GROUP, S5_P), (2 * S5_P) ** -0.5),
        "s5_c_im": nrm((L, S5_NG, S5_GROUP, S5_P), (2 * S5_P) ** -0.5),
        "s5_d": nrm((L, S5_CH), 1.0),
        "s5_w_glu": nrm((L, S5_CH, S5_CH), S5_CH ** -0.5),
        "s5_b_glu": nrm((L, S5_CH), 0.01),
        "nsa_cmp_pos": nrm((L, 2, CMP_LEN, HEAD_DIM), 0.02),
        "nsa_cmp_w1": nrm((L, 2, CMP_LEN * HEAD_DIM, CMP_HID), (CMP_LEN * HEAD_DIM) ** -0.5),
        "nsa_cmp_w2": nrm((L, 2, CMP_HID, HEAD_DIM), CMP_HID ** -0.5),
        "rel_bias": nrm((N_BUCKETS, NSA_HEADS), 0.1),
        "xa_wq": nrm((L, D_MODEL, XA_W), D_MODEL ** -0.5),
        "xa_wkv": nrm((L, D_MODEL, 2 * XA_W), D_MODEL ** -0.5),
        "xa_wo": nrm((L, XA_W, D_MODEL), BETA * XA_W ** -0.5),
        "ffn_w_up": nrm((L, D_MODEL, 2 * D_FF), D_MODEL ** -0.5),
        "ffn_conv_w": nrm((L, CONV_W, 2 * D_FF), CONV_W ** -0.5),
        "ffn_conv_b": nrm((L, 2 * D_FF), 0.01),
        "ffn_w_down": nrm((L, D_FF, D_MODEL), BETA * D_FF ** -0.5),
        "ln_g": 1.0 + nrm((L, 3, D_MODEL), 0.01),
        "ln_b": nrm((L, 3, D_MODEL), 0.01),
    }


def reference(x, mem, w_in, w_out, s5_lambda_re, s5_lambda_im, s5_log_dt, s5_b_re, s5_b_im,
              s5_c_re, s5_c_im, s5_d, s5_w_glu, s5_b_glu, nsa_cmp_pos, nsa_cmp_w1, nsa_cmp_w2,
              rel_bias, xa_wq, xa_wkv, xa_wo, ffn_w_up, ffn_conv_w, ffn_conv_b, ffn_w_down,
              ln_g, ln_b):
    h = x
    for l in range(DEPTH):
        mix = hybrid_mixer(h, w_in[l], w_out[l], s5_lambda_re[l], s5_lambda_im[l], s5_log_dt[l],
                           s5_b_re[l], s5_b_im[l], s5_c_re[l], s5_c_im[l], s5_d[l],
                           s5_w_glu[l], s5_b_glu[l], nsa_cmp_pos[l], nsa_cmp_w1[l], nsa_cmp_w2[l],
                           rel_bias)
        h = layer_norm(ALPHA * h + mix, ln_g[l, 0], ln_b[l, 0])
        xa = memory_cross_attention(h, mem, xa_wq[l], xa_wkv[l], xa_wo[l])
        h = layer_norm(ALPHA * h + xa, ln_g[l, 1], ln_b[l, 1])
        ff = conv_ffn(h, ffn_w_up[l], ffn_conv_w[l], ffn_conv_b[l], ffn_w_down[l])
        h = layer_norm(ALPHA * h + ff, ln_g[l, 2], ln_b[l, 2])
    return h
```

```python
import math

import numpy as np
from contextlib import ExitStack
import concourse.bass as bass
import concourse.mybir as mybir
from concourse.bass_utils import run_bass_kernel_spmd
from concourse.bass_types import AP

F32 = mybir.dt.float32
BF16 = mybir.dt.bfloat16
I32 = mybir.dt.int32
AF = mybir.ActivationFunctionType
ALU = mybir.AluOpType
AX = mybir.AxisListType

NDMA_SEMS = 24


class K:
    def __init__(self):
        self.nc = bass.Bass("TRN2", target_bir_lowering=False)
        nc = self.nc
        self.es = ExitStack()
        self.eng = {"pe": nc.tensor, "dve": nc.vector, "act": nc.scalar, "pool": nc.gpsimd, "sp": nc.sync}
        self.sem = {e: self.es.enter_context(nc.semaphore("s_" + e)) for e in self.eng}
        self.cnt = {e: 0 for e in self.eng}
        self.waited = {}
        self.dsem = [self.es.enter_context(nc.semaphore("d%d" % i)) for i in range(NDMA_SEMS)]
        self.dcnt = [0] * NDMA_SEMS
        self.dnext = 0
        self.lastw = {}
        self.readers = {}
        self.ninstr = 0
        self.out_tokens = []

    def sb(self, name, shape, dt=F32):
        self.nalloc = getattr(self, "nalloc", 0) + 1
        return self.es.enter_context(self.nc.sbuf_tensor("%s_%d" % (name, self.nalloc), list(shape), dt))

    def ps(self, name, shape, dt=F32):
        return self.es.enter_context(self.nc.psum_tensor(name, list(shape), dt))

    def dram_in(self, name, shape, dt=F32):
        return self.nc.dram_tensor(name, list(shape), dt, kind="ExternalInput").ap()

    def dram_out(self, name, shape, dt=F32):
        return self.nc.dram_tensor(name, list(shape), dt, kind="ExternalOutput").ap()

    def dram_tmp(self, name, shape, dt=F32):
        return self.nc.dram_tensor(name, list(shape), dt, kind="Internal").ap()

    def fill(self, v):
        if not hasattr(self, "_fills"):
            self._fills = {}
        if v not in self._fills:
            self._fills[v] = self.nc.gpsimd.to_reg(float(v))
        return self._fills[v]

    def _wait(self, e, tok):
        kind, src, val = tok
        k = (e, kind, src)
        if self.waited.get(k, 0) >= val:
            return
        self.waited[k] = val
        sem = self.sem[src] if kind == "e" else self.dsem[src]
        self.eng[e].wait_ge(sem, val)

    def _deps(self, e, reads, writes):
        toks = []
        for r in reads:
            if r in self.lastw:
                toks.append(self.lastw[r])
        for w in writes:
            if w in self.lastw:
                toks.append(self.lastw[w])
            toks.extend(self.readers.get(w, ()))
        for t in toks:
            if t[0] == "e" and t[1] == e and e == "pe":
                continue
            self._wait(e, t)

    def _commit(self, tok, reads, writes):
        for w in writes:
            self.lastw[w] = tok
            self.readers[w] = []
        for r in reads:
            if r not in writes:
                self.readers.setdefault(r, []).append(tok)
                if len(self.readers[r]) > 64:
                    self.readers[r] = self._compact(self.readers[r])

    @staticmethod
    def _compact(toks):
        best = {}
        for t in toks:
            k = (t[0], t[1])
            if k not in best or best[k][2] < t[2]:
                best[k] = t
        return list(best.values())

    def op(self, e, fn, reads=(), writes=()):
        self._deps(e, reads, writes)
        ins = fn(self.eng[e])
        self.cnt[e] += 1
        ins.then_inc(self.sem[e], 1)
        tok = ("e", e, self.cnt[e])
        self._commit(tok, reads, writes)
        self.ninstr += 1
        return tok

    def dma(self, e, out, in_, reads=(), writes=(), is_output=False, **kw):
        k = self.dnext
        self.dnext = (self.dnext + 1) % NDMA_SEMS
        if self.dcnt[k] > 0:
            self._wait(e, ("d", k, self.dcnt[k]))
        self._deps(e, reads, writes)
        ins = self.eng[e].dma_start(out=out, in_=in_, **kw)
        self.dcnt[k] += 16
        ins.then_inc(self.dsem[k], 16)
        tok = ("d", k, self.dcnt[k])
        self._commit(tok, reads, writes)
        self.ninstr += 1
        if is_output:
            self.out_tokens.append(tok)
        return tok

    def finish(self):
        for k in range(NDMA_SEMS):
            if self.dcnt[k] > 0:
                self._wait("sp", ("d", k, self.dcnt[k]))
        for e in self.eng:
            if e != "sp" and self.cnt[e] > 0:
                self._wait("sp", ("e", e, self.cnt[e]))
        self.es.close()
        return self.nc


def _barrier(self):
    for e in self.eng:
        for s in self.eng:
            if s != e and self.cnt[s] > 0:
                self._wait(e, ("e", s, self.cnt[s]))
        for q in range(NDMA_SEMS):
            if self.dcnt[q] > 0:
                self._wait(e, ("d", q, self.dcnt[q]))


class _Scope:
    def __init__(self, k):
        self.k = k

    def __enter__(self):
        self.saved = self.k.es
        self.k.es = ExitStack()
        return self

    def __exit__(self, *a):
        self.k.barrier()
        self.k.es.close()
        self.k.es = self.saved
        return False


K.barrier = _barrier
K.scope = lambda self: _Scope(self)


ALPHA = (2.0 * 4) ** 0.25
LN_EPS = 1e-5
D = 2048
DFF = 5632
NIN = 4656
NT = 9
TOK = NT * 128
XA_SCALE = 128 ** -0.5


def build_dense(do_chain, do_proj):
    k = K()
    nc = k.nc
    h_in = k.dram_in("h_in", [TOK, D])
    if do_chain:
        y5T = k.dram_in("y5T", [512, TOK])
        ysbT = k.dram_in("ysbT", [512, TOK])
        ynsaT = k.dram_in("ynsaT", [1024, TOK])
        w_glu = k.dram_in("w_glu", [512, 512])
        b_glu = k.dram_in("b_glu", [128, 4])
        w_out = k.dram_in("w_out", [D, D])
        ln_g = k.dram_in("ln_g", [3, D])
        ln_b = k.dram_in("ln_b", [3, D])
        memT = k.dram_in("memT", [D, 256])
        xa_wq = k.dram_in("xa_wq", [D, 512])
        xa_wkv = k.dram_in("xa_wkv", [D, 1024])
        xa_wo = k.dram_in("xa_wo", [512, D])
        w_up = k.dram_in("w_up", [D, 2 * DFF])
        convw = k.dram_in("convw", [128, 88, 3])
        convb = k.dram_in("convb", [128, 88])
        w_down = k.dram_in("w_down", [DFF, D])
        flag = k.dram_in("flag", [128, 1])
        h_out = k.dram_out("h_out", [1024, D])
    if do_proj:
        w_in = k.dram_in("w_in", [D, NIN])
        proj = k.dram_out("proj", [1024, NIN])

    H = k.sb("H", [128, NT, D])
    HT = k.sb("HT", [128, 16, TOK], BF16)
    WB = [k.sb("WB%d" % i, [128, 16, 256], BF16) for i in range(2)]
    ident = k.sb("ident", [128, 128])
    psb = [k.ps("ps%d" % i, [128, 512]) for i in range(8)]
    st = {"p": 0, "w": 0}

    def ps():
        i = st["p"]
        st["p"] = (i + 1) % 8
        return psb[i], "ps%d" % i

    def wb():
        i = st["w"]
        st["w"] = (i + 1) % 2
        return WB[i], "WB%d" % i

    k.op("pool", lambda e: e.memset(ident[:], 1.0), writes=["ident"])
    k.op("pool", lambda e: e.affine_select(out=ident[:], in_=ident[:], pattern=[[-1, 128]], compare_op=ALU.is_equal,
                                            fill=k.fill(0.0), base=0, channel_multiplier=1), reads=["ident"], writes=["ident"])
    for i in range(NT):
        k.dma("sp", H[:, i, :], h_in[i * 128:(i + 1) * 128, :], writes=["H%d" % i])

    def ht_keys(c0, c1):
        return ["HT%d" % i for i in range(c0 // 128, (c1 - 1) // 128 + 1)]

    def transposes(i):
        for kb in range(4):
            p, pk = ps()
            for q in range(4):
                kk = kb * 4 + q
                k.op("pe", lambda e: e.transpose(out=p[:, q * 128:(q + 1) * 128], in_=H[:, i, kk * 128:(kk + 1) * 128],
                                                 identity=ident[:]), reads=["H%d" % i, "ident"], writes=[pk])
            k.op("act", lambda e: e.activation(out=HT[:, kb * 4:(kb + 1) * 4, i * 128:(i + 1) * 128],
                                               in_=p[:].rearrange("p (a b) -> p a b", a=4), func=AF.Copy),
                 reads=[pk], writes=["HT%d" % i])

    def layer_norm(i, GB, stat, mv, sd):
        x = H[:, i, :]
        hk = "H%d" % i
        for c in range(4):
            k.op("dve", lambda e: e.bn_stats(out=stat[:, c, :], in_=H[:, i, c * 512:(c + 1) * 512]), reads=[hk], writes=["stat"])
        k.op("dve", lambda e: e.bn_aggr(out=mv[:], in_=stat[:].rearrange("p a b -> p (a b)")), reads=["stat"], writes=["mv"])
        k.op("dve", lambda e: e.tensor_scalar(out=sd[:], in0=mv[:, 1:2], scalar1=LN_EPS, scalar2=None, op0=ALU.add),
             reads=["mv"], writes=["sd"])
        k.op("act", lambda e: e.activation(out=sd[:], in_=sd[:], func=AF.Sqrt), reads=["sd"], writes=["sd"])
        k.op("dve", lambda e: e.reciprocal(out=sd[:], in_=sd[:]), reads=["sd"], writes=["sd"])
        k.op("dve", lambda e: e.tensor_scalar(out=x, in0=x, scalar1=mv[:, 0:1], scalar2=sd[:, 0:1], op0=ALU.subtract, op1=ALU.mult),
             reads=[hk, "mv", "sd"], writes=[hk])
        k.op("pool", lambda e: e.tensor_tensor(out=x, in0=x, in1=GB[:, 0, :], op=ALU.mult), reads=[hk, "GB"], writes=[hk])
        k.op("pool", lambda e: e.tensor_tensor(out=x, in0=x, in1=GB[:, 1, :], op=ALU.add), reads=[hk, "GB"], writes=[hk])

    def load_gb(GB, which):
        k.dma("sp", GB[:, 0, :], ln_g[which, :].partition_broadcast(128), writes=["GB"])
        k.dma("sp", GB[:, 1, :], ln_b[which, :].partition_broadcast(128), writes=["GB"])

    def resid(i, c0, c1, p, pk, first=True):
        x = H[:, i, c0:c1]
        if first:
            k.op("dve", lambda e: e.scalar_tensor_tensor(out=x, in0=x, scalar=ALPHA, in1=p, op0=ALU.mult, op1=ALU.add),
                 reads=["H%d" % i, pk], writes=["H%d" % i])
        else:
            k.op("dve", lambda e: e.tensor_tensor(out=x, in0=x, in1=p, op=ALU.add), reads=["H%d" % i, pk], writes=["H%d" % i])

    if do_chain:
        with k.scope():
            CATT = k.sb("CATT", [128, 16, TOK], BF16)
            Y5 = k.sb("Y5", [128, 4, 384])
            Z = k.sb("Z", [128, 4, 384])
            ZB = k.sb("ZB", [128, 4, 384], BF16)
            SG = k.sb("SG", [128, 384])
            WGLU = k.sb("WGLU", [128, 4, 512], BF16)
            BGLU = k.sb("BGLU", [128, 4])
            GB = k.sb("GB", [128, 2, D])
            stat = k.sb("stat", [128, 4, 6])
            mv = k.sb("mv", [128, 2])
            sd = k.sb("sd", [128, 1])
            load_gb(GB, 0)
            k.dma("pool", WGLU[:], w_glu.rearrange("(c p) n -> p c n", p=128), writes=["WGLU"])
            k.dma("sp", BGLU[:], b_glu[:, :], writes=["BGLU"])
            k.dma("pool", CATT[:, 4:8, :], ysbT.rearrange("(c p) t -> p c t", p=128), writes=["CA_sb"])
            k.dma("pool", CATT[:, 8:16, :], ynsaT.rearrange("(c p) t -> p c t", p=128), writes=["CA_nsa"])
            y5v = y5T.rearrange("(c p) t -> p c t", p=128)
            for tc in range(3):
                cs = slice(tc * 384, (tc + 1) * 384)
                k.dma("sp", Y5[:], y5v[:, :, cs], writes=["Y5"])
                k.op("act", lambda e: e.activation(out=Z[:], in_=Y5[:], func=AF.Gelu_apprx_tanh), reads=["Y5"], writes=["Z"])
                k.op("pool", lambda e: e.tensor_copy(out=ZB[:], in_=Z[:]), reads=["Z"], writes=["ZB"])
                for co in range(4):
                    p, pk = ps()
                    for ci in range(4):
                        k.op("pe", lambda e: e.matmul(p[:, 0:384], lhsT=WGLU[:, ci, co * 128:(co + 1) * 128], rhs=ZB[:, ci, :],
                                                      start=(ci == 0), stop=(ci == 3)), reads=["WGLU", "ZB"], writes=[pk])
                    k.op("act", lambda e: e.activation(out=SG[:], in_=p[:, 0:384], func=AF.Sigmoid, bias=BGLU[:, co:co + 1]),
                         reads=[pk, "BGLU"], writes=["SG"])
                    k.op("dve", lambda e: e.tensor_tensor(out=CATT[:, co, cs], in0=Z[:, co, :], in1=SG[:], op=ALU.mult),
                         reads=["Z", "SG"], writes=["CA_s5_%d" % tc])
            for n in range(8):
                w, wk = wb()
                k.dma("pool", w[:], w_out[:, n * 256:(n + 1) * 256].rearrange("(k p) n -> p k n", p=128), writes=[wk])
                for i in range(NT):
                    p, pk = ps()
                    for kk in range(16):
                        k.op("pe", lambda e: e.matmul(p[:, 0:256], lhsT=CATT[:, kk, i * 128:(i + 1) * 128], rhs=w[:, kk, :],
                                                      start=(kk == 0), stop=(kk == 15)),
                             reads=[wk, "CA_s5_%d" % (i // 3), "CA_sb", "CA_nsa"], writes=[pk])
                    resid(i, n * 256, (n + 1) * 256, p[:, 0:256], pk)
            for i in range(NT):
                layer_norm(i, GB, stat, mv, sd)
                transposes(i)

        with k.scope():
            MEMT = k.sb("MEMT", [128, 16, 256], BF16)
            KT = k.sb("KT", [128, 4, 256], BF16)
            V = k.sb("V", [128, 2, 512], BF16)
            QT = k.sb("QT", [128, 384], BF16)
            PT = k.sb("PT", [128, 2, 384], BF16)
            RZ = k.sb("RZ", [128, 384])
            OT = k.sb("OT", [128, 4, TOK], BF16)
            WO = k.sb("WO", [128, 4, D], BF16)
            ONES = k.sb("ONES", [128, 128], BF16)
            GB = k.sb("GB", [128, 2, D])
            stat = k.sb("stat", [128, 4, 6])
            mv = k.sb("mv", [128, 2])
            sd = k.sb("sd", [128, 1])
            load_gb(GB, 1)
            k.op("pool", lambda e: e.memset(ONES[:], 1.0), writes=["ONES"])
            k.dma("pool", MEMT[:], memT.rearrange("(k p) m -> p k m", p=128), writes=["MEMT"])
            k.dma("pool", WO[:], xa_wo.rearrange("(h p) n -> p h n", p=128), writes=["WO"])
            for c in range(4):
                w, wk = wb()
                k.dma("pool", w[:], xa_wkv[:, c * 256:(c + 1) * 256].rearrange("(k p) n -> p k n", p=128), writes=[wk])
                if c < 2:
                    for hh in range(2):
                        p, pk = ps()
                        for kk in range(16):
                            k.op("pe", lambda e: e.matmul(p[:, 0:256], lhsT=w[:, kk, hh * 128:(hh + 1) * 128], rhs=MEMT[:, kk, :],
                                                          start=(kk == 0), stop=(kk == 15)), reads=[wk, "MEMT"], writes=[pk])
                        k.op("act", lambda e: e.activation(out=KT[:, 2 * c + hh, :], in_=p[:, 0:256], func=AF.Copy),
                             reads=[pk], writes=["KT"])
                else:
                    for mt in range(2):
                        p, pk = ps()
                        for kk in range(16):
                            k.op("pe", lambda e: e.matmul(p[:, 0:256], lhsT=MEMT[:, kk, mt * 128:(mt + 1) * 128], rhs=w[:, kk, :],
                                                          start=(kk == 0), stop=(kk == 15)), reads=[wk, "MEMT"], writes=[pk])
                        k.op("act", lambda e: e.activation(out=V[:, mt, (c - 2) * 256:(c - 1) * 256], in_=p[:, 0:256], func=AF.Copy),
                             reads=[pk], writes=["V"])
            for c in range(2):
                w, wk = wb()
                k.dma("pool", w[:], xa_wq[:, c * 256:(c + 1) * 256].rearrange("(k p) n -> p k n", p=128), writes=[wk])
                for tc in range(3):
                    cs = slice(tc * 384, (tc + 1) * 384)
                    for hh in range(2):
                        h = 2 * c + hh
                        p, pk = ps()
                        for kk in range(16):
                            k.op("pe", lambda e: e.matmul(p[:, 0:384], lhsT=w[:, kk, hh * 128:(hh + 1) * 128], rhs=HT[:, kk, cs],
                                                          start=(kk == 0), stop=(kk == 15)),
                                 reads=[wk] + ht_keys(tc * 384, tc * 384 + 384), writes=[pk])
                        k.op("act", lambda e: e.activation(out=QT[:], in_=p[:, 0:384], func=AF.Copy), reads=[pk], writes=["QT"])
                        for mt in range(2):
                            p2, pk2 = ps()
                            k.op("pe", lambda e: e.matmul(p2[:, 0:384], lhsT=KT[:, h, mt * 128:(mt + 1) * 128], rhs=QT[:],
                                                          start=True, stop=True), reads=["KT", "QT"], writes=[pk2])
                            k.op("act", lambda e: e.activation(out=PT[:, mt, :], in_=p2[:, 0:384], func=AF.Exp, scale=XA_SCALE),
                                 reads=[pk2], writes=["PT%d" % mt])
                        po, pko = ps()
                        pz, pkz = ps()
                        for mt in range(2):
                            k.op("pe", lambda e: e.matmul(po[:, 0:384], lhsT=V[:, mt, h * 128:(h + 1) * 128], rhs=PT[:, mt, :],
                                                          start=(mt == 0), stop=(mt == 1)), reads=["V", "PT%d" % mt], writes=[pko])
                        for mt in range(2):
                            k.op("pe", lambda e: e.matmul(pz[:, 0:384], lhsT=ONES[:], rhs=PT[:, mt, :],
                                                          start=(mt == 0), stop=(mt == 1)), reads=["ONES", "PT%d" % mt], writes=[pkz])
                        k.op("dve", lambda e: e.reciprocal(out=RZ[:], in_=pz[:, 0:384]), reads=[pkz], writes=["RZ"])
                        k.op("dve", lambda e: e.tensor_tensor(out=OT[:, h, cs], in0=po[:, 0:384], in1=RZ[:], op=ALU.mult),
                             reads=[pko, "RZ"], writes=["OT%d_%d" % (h, tc)])
            for i in range(NT):
                for n in range(4):
                    p, pk = ps()
                    for h in range(4):
                        k.op("pe", lambda e: e.matmul(p[:], lhsT=OT[:, h, i * 128:(i + 1) * 128], rhs=WO[:, h, n * 512:(n + 1) * 512],
                                                      start=(h == 0), stop=(h == 3)),
                             reads=["WO", "OT%d_%d" % (h, i // 3)], writes=[pk])
                    resid(i, n * 512, (n + 1) * 512, p[:], pk)
                layer_norm(i, GB, stat, mv, sd)
                transposes(i)

        with k.scope():
            ACTT = k.sb("ACTT", [128, 11, 1024], BF16)
            WA = [k.sb("WA%d" % i, [128, 16, 128], BF16) for i in range(2)]
            WG = [k.sb("WG%d" % i, [128, 16, 128], BF16) for i in range(2)]
            WD = [k.sb("WD%d" % i, [128, 11, 256], BF16) for i in range(2)]
            CA = [k.sb("CA%d" % i, [128, 344]) for i in range(2)]
            CG = [k.sb("CG%d" % i, [128, 344]) for i in range(2)]
            CW = k.sb("CW", [128, 88, 3])
            CB = k.sb("CB", [128, 88])
            FL = k.sb("FL", [128, 1])
            GB = k.sb("GB", [128, 2, D])
            stat = k.sb("stat", [128, 4, 6])
            mv = k.sb("mv", [128, 2])
            sd = k.sb("sd", [128, 1])
            load_gb(GB, 2)
            k.dma("sp", CW[:], convw[:, :, :], writes=["CW"])
            k.dma("sp", CB[:], convb[:, :], writes=["CB"])
            k.dma("sp", FL[:], flag[:, :], writes=["FL"])
            pieces = [(0, 342), (342, 342), (684, 340)]
            cnt = 0
            wdc = 0
            for g in range(4):
                for jj in range(11):
                    j = g * 11 + jj
                    wa, wak = WA[cnt % 2], "WA%d" % (cnt % 2)
                    wg, wgk = WG[cnt % 2], "WG%d" % (cnt % 2)
                    k.dma("pool", wa[:], w_up[:, j * 128:(j + 1) * 128].rearrange("(k p) n -> p k n", p=128), writes=[wak])
                    k.dma("pool", wg[:], w_up[:, DFF + j * 128:DFF + (j + 1) * 128].rearrange("(k p) n -> p k n", p=128), writes=[wgk])
                    for pi, (t0, n) in enumerate(pieces):
                        c0 = 126 + t0
                        hk = ht_keys(c0, c0 + n + 2)
                        ca, cak = CA[cnt % 2], "CA%d" % (cnt % 2)
                        cg, cgk = CG[cnt % 2], "CG%d" % (cnt % 2)
                        cnt += 1
                        for (wt, wtk, ch, buf, bk) in ((wa, wak, j, ca, cak), (wg, wgk, 44 + j, cg, cgk)):
                            p, pk = ps()
                            for kk in range(16):
                                k.op("pe", lambda e: e.matmul(p[:, 0:n + 2], lhsT=wt[:, kk, :], rhs=HT[:, kk, c0:c0 + n + 2],
                                                              start=(kk == 0), stop=(kk == 15)), reads=[wtk] + hk, writes=[pk])
                            if pi == 0:
                                k.op("dve", lambda e: e.tensor_scalar(out=p[:, 0:2], in0=p[:, 0:2], scalar1=FL[:, 0:1], scalar2=None,
                                                                      op0=ALU.mult), reads=[pk, "FL"], writes=[pk])
                            k.op("act", lambda e: e.activation(out=buf[:, 0:n], in_=p[:, 2:n + 2], func=AF.Identity,
                                                               scale=CW[:, ch, 2:3], bias=CB[:, ch:ch + 1]),
                                 reads=[pk, "CW", "CB"], writes=[bk])
                            k.op("dve", lambda e: e.scalar_tensor_tensor(out=buf[:, 0:n], in0=p[:, 1:n + 1], scalar=CW[:, ch, 1:2],
                                                                         in1=buf[:, 0:n], op0=ALU.mult, op1=ALU.add),
                                 reads=[pk, "CW", bk], writes=[bk])
                            k.op("dve", lambda e: e.scalar_tensor_tensor(out=buf[:, 0:n], in0=p[:, 0:n], scalar=CW[:, ch, 0:1],
                                                                         in1=buf[:, 0:n], op0=ALU.mult, op1=ALU.add),
                                 reads=[pk, "CW", bk], writes=[bk])
                        k.op("act", lambda e: e.activation(out=cg[:, 0:n], in_=cg[:, 0:n], func=AF.Gelu_apprx_tanh),
                             reads=[cgk], writes=[cgk])
                        k.op("pool", lambda e: e.tensor_tensor(out=ACTT[:, jj, t0:t0 + n], in0=ca[:, 0:n], in1=cg[:, 0:n], op=ALU.mult),
                             reads=[cak, cgk], writes=["AT%d" % jj])
                for n8 in range(8):
                    wd, wdk = WD[wdc % 2], "WD%d" % (wdc % 2)
                    wdc += 1
                    k.dma("pool", wd[:], w_down[g * 1408:(g + 1) * 1408, n8 * 256:(n8 + 1) * 256].rearrange("(j p) n -> p j n", p=128),
                          writes=[wdk])
                    for i in range(1, NT):
                        p, pk = ps()
                        for jj in range(11):
                            k.op("pe", lambda e: e.matmul(p[:, 0:256], lhsT=ACTT[:, jj, (i - 1) * 128:i * 128], rhs=wd[:, jj, :],
                                                          start=(jj == 0), stop=(jj == 10)), reads=[wdk, "AT%d" % jj], writes=[pk])
                        resid(i, n8 * 256, (n8 + 1) * 256, p[:, 0:256], pk, first=(g == 0))
            for i in range(1, NT):
                layer_norm(i, GB, stat, mv, sd)
                k.dma("sp", h_out[(i - 1) * 128:i * 128, :], H[:, i, :], reads=["H%d" % i], is_output=True)
                if do_proj:
                    transposes(i)
    else:
        for i in range(1, NT):
            transposes(i)

    if do_proj:
        with k.scope():
            STG = [k.sb("STG%d" % i, [128, 256]) for i in range(4)]
            sc = 0
            for n in range(19):
                c0 = n * 256
                wn = min(256, NIN - c0)
                w, wk = wb()
                k.dma("pool", w[:, :, 0:wn], w_in[:, c0:c0 + wn].rearrange("(k p) n -> p k n", p=128), writes=[wk])
                for i in range(1, NT):
                    p, pk = ps()
                    for kk in range(16):
                        k.op("pe", lambda e: e.matmul(p[:, 0:wn], lhsT=HT[:, kk, i * 128:(i + 1) * 128], rhs=w[:, kk, 0:wn],
                                                      start=(kk == 0), stop=(kk == 15)), reads=[wk, "HT%d" % i], writes=[pk])
                    s, sk = STG[sc % 4], "STG%d" % (sc % 4)
                    sc += 1
                    k.op("act", lambda e: e.activation(out=s[:, 0:wn], in_=p[:, 0:wn], func=AF.Copy), reads=[pk], writes=[sk])
                    k.dma("sp", proj[(i - 1) * 128:i * 128, c0:c0 + wn], s[:, 0:wn], reads=[sk], is_output=True)
    k.finish()
    return nc


T = 8192
SCALE = 64 ** -0.5
TWO_PI = 2.0 * math.pi
C1 = 6.28125
C2 = TWO_PI - C1


def build_s5(k, uT, lam, bre, bim, cre, cim, dvec, y5T, psb):
    with k.scope():
        UT = k.sb("UT", [64, T])
        YT = k.sb("YT", [64, T])
        ident = k.sb("ident", [128, 128])
        DV = k.sb("DV", [64, 1])
        k.dma("sp", UT[:], uT[:, :], writes=["UT"])
        k.dma("sp", DV[:], dvec[:, :], writes=["DV"])
        k.op("pool", lambda e: e.memset(ident[:], 1.0), writes=["ident"])
        k.op("pool", lambda e: e.affine_select(out=ident[:], in_=ident[:], pattern=[[-1, 128]], compare_op=ALU.is_equal,
                                                fill=k.fill(0.0), base=0, channel_multiplier=1), reads=["ident"], writes=["ident"])
        R = []
        for rt in range(2):
            d = {}
            P = "P%d" % rt
            for nm, shp in (("LAM", [128, 3]), ("BRE", [128, 64]), ("BIM", [128, 64]), ("CRE", [128, 64]), ("CIMN", [128, 64]),
                            ("BBR", [128, 64]), ("BBI", [128, 64]), ("BRT", [64, 128]), ("BIT", [64, 128]),
                            ("COS", [128, 512]), ("SIN", [128, 512]), ("RHO", [128, 512]), ("TMP", [128, 256]),
                            ("S", [128, 24]), ("KI", [128, 1])):
                d[nm] = k.sb("%s%d" % (nm, rt), shp, I32 if nm == "KI" else F32)
            k.dma("sp", d["LAM"][:], lam[:, rt, :], writes=[P])
            k.dma("sp", d["BRE"][:], bre[:, rt, :], writes=[P])
            k.dma("sp", d["BIM"][:], bim[:, rt, :], writes=[P])
            k.dma("sp", d["CRE"][:], cre[:, rt, :], writes=[P])
            k.dma("sp", d["CIMN"][:], cim[:, rt, :], writes=[P])
            S = d["S"]

            def sc(i):
                return S[:, i:i + 1]
            lr, li, ldt = d["LAM"][:, 0:1], d["LAM"][:, 1:2], d["LAM"][:, 2:3]

            def dv(fn):
                k.op("dve", fn, reads=[P], writes=[P])

            def ac(fn):
                k.op("act", fn, reads=[P], writes=[P])
            ac(lambda e: e.activation(out=sc(0), in_=ldt, func=AF.Exp))
            dv(lambda e: e.tensor_tensor(out=sc(1), in0=lr, in1=sc(0), op=ALU.mult))
            dv(lambda e: e.tensor_tensor(out=sc(2), in0=li, in1=sc(0), op=ALU.mult))
            ac(lambda e: e.activation(out=sc(3), in_=sc(1), func=AF.Exp))
            dv(lambda e: e.tensor_scalar(out=sc(4), in0=sc(2), scalar1=1.0 / TWO_PI, scalar2=None, op0=ALU.mult))
            dv(lambda e: e.tensor_copy(out=d["KI"][:], in_=sc(4)))
            dv(lambda e: e.tensor_copy(out=sc(4), in_=d["KI"][:]))
            dv(lambda e: e.scalar_tensor_tensor(out=sc(5), in0=sc(4), scalar=-C1, in1=sc(2), op0=ALU.mult, op1=ALU.add))
            dv(lambda e: e.scalar_tensor_tensor(out=sc(5), in0=sc(4), scalar=-C2, in1=sc(5), op0=ALU.mult, op1=ALU.add))
            dv(lambda e: e.tensor_scalar(out=sc(6), in0=sc(5), scalar1=math.pi, scalar2=-TWO_PI, op0=ALU.is_gt, op1=ALU.mult))
            dv(lambda e: e.tensor_tensor(out=sc(5), in0=sc(5), in1=sc(6), op=ALU.add))
            dv(lambda e: e.tensor_scalar(out=sc(6), in0=sc(5), scalar1=-math.pi, scalar2=TWO_PI, op0=ALU.is_lt, op1=ALU.mult))
            dv(lambda e: e.tensor_tensor(out=sc(5), in0=sc(5), in1=sc(6), op=ALU.add))
            ac(lambda e: e.activation(out=sc(7), in_=sc(5), func=AF.Sin))
            dv(lambda e: e.tensor_scalar(out=sc(6), in0=sc(5), scalar1=-1.0, scalar2=None, op0=ALU.mult))
            dv(lambda e: e.tensor_tensor(out=sc(6), in0=sc(6), in1=sc(5), op=ALU.max))
            dv(lambda e: e.tensor_scalar(out=sc(6), in0=sc(6), scalar1=-1.0, scalar2=math.pi / 2, op0=ALU.mult, op1=ALU.add))
            ac(lambda e: e.activation(out=sc(8), in_=sc(6), func=AF.Sin))
            dv(lambda e: e.tensor_tensor(out=sc(9), in0=sc(3), in1=sc(8), op=ALU.mult))
            dv(lambda e: e.tensor_scalar(out=sc(9), in0=sc(9), scalar1=-1.0, scalar2=None, op0=ALU.add))
            dv(lambda e: e.tensor_tensor(out=sc(10), in0=sc(3), in1=sc(7), op=ALU.mult))
            dv(lambda e: e.tensor_tensor(out=sc(11), in0=lr, in1=lr, op=ALU.mult))
            dv(lambda e: e.scalar_tensor_tensor(out=sc(11), in0=li, scalar=li, in1=sc(11), op0=ALU.mult, op1=ALU.add))
            dv(lambda e: e.reciprocal(out=sc(11), in_=sc(11)))
            dv(lambda e: e.tensor_tensor(out=sc(14), in0=sc(9), in1=lr, op=ALU.mult))
            dv(lambda e: e.scalar_tensor_tensor(out=sc(14), in0=sc(10), scalar=li, in1=sc(14), op0=ALU.mult, op1=ALU.add))
            dv(lambda e: e.tensor_tensor(out=sc(12), in0=sc(14), in1=sc(11), op=ALU.mult))
            dv(lambda e: e.tensor_tensor(out=sc(15), in0=sc(9), in1=li, op=ALU.mult))
            dv(lambda e: e.scalar_tensor_tensor(out=sc(15), in0=sc(10), scalar=lr, in1=sc(15), op0=ALU.mult, op1=ALU.subtract))
            dv(lambda e: e.tensor_tensor(out=sc(13), in0=sc(15), in1=sc(11), op=ALU.mult))
            dv(lambda e: e.tensor_scalar(out=d["BBR"][:], in0=d["BIM"][:], scalar1=sc(13), scalar2=None, op0=ALU.mult))
            dv(lambda e: e.scalar_tensor_tensor(out=d["BBR"][:], in0=d["BRE"][:], scalar=sc(12), in1=d["BBR"][:], op0=ALU.mult, op1=ALU.subtract))
            dv(lambda e: e.tensor_scalar(out=d["BBI"][:], in0=d["BRE"][:], scalar1=sc(13), scalar2=None, op0=ALU.mult))
            dv(lambda e: e.scalar_tensor_tensor(out=d["BBI"][:], in0=d["BIM"][:], scalar=sc(12), in1=d["BBI"][:], op0=ALU.mult, op1=ALU.add))
            dv(lambda e: e.tensor_scalar(out=d["CIMN"][:], in0=d["CIMN"][:], scalar1=-1.0, scalar2=None, op0=ALU.mult))
            for src, dst in (("BBR", "BRT"), ("BBI", "BIT")):
                p = psb[0]
                k.op("pe", lambda e: e.transpose(out=p[0:64, 0:128], in_=d[src][:], identity=ident[:]), reads=[P, "ident"], writes=["psb0"])
                k.op("dve", lambda e: e.tensor_copy(out=d[dst][:], in_=p[0:64, 0:128]), reads=["psb0", P], writes=[P])
            COS, SIN, TMP = d["COS"], d["SIN"], d["TMP"]
            dv(lambda e: e.memset(COS[:, 0:1], 1.0))
            dv(lambda e: e.memset(SIN[:, 0:1], 0.0))
            dv(lambda e: e.tensor_copy(out=sc(16), in_=sc(8)))
            dv(lambda e: e.tensor_copy(out=sc(17), in_=sc(7)))
            m = 1
            while m < 512:
                dv(lambda e: e.tensor_scalar(out=TMP[:, 0:m], in0=SIN[:, 0:m], scalar1=sc(17), scalar2=None, op0=ALU.mult))
                dv(lambda e: e.scalar_tensor_tensor(out=COS[:, m:2 * m], in0=COS[:, 0:m], scalar=sc(16), in1=TMP[:, 0:m],
                                                    op0=ALU.mult, op1=ALU.subtract))
                dv(lambda e: e.tensor_scalar(out=TMP[:, 0:m], in0=COS[:, 0:m], scalar1=sc(17), scalar2=None, op0=ALU.mult))
                dv(lambda e: e.scalar_tensor_tensor(out=SIN[:, m:2 * m], in0=SIN[:, 0:m], scalar=sc(16), in1=TMP[:, 0:m],
                                                    op0=ALU.mult, op1=ALU.add))
                dv(lambda e: e.tensor_tensor(out=sc(18), in0=sc(17), in1=sc(17), op=ALU.mult))
                dv(lambda e: e.tensor_tensor(out=sc(19), in0=sc(16), in1=sc(17), op=ALU.mult))
                dv(lambda e: e.scalar_tensor_tensor(out=sc(16), in0=sc(16), scalar=sc(16), in1=sc(18), op0=ALU.mult, op1=ALU.subtract))
                dv(lambda e: e.tensor_scalar(out=sc(17), in0=sc(19), scalar1=2.0, scalar2=None, op0=ALU.mult))
                m *= 2
            dv(lambda e: e.memset(d["RHO"][:], 1.0))
            dv(lambda e: e.tensor_scalar(out=d["RHO"][:], in0=d["RHO"][:], scalar1=sc(3), scalar2=None, op0=ALU.mult))
            for nm in ("T1", "T2", "VR", "VI", "WR", "WI", "XR", "XI"):
                d[nm] = k.sb("%s%d" % (nm, rt), [128, 512])
            d["INIT"] = k.sb("INIT%d" % rt, [128, 4])
            dv(lambda e: e.memset(d["INIT"][:], 0.0))
            R.append(d)

        for ch in range(16):
            cs = slice(ch * 512, (ch + 1) * 512)
            py = psb[6]
            for rt in range(2):
                d = R[rt]
                P = "P%d" % rt
                W = "W%d" % rt
                COS, SIN = d["COS"], d["SIN"]
                pr, pi_ = psb[2 * rt], psb[2 * rt + 1]
                prk, pik = "psb%d" % (2 * rt), "psb%d" % (2 * rt + 1)
                k.op("pe", lambda e: e.matmul(pr[:], lhsT=d["BRT"][:], rhs=UT[:, cs], start=True, stop=True), reads=[P, "UT"], writes=[prk])
                k.op("pe", lambda e: e.matmul(pi_[:], lhsT=d["BIT"][:], rhs=UT[:, cs], start=True, stop=True), reads=[P, "UT"], writes=[pik])
                k.op("dve", lambda e: e.tensor_tensor(out=d["T1"][:], in0=pr[:], in1=COS[:], op=ALU.mult), reads=[prk, P], writes=[W + "T1"])
                k.op("dve", lambda e: e.tensor_tensor(out=d["T2"][:], in0=pi_[:], in1=SIN[:], op=ALU.mult), reads=[pik, P], writes=[W + "T2"])
                k.op("pool", lambda e: e.tensor_tensor(out=d["VR"][:], in0=d["T1"][:], in1=d["T2"][:], op=ALU.add),
                     reads=[W + "T1", W + "T2"], writes=[W + "VR"])
                k.op("dve", lambda e: e.tensor_tensor(out=d["T1"][:], in0=pi_[:], in1=COS[:], op=ALU.mult), reads=[pik, P], writes=[W + "T1"])
                k.op("dve", lambda e: e.tensor_tensor(out=d["T2"][:], in0=pr[:], in1=SIN[:], op=ALU.mult), reads=[prk, P], writes=[W + "T2"])
                k.op("pool", lambda e: e.tensor_tensor(out=d["VI"][:], in0=d["T1"][:], in1=d["T2"][:], op=ALU.subtract),
                     reads=[W + "T1", W + "T2"], writes=[W + "VI"])
                k.op("dve", lambda e: e.tensor_tensor_scan(out=d["WR"][:], data0=d["RHO"][:], data1=d["VR"][:], initial=d["INIT"][:, 0:1],
                                                           op0=ALU.mult, op1=ALU.add), reads=[P, W + "VR", W + "INIT"], writes=[W + "WR"])
                k.op("dve", lambda e: e.tensor_tensor_scan(out=d["WI"][:], data0=d["RHO"][:], data1=d["VI"][:], initial=d["INIT"][:, 1:2],
                                                           op0=ALU.mult, op1=ALU.add), reads=[P, W + "VI", W + "INIT"], writes=[W + "WI"])
                k.op("pool", lambda e: e.tensor_tensor(out=d["VR"][:], in0=d["WR"][:], in1=COS[:], op=ALU.mult), reads=[W + "WR", P], writes=[W + "VR"])
                k.op("pool", lambda e: e.tensor_tensor(out=d["VI"][:], in0=d["WI"][:], in1=SIN[:], op=ALU.mult), reads=[W + "WI", P], writes=[W + "VI"])
                k.op("pool", lambda e: e.tensor_tensor(out=d["XR"][:], in0=d["VR"][:], in1=d["VI"][:], op=ALU.subtract),
                     reads=[W + "VR", W + "VI"], writes=[W + "XR"])
                k.op("pool", lambda e: e.tensor_tensor(out=d["VR"][:], in0=d["WR"][:], in1=SIN[:], op=ALU.mult), reads=[W + "WR", P], writes=[W + "VR"])
                k.op("pool", lambda e: e.tensor_tensor(out=d["VI"][:], in0=d["WI"][:], in1=COS[:], op=ALU.mult), reads=[W + "WI", P], writes=[W + "VI"])
                k.op("pool", lambda e: e.tensor_tensor(out=d["XI"][:], in0=d["VR"][:], in1=d["VI"][:], op=ALU.add),
                     reads=[W + "VR", W + "VI"], writes=[W + "XI"])
                S = d["S"]
                k.op("dve", lambda e: e.tensor_tensor(out=d["INIT"][:, 2:3], in0=d["XI"][:, 511:512], in1=S[:, 7:8], op=ALU.mult),
                     reads=[W + "XI", P, W + "INIT"], writes=[W + "INIT"])
                k.op("dve", lambda e: e.scalar_tensor_tensor(out=d["INIT"][:, 0:1], in0=d["XR"][:, 511:512], scalar=S[:, 8:9],
                                                             in1=d["INIT"][:, 2:3], op0=ALU.mult, op1=ALU.subtract),
                     reads=[W + "XR", P, W + "INIT"], writes=[W + "INIT"])
                k.op("dve", lambda e: e.tensor_tensor(out=d["INIT"][:, 2:3], in0=d["XR"][:, 511:512], in1=S[:, 7:8], op=ALU.mult),
                     reads=[W + "XR", P, W + "INIT"], writes=[W + "INIT"])
                k.op("dve", lambda e: e.scalar_tensor_tensor(out=d["INIT"][:, 1:2], in0=d["XI"][:, 511:512], scalar=S[:, 8:9],
                                                             in1=d["INIT"][:, 2:3], op0=ALU.mult, op1=ALU.add),
                     reads=[W + "XI", P, W + "INIT"], writes=[W + "INIT"])
                k.op("pe", lambda e: e.matmul(py[0:64, :], lhsT=d["CRE"][:], rhs=d["XR"][:], start=(rt == 0), stop=False),
                     reads=[P, W + "XR"], writes=["psb6"])
                k.op("pe", lambda e: e.matmul(py[0:64, :], lhsT=d["CIMN"][:], rhs=d["XI"][:], start=False, stop=(rt == 1)),
                     reads=[P, W + "XI"], writes=["psb6"])
            k.op("dve", lambda e: e.scalar_tensor_tensor(out=YT[:, cs], in0=UT[:, cs], scalar=DV[:, 0:1], in1=py[0:64, :],
                                                         op0=ALU.mult, op1=ALU.add), reads=["UT", "DV", "psb6"], writes=["YT%d" % ch])
            k.dma("sp", y5T[:, cs], YT[:, cs], reads=["YT%d" % ch], is_output=True)


def build_sb(k, qT, kT, v, oT, psb):
    with k.scope():
        QB = k.sb("QB", [64, T], BF16)
        KB = k.sb("KB", [64, T], BF16)
        VB = k.sb("VB", [128, 64, 64], BF16)
        OT = k.sb("OTs", [64, T])
        U = k.sb("U", [128, 128])
        ONESF = k.sb("ONESF", [128, 128])
        k.dma("pool", QB[:], qT[:, :], writes=["QB"])
        k.dma("pool", KB[:], kT[:, :], writes=["KB"])
        k.dma("pool", VB[:], v.rearrange("(t p) d -> p t d", p=128), writes=["VB"])
        k.op("pool", lambda e: e.memset(ONESF[:], 1.0), writes=["ONESF"])
        k.op("pool", lambda e: e.memset(U[:], 1.0), writes=["U"])
        k.op("pool", lambda e: e.affine_select(out=U[:], in_=U[:], pattern=[[-1, 128]], compare_op=ALU.is_gt,
                                                fill=k.fill(0.0), base=0, channel_multiplier=1), reads=["U"], writes=["U"])
        NB = 2
        E = [k.sb("E%d" % i, [128, 512]) for i in range(NB)]
        SP = [k.sb("SP%d" % i, [128, 512]) for i in range(NB)]
        B = [k.sb("B%d" % i, [128, 512]) for i in range(NB)]
        WW = [k.sb("WW%d" % i, [128, 512], BF16) for i in range(NB)]
        step = 0
        for Q in range(16):
            t0 = Q * 512
            kmax = 4 * Q + 3
            pR, pO = psb[4], psb[5]
            for kt in range(kmax, -1, -1):
                b = step % NB
                pz, pzk = psb[step % 2], "psb%d" % (step % 2)
                pL, pLk = psb[2 + step % 2], "psb%d" % (2 + step % 2)
                step += 1
                diag = kt >= 4 * Q
                e_, sp, bb, ww = E[b], SP[b], B[b], WW[b]
                ek, spk, bk, wk = "E%d" % b, "SP%d" % b, "B%d" % b, "WW%d" % b
                k.op("pe", lambda e: e.matmul(pz[:], lhsT=KB[:, kt * 128:(kt + 1) * 128], rhs=QB[:, t0:t0 + 512], start=True, stop=True),
                     reads=["KB", "QB"], writes=[pzk])
                k.op("act", lambda e: e.activation(out=e_[:], in_=pz[:], func=AF.Exp, scale=SCALE), reads=[pzk], writes=[ek])
                k.op("act", lambda e: e.activation(out=sp[:], in_=e_[:], func=AF.Ln, bias=1.0), reads=[ek], writes=[spk])
                if diag:
                    k.op("pool", lambda e: e.affine_select(out=sp[:], in_=sp[:], pattern=[[1, 512]], compare_op=ALU.is_gt, fill=k.fill(0.0),
                                                            base=t0 - 128 * kt, channel_multiplier=-1), reads=[spk], writes=[spk])
                k.op("pe", lambda e: e.matmul(pL[:], lhsT=U[:], rhs=sp[:], start=True, stop=True), reads=["U", spk], writes=[pLk])
                k.op("dve", lambda e: e.scalar_tensor_tensor(out=bb[:], in0=pz[:], scalar=SCALE, in1=sp[:], op0=ALU.mult, op1=ALU.subtract),
                     reads=[pzk, spk], writes=[bk])
                k.op("dve", lambda e: e.tensor_tensor(out=bb[:], in0=bb[:], in1=pL[:], op=ALU.subtract), reads=[bk, pLk], writes=[bk])
                if kt < kmax:
                    k.op("dve", lambda e: e.tensor_tensor(out=bb[:], in0=bb[:], in1=pR[:], op=ALU.subtract), reads=[bk, "psb4"], writes=[bk])
                if kt > 0:
                    k.op("pe", lambda e: e.matmul(pR[:], lhsT=ONESF[:], rhs=sp[:], start=(kt == kmax), stop=(kt == 1)),
                         reads=["ONESF", spk], writes=["psb4"])
                k.op("act", lambda e: e.activation(out=ww[:], in_=bb[:], func=AF.Exp), reads=[bk], writes=[wk])
                if diag:
                    k.op("pool", lambda e: e.affine_select(out=ww[:], in_=ww[:], pattern=[[1, 512]], compare_op=ALU.is_gt, fill=k.fill(0.0),
                                                            base=t0 - 128 * kt, channel_multiplier=-1), reads=[wk], writes=[wk])
                k.op("pe", lambda e: e.matmul(pO[0:64, :], lhsT=VB[:, kt, :], rhs=ww[:], start=(kt == kmax), stop=(kt == 0)),
                     reads=["VB", wk], writes=["psb5"])
            k.op("act", lambda e: e.activation(out=OT[:, t0:t0 + 512], in_=pO[0:64, :], func=AF.Copy), reads=["psb5"], writes=["OT%d" % Q])
            k.dma("sp", oT[:, t0:t0 + 512], OT[:, t0:t0 + 512], reads=["OT%d" % Q], is_output=True)


DMIN = -2064
ND = 5120
FAR_D = 790


def build_nsa(k, nqT, kcA, kcB, vcA, vcB, kslT, vsl, kswT, vsw, graw, w1, w2, posT, tabx, tab31, OH, OHW, ovv, eall, ynsa, psb, ptb):
    k.fill(0.0), k.fill(1e4), k.fill(-1.0)
    fsd = k.dram_tmp("fsd", [4, ND])
    fwd = k.dram_tmp("fwd", [4, ND])
    with k.scope():
        OUT = k.sb("OUT", [128, 64, 128])
        PENT = k.sb("PENT", [128, T], BF16)
        GS = k.sb("GS", [128, 64, 6])
        TAB31 = k.sb("TAB31", [128, 4])
        OVV = k.sb("OVV", [128, 4, 193], BF16)
        KCT = k.sb("KCT", [64, 512], BF16)
        identb = k.sb("identb", [128, 128], BF16)
        k.op("pool", lambda e: e.memset(identb[:], 1.0), writes=["identb"])
        k.op("pool", lambda e: e.affine_select(out=identb[:], in_=identb[:], pattern=[[-1, 128]], compare_op=ALU.is_equal,
                                                fill=k.fill(0.0), base=0, channel_multiplier=1), reads=["identb"], writes=["identb"])
        k.dma("sp", GS[:], graw.rearrange("(t p) c -> p t c", p=128), writes=["GS"])
        k.op("act", lambda e: e.activation(out=GS[:], in_=GS[:], func=AF.Sigmoid), reads=["GS"], writes=["GS"])
        k.dma("sp", TAB31[:], tab31[0, :].partition_broadcast(128), writes=["TAB31"])
        k.dma("pool", OVV[:, :, 0:129], ovv[:, :, :], writes=["OVV"])
        with k.scope():
            TABX = k.sb("TABX", [33, 4])
            OHS = k.sb("OHS", [33, ND])
            FSB = k.sb("FSB", [4, ND])
            k.dma("sp", TABX[:], tabx[:, :], writes=["TABX"])
            for which, (src, dst) in enumerate(((OH, fsd), (OHW, fwd))):
                k.dma("sp", OHS[:], src[:, :], writes=["OHS"])
                for c in range(ND // 512):
                    p, pk = psb[c % 2], "psb%d" % (c % 2)
                    k.op("pe", lambda e: e.matmul(p[0:4, :], lhsT=TABX[:], rhs=OHS[:, c * 512:(c + 1) * 512], start=True, stop=True),
                         reads=["TABX", "OHS"], writes=[pk])
                    k.op("dve", lambda e: e.tensor_copy(out=FSB[:, c * 512:(c + 1) * 512], in_=p[0:4, :]), reads=[pk], writes=["FSB"])
                k.dma("sp", dst[:, :], FSB[:], reads=["FSB"], writes=["FD%d" % which])
        with k.scope():
            XA = k.sb("XA", [64, 512, 16], BF16)
            XB = k.sb("XB", [64, 512, 16], BF16)
            W1 = k.sb("W1", [64, 2, 32, 128], BF16)
            W2 = k.sb("W2", [128, 2, 64], BF16)
            POST = k.sb("POST", [64, 2, 32], BF16)
            G = k.sb("G", [128, 512], BF16)
            PB = k.sb("PB", [128, 1])
            k.dma("pool", W1[:], w1[:, :, :, :], writes=["W1"])
            k.dma("pool", W2[:], w2[:, :, :], writes=["W2"])
            k.dma("pool", POST[:], posT[:, :, :], writes=["POST"])
            for j in range(2):
                k.dma("pool", XA[:], (kcA if j == 0 else vcA)[:, :, :], writes=["XA"])
                k.dma("pool", XB[:], (kcB if j == 0 else vcB)[:, :, :], writes=["XB"])
                ph, pb = psb[0], psb[1]
                for l in range(32):
                    X, xk = (XA, "XA") if l < 16 else (XB, "XB")
                    k.op("pe", lambda e: e.matmul(ph[:], lhsT=W1[:, j, l, :], rhs=X[:, :, l % 16], start=(l == 0), stop=(l == 31)),
                         reads=["W1", xk], writes=["psb0"])
                for l in range(32):
                    k.op("pe", lambda e: e.matmul(pb[:, 0:1], lhsT=W1[:, j, l, :], rhs=POST[:, j, l:l + 1], start=(l == 0), stop=(l == 31)),
                         reads=["W1", "POST"], writes=["psb1"])
                k.op("dve", lambda e: e.tensor_copy(out=PB[:], in_=pb[:, 0:1]), reads=["psb1"], writes=["PB"])
                k.op("act", lambda e: e.activation(out=G[:], in_=ph[:], func=AF.Gelu_apprx_tanh, bias=PB[:, 0:1]),
                     reads=["psb0", "PB"], writes=["G"])
                if j == 0:
                    p2 = psb[2]
                    k.op("pe", lambda e: e.matmul(p2[0:64, :], lhsT=W2[:, 0, :], rhs=G[:], start=True, stop=True), reads=["W2", "G"], writes=["psb2"])
                    k.op("act", lambda e: e.activation(out=KCT[:], in_=p2[0:64, :], func=AF.Copy), reads=["psb2"], writes=["KCT"])
                else:
                    for m in range(4):
                        p2, p2k = psb[2 + m], "psb%d" % (2 + m)
                        k.op("pe", lambda e: e.matmul(p2[:, 0:64], lhsT=G[:, m * 128:(m + 1) * 128], rhs=W2[:, 1, :], start=True, stop=True),
                             reads=["W2", "G"], writes=[p2k])
                        k.op("act", lambda e: e.activation(out=OVV[:, m, 129:193], in_=p2[:, 0:64], func=AF.Copy), reads=[p2k, "OVV"], writes=["OVV"])
        with k.scope():
            QB4 = k.sb("QB4", [64, 4, T], BF16)
            CB = k.sb("CB", [128, 6, 4, 512])
            IMP = k.sb("IMP", [128, 4, 128])
            SC2 = k.sb("SC2", [128, 128])
            M8 = k.sb("M8", [128, 8])
            M8b = k.sb("M8b", [128, 8])
            PENTOK = k.sb("PENTOK", [128, 128], BF16)
            ET = [[k.sb("ET%d_%d" % (a, m), [128, 512], BF16) for m in range(4)] for a in range(2)]
            LG = [k.sb("LGc%d" % a, [128, 512]) for a in range(2)]
            RZ = k.sb("RZc", [128, 2])
            k.dma("pool", QB4[:], nqT[:, :, :], writes=["QB4"])
            for j in range(6):
                for hj in range(4):
                    src = AP(tensor=fsd.tensor, offset=hj * ND + (512 * j - 31 - 2032 - DMIN), ap=[[16, 128], [1, 512]])
                    k.dma("sp", CB[:, j, hj, :], src, reads=["FD0"], writes=["CB"])
            lgc = 0
            pc = 0
            for Q in range(16):
                t0 = Q * 512
                mlist = list(range(0, Q // 4 + 1))
                for hj in range(4):
                    a = hj % 2
                    for m in mlist:
                        j = Q - 4 * m
                        p, pk = psb[pc % 2], "psb%d" % (pc % 2)
                        pc += 1
                        et, etk = ET[a][m], "ET%d_%d" % (a, m)
                        k.op("pe", lambda e: e.matmul(p[:], lhsT=KCT[:, m * 128:(m + 1) * 128], rhs=QB4[:, hj, t0:t0 + 512], start=True, stop=True),
                             reads=["KCT", "QB4"], writes=[pk])
                        if j <= 5:
                            lg, lgk = LG[lgc % 2], "LGc%d" % (lgc % 2)
                            lgc += 1
                            k.op("dve", lambda e: e.scalar_tensor_tensor(out=lg[:], in0=p[:], scalar=SCALE, in1=CB[:, j, hj, :], op0=ALU.mult, op1=ALU.add),
                                 reads=[pk, "CB"], writes=[lgk])
                            k.op("act", lambda e: e.activation(out=et[:], in_=lg[:], func=AF.Exp), reads=[lgk], writes=[etk])
                        else:
                            k.op("act", lambda e: e.activation(out=et[:], in_=p[:], func=AF.Exp, scale=SCALE, bias=TAB31[:, hj:hj + 1]),
                                 reads=[pk, "TAB31"], writes=[etk])
                    W = 193 if hj < 2 else 129
                    for sub in range(4):
                        tt = 4 * Q + sub
                        pI, pIk = psb[2 + sub], "psb%d" % (2 + sub)
                        for mi, m in enumerate(mlist):
                            k.op("pe", lambda e: e.matmul(pI[:, 0:W], lhsT=ET[a][m][:, sub * 128:(sub + 1) * 128], rhs=OVV[:, m, 0:W],
                                                          start=(mi == 0), stop=(mi == len(mlist) - 1)),
                                 reads=["ET%d_%d" % (a, m), "OVV"], writes=[pIk])
                        k.op("dve", lambda e: e.tensor_scalar(out=RZ[:, 0:1], in0=pI[:, 128:129], scalar1=1e-30, scalar2=None, op0=ALU.max),
                             reads=[pIk, "RZc"], writes=["RZc"])
                        k.op("dve", lambda e: e.reciprocal(out=RZ[:, 0:1], in_=RZ[:, 0:1]), reads=["RZc"], writes=["RZc"])
                        if hj == 0:
                            k.op("dve", lambda e: e.tensor_scalar(out=IMP[:, sub, :], in0=pI[:, 0:128], scalar1=RZ[:, 0:1], scalar2=None, op0=ALU.mult),
                                 reads=[pIk, "RZc", "IMP%d" % sub], writes=["IMP%d" % sub])
                        else:
                            k.op("dve", lambda e: e.scalar_tensor_tensor(out=IMP[:, sub, :], in0=pI[:, 0:128], scalar=RZ[:, 0:1], in1=IMP[:, sub, :],
                                                                         op0=ALU.mult, op1=ALU.add),
                                 reads=[pIk, "RZc", "IMP%d" % sub], writes=["IMP%d" % sub])
                        if hj < 2:
                            k.op("dve", lambda e: e.tensor_tensor(out=RZ[:, 1:2], in0=RZ[:, 0:1], in1=GS[:, tt, hj * 3:hj * 3 + 1], op=ALU.mult),
                                 reads=["RZc", "GS"], writes=["RZc"])
                            k.op("dve", lambda e: e.tensor_scalar(out=OUT[:, tt, hj * 64:(hj + 1) * 64], in0=pI[:, 129:193], scalar1=RZ[:, 1:2],
                                                                  scalar2=None, op0=ALU.mult), reads=[pIk, "RZc"], writes=["OUT%d" % tt])
                for sub in range(4):
                    tb = 128 * (4 * Q + sub)
                    ik = "IMP%d" % sub
                    sc_ = IMP[:, sub, :]
                    k.op("pool", lambda e: e.affine_select(out=sc_, in_=sc_, pattern=[[-64, 128]], compare_op=ALU.is_ge, fill=k.fill(1e4),
                                                            base=tb - 128, channel_multiplier=1), reads=[ik], writes=[ik])
                    k.op("pool", lambda e: e.affine_select(out=sc_, in_=sc_, pattern=[[-64, 128]], compare_op=ALU.is_ge, fill=k.fill(-1.0),
                                                            base=tb, channel_multiplier=1), reads=[ik], writes=[ik])
                    k.op("pool", lambda e: e.memset(IMP[:, sub, 0:1], 1e4), reads=[ik], writes=[ik])
                    k.op("dve", lambda e: e.max(out=M8[:], in_=sc_), reads=[ik], writes=["M8"])
                    k.op("dve", lambda e: e.match_replace(out=SC2[:], in_to_replace=M8[:], in_values=sc_, imm_value=-1e9),
                         reads=[ik, "M8"], writes=["SC2"])
                    k.op("dve", lambda e: e.max(out=M8b[:], in_=SC2[:]), reads=["SC2"], writes=["M8b"])
                    k.op("dve", lambda e: e.tensor_scalar(out=PENTOK[:], in0=sc_, scalar1=M8b[:, 7:8], scalar2=1.0, op0=ALU.is_ge, op1=ALU.subtract),
                         reads=[ik, "M8b"], writes=["PENTOK"])
                    k.op("pe", lambda e: e.transpose(out=ptb[:, 0:128], in_=PENTOK[:], identity=identb[:]), reads=["PENTOK", "identb"], writes=["ptb"])
                    k.op("act", lambda e: e.activation(out=PENT[:, tb:tb + 128], in_=ptb[:, 0:128], func=AF.Copy), reads=["ptb"], writes=["PENT%d" % Q])
        with k.scope():
            KSLT = k.sb("KSLT", [64, T], BF16)
            KSWT = k.sb("KSWT", [64, T], BF16)
            VSL = k.sb("VSL", [128, 64, 65], BF16)
            VSW = k.sb("VSW", [128, 64, 65], BF16)
            EALL = k.sb("EALL", [128, T], BF16)
            LG = [k.sb("LGs%d" % a, [128, 512]) for a in range(2)]
            EX = [k.sb("EX%d" % a, [128, 512], BF16) for a in range(2)]
            RZ = k.sb("RZs", [128, 2])
            k.dma("pool", KSLT[:], kslT[:, :], writes=["KSLT"])
            k.dma("pool", KSWT[:], kswT[:, :], writes=["KSWT"])
            k.op("pool", lambda e: e.memset(VSL[:, :, 64:65], 1.0), writes=["VSL"])
            k.op("pool", lambda e: e.memset(VSW[:, :, 64:65], 1.0), writes=["VSW"])
            k.dma("pool", VSL[:, :, 0:64], vsl.rearrange("(t p) d -> p t d", p=128), reads=["VSL"], writes=["VSL"])
            k.dma("pool", VSW[:, :, 0:64], vsw.rearrange("(t p) d -> p t d", p=128), reads=["VSW"], writes=["VSW"])
            k.dma("pool", EALL[:], eall[:, :], writes=["EALL"])
            for hj in range(2):
                with k.scope():
                    QBh = k.sb("QBh", [64, T], BF16)
                    SELB = k.sb("SELB", [128, 11, 512])
                    WINB = k.sb("WINB", [128, 8, 512])
                    k.dma("pool", QBh[:], nqT[:, hj, :], writes=["QBh"])
                    for di in range(11):
                        D0 = -384 + 128 * di
                        src = AP(tensor=fsd.tensor, offset=hj * ND + (D0 - 127 - DMIN), ap=[[1, 128], [1, 512]])
                        k.dma("sp", SELB[:, di, :], src, reads=["FD0"], writes=["SELB"])
                    for jw in range(8):
                        D0 = 512 - 128 * jw
                        src = AP(tensor=fwd.tensor, offset=hj * ND + (D0 - 127 - DMIN), ap=[[1, 128], [1, 512]])
                        k.dma("sp", WINB[:, jw, :], src, reads=["FD1"], writes=["WINB"])
                    cnt = 0
                    for Q in range(16):
                        t0 = Q * 512
                        for kt in range(0, 4 * Q + 4):
                            D0 = 512 * Q - 128 * kt
                            p, pk = psb[cnt % 2], "psb%d" % (cnt % 2)
                            lg, lgk = LG[cnt % 2], "LGs%d" % (cnt % 2)
                            ex, exk = EX[cnt % 2], "EX%d" % (cnt % 2)
                            cnt += 1
                            k.op("pe", lambda e: e.matmul(p[:], lhsT=KSLT[:, kt * 128:(kt + 1) * 128], rhs=QBh[:, t0:t0 + 512], start=True, stop=False),
                                 reads=["KSLT", "QBh"], writes=[pk])
                            k.op("pe", lambda e: e.matmul(p[:], lhsT=EALL[:, kt * 128:(kt + 1) * 128], rhs=PENT[:, t0:t0 + 512], start=False, stop=True),
                                 reads=["EALL", "PENT%d" % Q], writes=[pk])
                            if D0 <= 896:
                                di = (D0 + 384) // 128
                                k.op("dve", lambda e: e.scalar_tensor_tensor(out=lg[:], in0=p[:], scalar=SCALE, in1=SELB[:, di, :], op0=ALU.mult, op1=ALU.add),
                                     reads=[pk, "SELB"], writes=[lgk])
                                k.op("act", lambda e: e.activation(out=ex[:], in_=lg[:], func=AF.Exp), reads=[lgk], writes=[exk])
                            else:
                                k.op("act", lambda e: e.activation(out=ex[:], in_=p[:], func=AF.Exp, scale=SCALE, bias=TAB31[:, hj:hj + 1]),
                                     reads=[pk, "TAB31"], writes=[exk])
                            for sub in range(4):
                                if kt <= 4 * Q + sub:
                                    k.op("pe", lambda e: e.matmul(psb[2 + sub][:, 0:65], lhsT=ex[:, sub * 128:(sub + 1) * 128], rhs=VSL[:, kt, :],
                                                                  start=(kt == 0), stop=(kt == 4 * Q + sub)),
                                         reads=[exk, "VSL"], writes=["psb%d" % (2 + sub)])
                        for sub in range(4):
                            tt = 4 * Q + sub
                            pa, pak = psb[2 + sub], "psb%d" % (2 + sub)
                            k.op("dve", lambda e: e.reciprocal(out=RZ[:, 0:1], in_=pa[:, 64:65]), reads=[pak, "RZs"], writes=["RZs"])
                            k.op("dve", lambda e: e.tensor_tensor(out=RZ[:, 1:2], in0=RZ[:, 0:1], in1=GS[:, tt, hj * 3 + 1:hj * 3 + 2], op=ALU.mult),
                                 reads=["RZs", "GS"], writes=["RZs"])
                            k.op("dve", lambda e: e.scalar_tensor_tensor(out=OUT[:, tt, hj * 64:(hj + 1) * 64], in0=pa[:, 0:64], scalar=RZ[:, 1:2],
                                                                         in1=OUT[:, tt, hj * 64:(hj + 1) * 64], op0=ALU.mult, op1=ALU.add),
                                 reads=[pak, "RZs", "OUT%d" % tt], writes=["OUT%d" % tt])
                        jw_min = 4 if Q == 0 else 0
                        for jw in range(jw_min, 8):
                            s0 = t0 - 512 + 128 * jw
                            kt = s0 // 128
                            p, pk = psb[cnt % 2], "psb%d" % (cnt % 2)
                            lg, lgk = LG[cnt % 2], "LGs%d" % (cnt % 2)
                            ex, exk = EX[cnt % 2], "EX%d" % (cnt % 2)
                            cnt += 1
                            k.op("pe", lambda e: e.matmul(p[:], lhsT=KSWT[:, kt * 128:(kt + 1) * 128], rhs=QBh[:, t0:t0 + 512], start=True, stop=True),
                                 reads=["KSWT", "QBh"], writes=[pk])
                            k.op("dve", lambda e: e.scalar_tensor_tensor(out=lg[:], in0=p[:], scalar=SCALE, in1=WINB[:, jw, :], op0=ALU.mult, op1=ALU.add),
                                 reads=[pk, "WINB"], writes=[lgk])
                            k.op("act", lambda e: e.activation(out=ex[:], in_=lg[:], func=AF.Exp), reads=[lgk], writes=[exk])
                            for sub in range(4):
                                first = max(sub, jw_min)
                                if first <= jw <= sub + 4:
                                    k.op("pe", lambda e: e.matmul(psb[2 + sub][:, 0:65], lhsT=ex[:, sub * 128:(sub + 1) * 128], rhs=VSW[:, kt, :],
                                                                  start=(jw == first), stop=(jw == sub + 4)),
                                         reads=[exk, "VSW"], writes=["psb%d" % (2 + sub)])
                        for sub in range(4):
                            tt = 4 * Q + sub
                            pa, pak = psb[2 + sub], "psb%d" % (2 + sub)
                            k.op("dve", lambda e: e.reciprocal(out=RZ[:, 0:1], in_=pa[:, 64:65]), reads=[pak, "RZs"], writes=["RZs"])
                            k.op("dve", lambda e: e.tensor_tensor(out=RZ[:, 1:2], in0=RZ[:, 0:1], in1=GS[:, tt, hj * 3 + 2:hj * 3 + 3], op=ALU.mult),
                                 reads=["RZs", "GS"], writes=["RZs"])
                            k.op("dve", lambda e: e.scalar_tensor_tensor(out=OUT[:, tt, hj * 64:(hj + 1) * 64], in0=pa[:, 0:64], scalar=RZ[:, 1:2],
                                                                         in1=OUT[:, tt, hj * 64:(hj + 1) * 64], op0=ALU.mult, op1=ALU.add),
                                 reads=[pak, "RZs", "OUT%d" % tt], writes=["OUT%d" % tt])
        for tt in range(64):
            k.dma("sp", ynsa[tt * 128:(tt + 1) * 128, :], OUT[:, tt, :], reads=["OUT%d" % tt], is_output=True)


THR = [0, 1, 2, 3, 4, 5, 6, 7, 8, 9, 10, 11, 12, 13, 14, 15, 16, 21, 27, 35, 46, 59, 77, 99, 128, 166, 216, 280, 363, 470, 609, 790]
PI = np.array([128 * (i // 128) + 127 - i % 128 for i in range(8192)])
_CACHE = {}


def nsa_consts():
    d = DMIN + np.arange(ND)
    bucket = np.zeros(ND, np.int64)
    for kk in range(1, 32):
        bucket += (d >= THR[kk])
    OH = np.zeros((33, ND), np.float32)
    OHW = np.zeros((33, ND), np.float32)
    valid = d >= 0
    OH[bucket[valid], np.nonzero(valid)[0]] = 1.0
    OH[32, ~valid] = 1.0
    vw = (d >= 0) & (d < 512)
    OHW[bucket[vw], np.nonzero(vw)[0]] = 1.0
    OHW[32, ~vw] = 1.0
    n = PI[:512]
    jj = np.arange(128)
    ov = ((16 * n[:, None] < 64 * (jj[None, :] + 1)) & (16 * n[:, None] + 31 >= 64 * jj[None, :]) & (n[:, None] <= 510)).astype(np.float32)
    ovv = np.concatenate([ov, (n[:, None] <= 510).astype(np.float32)], 1)
    ovv = np.ascontiguousarray(ovv.reshape(4, 128, 129).transpose(1, 0, 2))
    eall = np.zeros((128, 8192), np.float32)
    eall[PI // 64, np.arange(8192)] = 240000.0
    return dict(OH=OH, OHW=OHW, ovv=ovv, eall=eall)


def core_heads(c):
    g = c // 2
    own = [4 * g + 2 * (c % 2), 4 * g + 2 * (c % 2) + 1]
    oth = [4 * g + 2 * (1 - c % 2), 4 * g + 2 * (1 - c % 2) + 1]
    return g, own + oth


def s5_maps(proj, inp, l, c):
    m = {}
    m["uT"] = np.ascontiguousarray(proj[:, 64 * c:64 * c + 64].T)
    lam = np.zeros((128, 2, 3), np.float32)
    bre = np.zeros((128, 2, 64), np.float32)
    bim = np.zeros_like(bre)
    cre = np.zeros_like(bre)
    cim = np.zeros_like(bre)
    for rt in range(2):
        for half in range(2):
            gl = 2 * rt + half
            g = 4 * c + gl
            rows = slice(half * 64, half * 64 + 64)
            lam[rows, rt, 0] = inp["s5_lambda_re"][l, g]
            lam[rows, rt, 1] = inp["s5_lambda_im"][l, g]
            lam[rows, rt, 2] = inp["s5_log_dt"][l, g]
            bre[rows, rt, gl * 16:gl * 16 + 16] = inp["s5_b_re"][l, g]
            bim[rows, rt, gl * 16:gl * 16 + 16] = inp["s5_b_im"][l, g]
            cre[rows, rt, gl * 16:gl * 16 + 16] = inp["s5_c_re"][l, g].T
            cim[rows, rt, gl * 16:gl * 16 + 16] = inp["s5_c_im"][l, g].T
    m.update(lam=lam, bre=bre, bim=bim, cre=cre, cim=cim)
    m["dvec"] = np.ascontiguousarray(inp["s5_d"][l, 64 * c:64 * c + 64].reshape(64, 1))
    return m


def sb_maps(proj, c):
    return {"sqT": np.ascontiguousarray(proj[:, 512 + 64 * c:512 + 64 * c + 64].T),
            "skT": np.ascontiguousarray(proj[:, 1024 + 64 * c:1024 + 64 * c + 64].T),
            "sv": np.ascontiguousarray(proj[:, 1536 + 64 * c:1536 + 64 * c + 64])}


def nsa_maps(proj, inp, l, c, consts):
    g, heads = core_heads(c)
    m = dict(consts)
    m["nqT"] = np.ascontiguousarray(np.stack([proj[:, 2048 + h * 64:2048 + h * 64 + 64].T for h in heads], 1))

    def seg(a):
        x = a.reshape(512, 16, 64)
        n = PI[:512]
        A = x[n]
        B = np.zeros_like(A)
        ok = n + 1 <= 511
        B[ok] = x[n[ok] + 1]
        return np.ascontiguousarray(A.transpose(2, 0, 1)), np.ascontiguousarray(B.transpose(2, 0, 1))
    m["kcA"], m["kcB"] = seg(proj[:, 3072 + 64 * g:3072 + 64 * g + 64])
    m["vcA"], m["vcB"] = seg(proj[:, 3328 + 64 * g:3328 + 64 * g + 64])
    m["kslT"] = np.ascontiguousarray(proj[:, 3584 + 64 * g:3584 + 64 * g + 64][PI].T)
    m["vsl"] = np.ascontiguousarray(proj[:, 3840 + 64 * g:3840 + 64 * g + 64][PI])
    m["kswT"] = np.ascontiguousarray(proj[:, 4096 + 64 * g:4096 + 64 * g + 64][PI].T)
    m["vsw"] = np.ascontiguousarray(proj[:, 4352 + 64 * g:4352 + 64 * g + 64][PI])
    m["graw"] = np.ascontiguousarray(np.concatenate([proj[:, 4608 + h * 3:4608 + h * 3 + 3] for h in heads[:2]], 1))
    m["cw1"] = np.ascontiguousarray(inp["nsa_cmp_w1"][l].reshape(2, 32, 64, 128).transpose(2, 0, 1, 3))
    m["cw2"] = np.ascontiguousarray(inp["nsa_cmp_w2"][l].transpose(1, 0, 2))
    m["posT"] = np.ascontiguousarray(inp["nsa_cmp_pos"][l].transpose(2, 0, 1))
    tab = inp["rel_bias"][:, heads]
    m["tabx"] = np.ascontiguousarray(np.concatenate([tab, np.full((1, 4), -30000.0, np.float32)], 0))
    m["tab31"] = np.ascontiguousarray(tab[31:32, :])
    return m


def declare_mixer(k):
    a = dict(uT=k.dram_in("uT", [64, T]), lam=k.dram_in("lam", [128, 2, 3]), bre=k.dram_in("bre", [128, 2, 64]), bim=k.dram_in("bim", [128, 2, 64]),
             cre=k.dram_in("cre", [128, 2, 64]), cim=k.dram_in("cim", [128, 2, 64]), dvec=k.dram_in("dvec", [64, 1]), y5T=k.dram_out("y5T", [64, T]))
    b = dict(qT=k.dram_in("sqT", [64, T]), kT=k.dram_in("skT", [64, T]), v=k.dram_in("sv", [T, 64]), oT=k.dram_out("sboT", [64, T]))
    n = dict(nqT=k.dram_in("nqT", [64, 4, T]), kcA=k.dram_in("kcA", [64, 512, 16]), kcB=k.dram_in("kcB", [64, 512, 16]),
             vcA=k.dram_in("vcA", [64, 512, 16]), vcB=k.dram_in("vcB", [64, 512, 16]), kslT=k.dram_in("kslT", [64, T]), vsl=k.dram_in("vsl", [T, 64]),
             kswT=k.dram_in("kswT", [64, T]), vsw=k.dram_in("vsw", [T, 64]), graw=k.dram_in("graw", [T, 6]),
             w1=k.dram_in("cw1", [64, 2, 32, 128]), w2=k.dram_in("cw2", [128, 2, 64]), posT=k.dram_in("posT", [64, 2, 32]),
             tabx=k.dram_in("tabx", [33, 4]), tab31=k.dram_in("tab31", [1, 4]), OH=k.dram_in("OH", [33, ND]), OHW=k.dram_in("OHW", [33, ND]),
             ovv=k.dram_in("ovv", [128, 4, 129]), eall=k.dram_in("eall", [128, T]), ynsa=k.dram_out("ynsa", [T, 128]))
    return a, b, n


def build_mixer():
    k = K()
    psb = [k.ps("psb%d" % i, [128, 512]) for i in range(7)]
    ptb = k.ps("ptb", [128, 1024], BF16)
    a, b, n = declare_mixer(k)
    build_s5(k, psb=psb, **a)
    build_sb(k, psb=psb, **b)
    build_nsa(k, psb=psb, ptb=ptb, **n)
    k.finish()
    return k.nc


def halo_rows(a, c):
    if c == 0:
        return np.concatenate([np.zeros((128, a.shape[1]), a.dtype), a[0:1024]], 0)
    return a[1024 * c - 128:1024 * c + 1024]


def dense_in_maps(hfull, y5, ysb, ynsa, l, inp, do_chain, lp):
    maps = []
    for c in range(8):
        m = {"h_in": np.ascontiguousarray(halo_rows(hfull, c))}
        if do_chain:
            m["y5T"] = np.ascontiguousarray(halo_rows(y5, c).T)
            m["ysbT"] = np.ascontiguousarray(halo_rows(ysb, c).T)
            m["ynsaT"] = np.ascontiguousarray(halo_rows(ynsa, c).T)
            m["w_glu"] = inp["s5_w_glu"][l]
            m["b_glu"] = np.ascontiguousarray(inp["s5_b_glu"][l].reshape(4, 128).T)
            m["w_out"] = inp["w_out"][l]
            m["ln_g"] = inp["ln_g"][l]
            m["ln_b"] = inp["ln_b"][l]
            m["memT"] = np.ascontiguousarray(inp["mem"][0].T)
            m["xa_wq"] = inp["xa_wq"][l]
            m["xa_wkv"] = inp["xa_wkv"][l]
            m["xa_wo"] = inp["xa_wo"][l]
            m["w_up"] = inp["ffn_w_up"][l]
            m["convw"] = np.ascontiguousarray(inp["ffn_conv_w"][l].reshape(3, 88, 128).transpose(2, 1, 0))
            m["convb"] = np.ascontiguousarray(inp["ffn_conv_b"][l].reshape(88, 128).T)
            m["w_down"] = inp["ffn_w_down"][l]
            m["flag"] = np.full((128, 1), 0.0 if c == 0 else 1.0, np.float32)
        m["w_in"] = inp["w_in"][lp]
        maps.append(m)
    return maps


def kernel(**inputs):
    inp = {kk: np.ascontiguousarray(np.asarray(v, dtype=np.float32)) for kk, v in inputs.items()}
    cores = list(range(8))
    if "P0" not in _CACHE:
        _CACHE["P0"] = build_dense(False, True)
        _CACHE["M"] = build_mixer()
        _CACHE["D"] = build_dense(True, True)
        _CACHE["consts"] = nsa_consts()
    consts = _CACHE["consts"]
    h = inp["x"][0]
    res = run_bass_kernel_spmd(_CACHE["P0"], dense_in_maps(h, None, None, None, 0, inp, False, 0), core_ids=cores)
    proj = np.concatenate([r["proj"] for r in res.results], 0)
    for l in range(4):
        maps = []
        for c in range(8):
            m = s5_maps(proj, inp, l, c)
            m.update(sb_maps(proj, c))
            m.update(nsa_maps(proj, inp, l, c, consts))
            maps.append(m)
        res = run_bass_kernel_spmd(_CACHE["M"], maps, core_ids=cores)
        y5 = np.concatenate([r["y5T"] for r in res.results], 0).T
        ysb = np.concatenate([r["sboT"] for r in res.results], 0).T
        ynsa = np.zeros((8192, 1024), np.float32)
        for c in range(8):
            g, heads = core_heads(c)
            for hj in range(2):
                ynsa[:, heads[hj] * 64:heads[hj] * 64 + 64] = res.results[c]["ynsa"][:, hj * 64:(hj + 1) * 64]
        res = run_bass_kernel_spmd(_CACHE["D"], dense_in_maps(h, y5, ysb, ynsa, l, inp, True, min(l + 1, 3)), core_ids=cores)
        h = np.concatenate([r["h_out"] for r in res.results], 0)
        proj = np.concatenate([r["proj"] for r in res.results], 0)
    return h[None].astype(np.float32)
```

```python
import numpy as np
from contextlib import ExitStack
import concourse.bass as bass
import concourse.mybir as mybir
from concourse.bass_utils import run_bass_kernel_spmd

F32 = mybir.dt.float32
BF16 = mybir.dt.bfloat16
I32 = mybir.dt.int32
AF = mybir.ActivationFunctionType
ALU = mybir.AluOpType
AX = mybir.AxisListType

NDMA_SEMS = 24
GEN_MAX = 16000


class K:
    def __init__(self):
        self.nc = bass.Bass("TRN2", target_bir_lowering=False)
        nc = self.nc
        self.es = ExitStack()
        self.es0 = self.es
        self.eng = {"pe": nc.tensor, "dve": nc.vector, "act": nc.scalar, "pool": nc.gpsimd, "sp": nc.sync}
        self.sem = {}
        self.cnt = {}
        self.gen = {e: 0 for e in self.eng}
        ngen = {"pe": 10, "dve": 6, "act": 6, "pool": 3, "sp": 1}
        for e in self.eng:
            for g in range(ngen[e]):
                self.sem[(e, g)] = self.es.enter_context(nc.semaphore("s_%s_%d" % (e, g)))
                self.cnt[(e, g)] = 0
        self.csems = [self.es.enter_context(nc.semaphore("s_cc%d" % i)) for i in range(10)]
        self.ccnt = 0
        self.waited = {}
        self.dsem = [self.es.enter_context(nc.semaphore("d%d" % i)) for i in range(NDMA_SEMS)]
        self.dcnt = [0] * NDMA_SEMS
        self.dnext = 0
        self.lastw = {}
        self.readers = {}
        self.ninstr = 0
        self.out_tokens = []

    def sb(self, name, shape, dt=F32):
        self.nalloc = getattr(self, "nalloc", 0) + 1
        return self.es.enter_context(self.nc.sbuf_tensor("%s_%d" % (name, self.nalloc), list(shape), dt))

    def ps(self, name, shape, dt=F32):
        return self.es.enter_context(self.nc.psum_tensor(name, list(shape), dt))

    def dram_in(self, name, shape, dt=F32):
        return self.nc.dram_tensor(name, list(shape), dt, kind="ExternalInput").ap()

    def dram_out(self, name, shape, dt=F32):
        return self.nc.dram_tensor(name, list(shape), dt, kind="ExternalOutput").ap()

    def dram_tmp(self, name, shape, dt=F32):
        return self.nc.dram_tensor(name, list(shape), dt).ap()

    def fill(self, v):
        if not hasattr(self, "_fills"):
            self._fills = {}
        if v not in self._fills:
            self._fills[v] = self.nc.gpsimd.to_reg(float(v))
        return self._fills[v]

    def _wait(self, e, tok):
        kind, src, val = tok
        k = (e, kind, src)
        if self.waited.get(k, 0) >= val:
            return
        self.waited[k] = val
        sem = self.sem[src] if kind == "e" else (self.csems[src] if kind == "c" else self.dsem[src])
        self.eng[e].wait_ge(sem, val)

    def _deps(self, e, reads, writes):
        toks = []
        for r in reads:
            if r in self.lastw:
                toks.append(self.lastw[r])
        for w in writes:
            if w in self.lastw:
                toks.append(self.lastw[w])
            toks.extend(self.readers.get(w, ()))
        for t in toks:
            if t[0] == "e" and t[1][0] == e and e == "pe":
                continue
            self._wait(e, t)

    def _commit(self, tok, reads, writes):
        for w in writes:
            self.lastw[w] = tok
            self.readers[w] = []
        for r in reads:
            if r not in writes:
                self.readers.setdefault(r, []).append(tok)
                if len(self.readers[r]) > 64:
                    self.readers[r] = self._compact(self.readers[r])

    @staticmethod
    def _compact(toks):
        best = {}
        for t in toks:
            k = (t[0], t[1])
            if k not in best or best[k][2] < t[2]:
                best[k] = t
        return list(best.values())

    def op(self, e, fn, reads=(), writes=()):
        self._deps(e, reads, writes)
        ins = fn(self.eng[e])
        g = self.gen[e]
        if self.cnt[(e, g)] >= GEN_MAX:
            g += 1
            self.gen[e] = g
            assert (e, g) in self.sem, "out of pre-allocated semaphore generations for " + e
        self.cnt[(e, g)] += 1
        ins.then_inc(self.sem[(e, g)], 1)
        tok = ("e", (e, g), self.cnt[(e, g)])
        self._commit(tok, reads, writes)
        self.ninstr += 1
        return tok

    def dma(self, e, out, in_, reads=(), writes=(), is_output=False, **kw):
        k = self.dnext
        self.dnext = (self.dnext + 1) % NDMA_SEMS
        if self.dcnt[k] > 0:
            self._wait(e, ("d", k, self.dcnt[k]))
        self._deps(e, reads, writes)
        ins = self.eng[e].dma_start(out=out, in_=in_, **kw)
        self.dcnt[k] += 16
        ins.then_inc(self.dsem[k], 16)
        tok = ("d", k, self.dcnt[k])
        self._commit(tok, reads, writes)
        self.ninstr += 1
        if is_output:
            self.out_tokens.append(tok)
        return tok

    def finish(self):
        for k in range(NDMA_SEMS):
            if self.dcnt[k] > 0:
                self._wait("sp", ("d", k, self.dcnt[k]))
        for e in self.eng:
            g = self.gen[e]
            if e != "sp" and self.cnt[(e, g)] > 0:
                self._wait("sp", ("e", (e, g), self.cnt[(e, g)]))
            elif e != "sp" and g > 0:
                self._wait("sp", ("e", (e, g - 1), self.cnt[(e, g - 1)]))
        if self.ccnt > 0:
            self._wait("sp", ("c", self.ccnt - 1, 1))
        self.es.close()
        return self.nc


def _barrier(self):
    for e in self.eng:
        for s in self.eng:
            g = self.gen[s]
            if s != e and self.cnt[(s, g)] > 0:
                self._wait(e, ("e", (s, g), self.cnt[(s, g)]))
            elif s != e and g > 0:
                self._wait(e, ("e", (s, g - 1), self.cnt[(s, g - 1)]))
        if self.ccnt > 0:
            self._wait(e, ("c", self.ccnt - 1, 1))
        for q in range(NDMA_SEMS):
            if self.dcnt[q] > 0:
                self._wait(e, ("d", q, self.dcnt[q]))


class _Scope:
    def __init__(self, k):
        self.k = k

    def __enter__(self):
        self.saved = self.k.es
        self.k.es = ExitStack()
        return self

    def __exit__(self, *a):
        self.k.barrier()
        self.k.es.close()
        self.k.es = self.saved
        return False


K.barrier = _barrier
K.scope = lambda self: _Scope(self)


def _collective(self, kind, src, dst, reads=(), writes=()):
    self._deps("pool", reads, writes)
    ins = self.nc.gpsimd.collective_compute(kind, ALU.bypass, replica_groups=[list(range(8))], ins=[src.opt()], outs=[dst.opt()])
    sem = self.csems[self.ccnt]
    ins.then_inc(sem)
    tok = ("c", self.ccnt, 1)
    self.ccnt += 1
    self._commit(tok, reads, writes)
    return tok


K.collective = _collective


def _gather(self, out, in_, idx, reads=(), writes=()):
    q = self.dnext
    self.dnext = (self.dnext + 1) % NDMA_SEMS
    if self.dcnt[q] > 0:
        self._wait("pool", ("d", q, self.dcnt[q]))
    self._deps("pool", reads, writes)
    ins = self.nc.gpsimd.indirect_dma_start(out=out, out_offset=None, in_=in_, in_offset=bass.IndirectOffsetOnAxis(ap=idx, axis=0))
    self.dcnt[q] += 16
    ins.then_inc(self.dsem[q], 16)
    tok = ("d", q, self.dcnt[q])
    self._commit(tok, reads, writes)
    self.ninstr += 1
    return tok


K.gather = _gather
U32 = mybir.dt.uint32

from concourse.bass_types import AP
import math

ALPHA = (2.0 * 4) ** 0.25
LN_EPS = 1e-5
D = 2048
DFF = 5632
NIN = 4656
NT = 9
TOK = NT * 128
XA_SCALE = 128 ** -0.5


def dense_layer(k, l, NL, W, h_in, G1, G2, B1, HSAVE, h_out, IDX, psb, stage=None):
    nc = k.nc
    w_glu, b_glu, w_out, ln_g, ln_b = W["w_glu"][l], W["b_glu"][l], W["w_out"][l], W["ln_g"][l], W["ln_b"][l]
    memT, xa_wq, xa_wkv, xa_wo = W["memT"], W["xa_wq"][l], W["xa_wkv"][l], W["xa_wo"][l]
    w_up, convw, convb, w_down, flag = W["w_up"][l], W["convw"][l], W["convb"][l], W["w_down"][l], W["flag"]
    do_chain = True
    do_proj = l < NL - 1
    G2a = G2.rearrange("r (w t) -> (r w) t", w=8)
    G2b = G2.rearrange("r (w t) -> (r w) t", w=24)
    H = k.sb("H", [128, NT, D])
    HT = k.sb("HT", [128, 16, TOK], BF16)
    WB = [k.sb("WB%d" % i, [128, 16, 256], BF16) for i in range(2)]
    ident = k.sb("ident", [128, 128])
    st = {"p": 0, "w": 0}

    def ps():
        i = st["p"]
        st["p"] = (i + 1) % 7
        return psb[i], "psb%d" % i

    def wb():
        i = st["w"]
        st["w"] = (i + 1) % 2
        return WB[i], "WB%d" % i

    k.op("pool", lambda e: e.memset(ident[:], 1.0), writes=["ident"])
    k.op("pool", lambda e: e.affine_select(out=ident[:], in_=ident[:], pattern=[[-1, 128]], compare_op=ALU.is_equal,
                                            fill=k.fill(0.0), base=0, channel_multiplier=1), reads=["ident"], writes=["ident"])
    if l == 0:
        for i in range(NT):
            k.dma("sp", H[:, i, :], h_in[i * 128:(i + 1) * 128, :], writes=["H%d" % i])
    else:
        for i in range(1, NT):
            k.dma("sp", H[:, i, :], HSAVE[i * 128:(i + 1) * 128, :], reads=["HSAVE"], writes=["H%d" % i])
        k.gather(H[:, 0, :], G1.rearrange("(a b) t -> a (b t)", b=2), IDX[:, 144:145], reads=["G1", "IDX"], writes=["H0"])

    def ht_keys(c0, c1):
        return ["HT%d" % i for i in range(c0 // 128, (c1 - 1) // 128 + 1)]

    def transposes(i):
        for kb in range(4):
            p, pk = ps()
            for q in range(4):
                kk = kb * 4 + q
                k.op("pe", lambda e: e.transpose(out=p[:, q * 128:(q + 1) * 128], in_=H[:, i, kk * 128:(kk + 1) * 128],
                                                 identity=ident[:]), reads=["H%d" % i, "ident"], writes=[pk])
            k.op("act", lambda e: e.activation(out=HT[:, kb * 4:(kb + 1) * 4, i * 128:(i + 1) * 128],
                                               in_=p[:].rearrange("p (a b) -> p a b", a=4), func=AF.Copy),
                 reads=[pk], writes=["HT%d" % i])

    def layer_norm(i, GB, stat, mv, sd):
        x = H[:, i, :]
        hk = "H%d" % i
        for c in range(4):
            k.op("dve", lambda e: e.bn_stats(out=stat[:, c, :], in_=H[:, i, c * 512:(c + 1) * 512]), reads=[hk], writes=["stat"])
        k.op("dve", lambda e: e.bn_aggr(out=mv[:], in_=stat[:].rearrange("p a b -> p (a b)")), reads=["stat"], writes=["mv"])
        k.op("dve", lambda e: e.tensor_scalar(out=sd[:], in0=mv[:, 1:2], scalar1=LN_EPS, scalar2=None, op0=ALU.add),
             reads=["mv"], writes=["sd"])
        k.op("act", lambda e: e.activation(out=sd[:], in_=sd[:], func=AF.Sqrt), reads=["sd"], writes=["sd"])
        k.op("dve", lambda e: e.reciprocal(out=sd[:], in_=sd[:]), reads=["sd"], writes=["sd"])
        k.op("dve", lambda e: e.tensor_scalar(out=x, in0=x, scalar1=mv[:, 0:1], scalar2=sd[:, 0:1], op0=ALU.subtract, op1=ALU.mult),
             reads=[hk, "mv", "sd"], writes=[hk])
        k.op("pool", lambda e: e.tensor_tensor(out=x, in0=x, in1=GB[:, 0, :], op=ALU.mult), reads=[hk, "GB"], writes=[hk])
        k.op("pool", lambda e: e.tensor_tensor(out=x, in0=x, in1=GB[:, 1, :], op=ALU.add), reads=[hk, "GB"], writes=[hk])

    def load_gb(GB, which):
        k.dma("sp", GB[:, 0, :], ln_g[which, :].partition_broadcast(128), writes=["GB"])
        k.dma("sp", GB[:, 1, :], ln_b[which, :].partition_broadcast(128), writes=["GB"])

    def resid(i, c0, c1, p, pk, first=True):
        x = H[:, i, c0:c1]
        if first:
            k.op("dve", lambda e: e.scalar_tensor_tensor(out=x, in0=x, scalar=ALPHA, in1=p, op0=ALU.mult, op1=ALU.add),
                 reads=["H%d" % i, pk], writes=["H%d" % i])
        else:
            k.op("dve", lambda e: e.tensor_tensor(out=x, in0=x, in1=p, op=ALU.add), reads=["H%d" % i, pk], writes=["H%d" % i])

    if do_chain:
        with k.scope():
            CATT = k.sb("CATT", [128, 16, TOK], BF16)
            Y5 = k.sb("Y5", [128, 4, 384])
            Z = k.sb("Z", [128, 4, 384])
            ZB = k.sb("ZB", [128, 4, 384], BF16)
            SG = k.sb("SG", [128, 384])
            WGLU = k.sb("WGLU", [128, 4, 512], BF16)
            BGLU = k.sb("BGLU", [128, 4])
            GB = k.sb("GB", [128, 2, D])
            stat = k.sb("stat", [128, 4, 6])
            mv = k.sb("mv", [128, 2])
            sd = k.sb("sd", [128, 1])
            load_gb(GB, 0)
            k.dma("pool", WGLU[:], w_glu.rearrange("(c p) n -> p c n", p=128), writes=["WGLU"])
            k.dma("sp", BGLU[:], b_glu[:, :], writes=["BGLU"])
            for kc in range(4):
                k.gather(CATT[:, 4 + kc, :], G2a, IDX[:, 132 + kc:133 + kc], reads=["G2", "IDX"], writes=["CA_sb"])
            for r in range(8):
                k.gather(CATT[:, 8 + r, :], G2a, IDX[:, 136 + r:137 + r], reads=["G2", "IDX"], writes=["CA_nsa"])
            for tc in range(3):
                cs = slice(tc * 384, (tc + 1) * 384)
                for kc in range(4):
                    k.gather(Y5[:, kc, :], G2b, IDX[:, 120 + kc * 3 + tc:121 + kc * 3 + tc], reads=["G2", "IDX"], writes=["Y5"])
                k.op("act", lambda e: e.activation(out=Z[:], in_=Y5[:], func=AF.Gelu_apprx_tanh), reads=["Y5"], writes=["Z"])
                k.op("pool", lambda e: e.tensor_copy(out=ZB[:], in_=Z[:]), reads=["Z"], writes=["ZB"])
                for co in range(4):
                    p, pk = ps()
                    for ci in range(4):
                        k.op("pe", lambda e: e.matmul(p[:, 0:384], lhsT=WGLU[:, ci, co * 128:(co + 1) * 128], rhs=ZB[:, ci, :],
                                                      start=(ci == 0), stop=(ci == 3)), reads=["WGLU", "ZB"], writes=[pk])
                    k.op("act", lambda e: e.activation(out=SG[:], in_=p[:, 0:384], func=AF.Sigmoid, bias=BGLU[:, co:co + 1]),
                         reads=[pk, "BGLU"], writes=["SG"])
                    k.op("dve", lambda e: e.tensor_tensor(out=CATT[:, co, cs], in0=Z[:, co, :], in1=SG[:], op=ALU.mult),
                         reads=["Z", "SG"], writes=["CA_s5_%d" % tc])
            for n in range(8):
                w, wk = wb()
                k.dma("pool", w[:], w_out[:, n * 256:(n + 1) * 256].rearrange("(k p) n -> p k n", p=128), writes=[wk])
                for i in range(NT):
                    p, pk = ps()
                    for kk in range(16):
                        k.op("pe", lambda e: e.matmul(p[:, 0:256], lhsT=CATT[:, kk, i * 128:(i + 1) * 128], rhs=w[:, kk, :],
                                                      start=(kk == 0), stop=(kk == 15)),
                             reads=[wk, "CA_s5_%d" % (i // 3), "CA_sb", "CA_nsa"], writes=[pk])
                    resid(i, n * 256, (n + 1) * 256, p[:, 0:256], pk)
            for i in range(NT):
                layer_norm(i, GB, stat, mv, sd)
                transposes(i)

        if stage == 4:
            for i in range(1, NT):
                k.dma("sp", h_out[(i - 1) * 128:i * 128, :], H[:, i, :], reads=["H%d" % i], is_output=True)
            return
        with k.scope():
            MEMT = k.sb("MEMT", [128, 16, 256], BF16)
            KT = k.sb("KT", [128, 4, 256], BF16)
            V = k.sb("V", [128, 2, 512], BF16)
            QT = k.sb("QT", [128, 384], BF16)
            PT = k.sb("PT", [128, 2, 384], BF16)
            RZ = k.sb("RZ", [128, 384])
            OT = k.sb("OT", [128, 4, TOK], BF16)
            WO = k.sb("WO", [128, 4, D], BF16)
            ONES = k.sb("ONES", [128, 128], BF16)
            GB = k.sb("GB", [128, 2, D])
            stat = k.sb("stat", [128, 4, 6])
            mv = k.sb("mv", [128, 2])
            sd = k.sb("sd", [128, 1])
            load_gb(GB, 1)
            k.op("pool", lambda e: e.memset(ONES[:], 1.0), writes=["ONES"])
            k.dma("pool", MEMT[:], memT.rearrange("(k p) m -> p k m", p=128), writes=["MEMT"])
            k.dma("pool", WO[:], xa_wo.rearrange("(h p) n -> p h n", p=128), writes=["WO"])
            for c in range(4):
                w, wk = wb()
                k.dma("pool", w[:], xa_wkv[:, c * 256:(c + 1) * 256].rearrange("(k p) n -> p k n", p=128), writes=[wk])
                if c < 2:
                    for hh in range(2):
                        p, pk = ps()
                        for kk in range(16):
                            k.op("pe", lambda e: e.matmul(p[:, 0:256], lhsT=w[:, kk, hh * 128:(hh + 1) * 128], rhs=MEMT[:, kk, :],
                                                          start=(kk == 0), stop=(kk == 15)), reads=[wk, "MEMT"], writes=[pk])
                        k.op("act", lambda e: e.activation(out=KT[:, 2 * c + hh, :], in_=p[:, 0:256], func=AF.Copy),
                             reads=[pk], writes=["KT"])
                else:
                    for mt in range(2):
                        p, pk = ps()
                        for kk in range(16):
                            k.op("pe", lambda e: e.matmul(p[:, 0:256], lhsT=MEMT[:, kk, mt * 128:(mt + 1) * 128], rhs=w[:, kk, :],
                                                          start=(kk == 0), stop=(kk == 15)), reads=[wk, "MEMT"], writes=[pk])
                        k.op("act", lambda e: e.activation(out=V[:, mt, (c - 2) * 256:(c - 1) * 256], in_=p[:, 0:256], func=AF.Copy),
                             reads=[pk], writes=["V"])
            for c in range(2):
                w, wk = wb()
                k.dma("pool", w[:], xa_wq[:, c * 256:(c + 1) * 256].rearrange("(k p) n -> p k n", p=128), writes=[wk])
                for tc in range(3):
                    cs = slice(tc * 384, (tc + 1) * 384)
                    for hh in range(2):
                        h = 2 * c + hh
                        p, pk = ps()
                        for kk in range(16):
                            k.op("pe", lambda e: e.matmul(p[:, 0:384], lhsT=w[:, kk, hh * 128:(hh + 1) * 128], rhs=HT[:, kk, cs],
                                                          start=(kk == 0), stop=(kk == 15)),
                                 reads=[wk] + ht_keys(tc * 384, tc * 384 + 384), writes=[pk])
                        k.op("act", lambda e: e.activation(out=QT[:], in_=p[:, 0:384], func=AF.Copy), reads=[pk], writes=["QT"])
                        for mt in range(2):
                            p2, pk2 = ps()
                            k.op("pe", lambda e: e.matmul(p2[:, 0:384], lhsT=KT[:, h, mt * 128:(mt + 1) * 128], rhs=QT[:],
                                                          start=True, stop=True), reads=["KT", "QT"], writes=[pk2])
                            k.op("act", lambda e: e.activation(out=PT[:, mt, :], in_=p2[:, 0:384], func=AF.Exp, scale=XA_SCALE),
                                 reads=[pk2], writes=["PT%d" % mt])
                        po, pko = ps()
                        pz, pkz = ps()
                        for mt in range(2):
                            k.op("pe", lambda e: e.matmul(po[:, 0:384], lhsT=V[:, mt, h * 128:(h + 1) * 128], rhs=PT[:, mt, :],
                                                          start=(mt == 0), stop=(mt == 1)), reads=["V", "PT%d" % mt], writes=[pko])
                        for mt in range(2):
                            k.op("pe", lambda e: e.matmul(pz[:, 0:384], lhsT=ONES[:], rhs=PT[:, mt, :],
                                                          start=(mt == 0), stop=(mt == 1)), reads=["ONES", "PT%d" % mt], writes=[pkz])
                        k.op("dve", lambda e: e.reciprocal(out=RZ[:], in_=pz[:, 0:384]), reads=[pkz], writes=["RZ"])
                        k.op("dve", lambda e: e.tensor_tensor(out=OT[:, h, cs], in0=po[:, 0:384], in1=RZ[:], op=ALU.mult),
                             reads=[pko, "RZ"], writes=["OT%d_%d" % (h, tc)])
            for i in range(NT):
                for n in range(4):
                    p, pk = ps()
                    for h in range(4):
                        k.op("pe", lambda e: e.matmul(p[:], lhsT=OT[:, h, i * 128:(i + 1) * 128], rhs=WO[:, h, n * 512:(n + 1) * 512],
                                                      start=(h == 0), stop=(h == 3)),
                             reads=["WO", "OT%d_%d" % (h, i // 3)], writes=[pk])
                    resid(i, n * 512, (n + 1) * 512, p[:], pk)
                layer_norm(i, GB, stat, mv, sd)
                transposes(i)

        if stage == 5:
            for i in range(1, NT):
                k.dma("sp", h_out[(i - 1) * 128:i * 128, :], H[:, i, :], reads=["H%d" % i], is_output=True)
            return
        with k.scope():
            ACTT = k.sb("ACTT", [128, 11, 1024], BF16)
            WA = [k.sb("WA%d" % i, [128, 16, 128], BF16) for i in range(2)]
            WG = [k.sb("WG%d" % i, [128, 16, 128], BF16) for i in range(2)]
            WD = [k.sb("WD%d" % i, [128, 11, 256], BF16) for i in range(2)]
            CA = [k.sb("CA%d" % i, [128, 344]) for i in range(2)]
            CG = [k.sb("CG%d" % i, [128, 344]) for i in range(2)]
            CW = k.sb("CW", [128, 88, 3])
            CB = k.sb("CB", [128, 88])
            FL = k.sb("FL", [128, 1])
            GB = k.sb("GB", [128, 2, D])
            stat = k.sb("stat", [128, 4, 6])
            mv = k.sb("mv", [128, 2])
            sd = k.sb("sd", [128, 1])
            load_gb(GB, 2)
            k.dma("sp", CW[:], convw[:, :, :], writes=["CW"])
            k.dma("sp", CB[:], convb[:, :], writes=["CB"])
            k.dma("sp", FL[:], flag[:, :], writes=["FL"])
            pieces = [(0, 342), (342, 342), (684, 340)]
            cnt = 0
            wdc = 0
            for g in range(4):
                for jj in range(11):
                    j = g * 11 + jj
                    wa, wak = WA[cnt % 2], "WA%d" % (cnt % 2)
                    wg, wgk = WG[cnt % 2], "WG%d" % (cnt % 2)
                    k.dma("pool", wa[:], w_up[:, j * 128:(j + 1) * 128].rearrange("(k p) n -> p k n", p=128), writes=[wak])
                    k.dma("pool", wg[:], w_up[:, DFF + j * 128:DFF + (j + 1) * 128].rearrange("(k p) n -> p k n", p=128), writes=[wgk])
                    for pi, (t0, n) in enumerate(pieces):
                        c0 = 126 + t0
                        hk = ht_keys(c0, c0 + n + 2)
                        ca, cak = CA[cnt % 2], "CA%d" % (cnt % 2)
                        cg, cgk = CG[cnt % 2], "CG%d" % (cnt % 2)
                        cnt += 1
                        for (wt, wtk, ch, buf, bk) in ((wa, wak, j, ca, cak), (wg, wgk, 44 + j, cg, cgk)):
                            p, pk = ps()
                            for kk in range(16):
                                k.op("pe", lambda e: e.matmul(p[:, 0:n + 2], lhsT=wt[:, kk, :], rhs=HT[:, kk, c0:c0 + n + 2],
                                                              start=(kk == 0), stop=(kk == 15)), reads=[wtk] + hk, writes=[pk])
                            if pi == 0:
                                k.op("dve", lambda e: e.tensor_scalar(out=p[:, 0:2], in0=p[:, 0:2], scalar1=FL[:, 0:1], scalar2=None,
                                                                      op0=ALU.mult), reads=[pk, "FL"], writes=[pk])
                            k.op("act", lambda e: e.activation(out=buf[:, 0:n], in_=p[:, 2:n + 2], func=AF.Identity,
                                                               scale=CW[:, ch, 2:3], bias=CB[:, ch:ch + 1]),
                                 reads=[pk, "CW", "CB"], writes=[bk])
                            k.op("dve", lambda e: e.scalar_tensor_tensor(out=buf[:, 0:n], in0=p[:, 1:n + 1], scalar=CW[:, ch, 1:2],
                                                                         in1=buf[:, 0:n], op0=ALU.mult, op1=ALU.add),
                                 reads=[pk, "CW", bk], writes=[bk])
                            k.op("dve", lambda e: e.scalar_tensor_tensor(out=buf[:, 0:n], in0=p[:, 0:n], scalar=CW[:, ch, 0:1],
                                                                         in1=buf[:, 0:n], op0=ALU.mult, op1=ALU.add),
                                 reads=[pk, "CW", bk], writes=[bk])
                        k.op("act", lambda e: e.activation(out=cg[:, 0:n], in_=cg[:, 0:n], func=AF.Gelu_apprx_tanh),
                             reads=[cgk], writes=[cgk])
                        k.op("pool", lambda e: e.tensor_tensor(out=ACTT[:, jj, t0:t0 + n], in0=ca[:, 0:n], in1=cg[:, 0:n], op=ALU.mult),
                             reads=[cak, cgk], writes=["AT%d" % jj])
                for n8 in range(8):
                    wd, wdk = WD[wdc % 2], "WD%d" % (wdc % 2)
                    wdc += 1
                    k.dma("pool", wd[:], w_down[g * 1408:(g + 1) * 1408, n8 * 256:(n8 + 1) * 256].rearrange("(j p) n -> p j n", p=128),
                          writes=[wdk])
                    for i in range(1, NT):
                        p, pk = ps()
                        for jj in range(11):
                            k.op("pe", lambda e: e.matmul(p[:, 0:256], lhsT=ACTT[:, jj, (i - 1) * 128:i * 128], rhs=wd[:, jj, :],
                                                          start=(jj == 0), stop=(jj == 10)), reads=[wdk, "AT%d" % jj], writes=[pk])
                        resid(i, n8 * 256, (n8 + 1) * 256, p[:, 0:256], pk, first=(g == 0))
            for i in range(1, NT):
                layer_norm(i, GB, stat, mv, sd)
                if not do_proj:
                    k.dma("sp", h_out[(i - 1) * 128:i * 128, :], H[:, i, :], reads=["H%d" % i], is_output=True)
                else:
                    k.dma("sp", HSAVE[i * 128:(i + 1) * 128, :], H[:, i, :], reads=["H%d" % i], writes=["HSAVE"])
                    transposes(i)
            if do_proj:
                k.dma("sp", B1[4656:4912, :].rearrange("(p a) t -> p (a t)", a=2), H[:, NT - 1, :], reads=["H%d" % (NT - 1), "G1"], writes=["B1"])
    if do_proj:
        proj_phase(k, HT, ht_keys, W["w_in"][l + 1], B1, psb)


def proj_phase(k, HT, ht_keys, w_in, B1, psb):
    with k.scope():
        WP = [k.sb("WP%d" % i, [128, 16, 128], BF16) for i in range(2)]
        STG = [k.sb("STG%d" % i, [128, 1024]) for i in range(2)]
        pc = 0
        for n in range(37):
            c0 = n * 128
            cw = min(128, NIN - c0)
            w, wk = WP[n % 2], "WP%d" % (n % 2)
            s, sk = STG[n % 2], "STG%d" % (n % 2)
            k.dma("pool", w[:, :, 0:cw], w_in[:, c0:c0 + cw].rearrange("(k p) n -> p k n", p=128), writes=[wk])
            for half in range(2):
                p, pk = psb[pc % 7], "psb%d" % (pc % 7)
                pc += 1
                t0 = 128 + half * 512
                for kk in range(16):
                    k.op("pe", lambda e: e.matmul(p[0:cw, :], lhsT=w[:, kk, 0:cw], rhs=HT[:, kk, t0:t0 + 512],
                                                  start=(kk == 0), stop=(kk == 15)), reads=[wk] + ht_keys(t0, t0 + 512), writes=[pk])
                k.op("act", lambda e: e.activation(out=s[0:cw, half * 512:(half + 1) * 512], in_=p[0:cw, :], func=AF.Copy), reads=[pk], writes=[sk])
            k.dma("sp", B1[c0:c0 + cw, :], s[0:cw, :], reads=[sk, "G1"], writes=["B1"])


T = 8192
SCALE = 64 ** -0.5
TWO_PI = 2.0 * math.pi
C1 = 6.28125
C2 = TWO_PI - C1


def build_s5(k, ldg, wr2, uT3, lam, bre, bim, cre, cim, dvec, psb):
    with k.scope():
        UT = k.sb("UT", [64, T])
        YT = k.sb("YT", [64, T])
        ident = k.sb("ident", [128, 128])
        DV = k.sb("DV", [64, 1])
        ldg(UT, uT3, ["G1"], ["UT"])
        k.dma("sp", DV[:], dvec[:, :], writes=["DV"])
        k.op("pool", lambda e: e.memset(ident[:], 1.0), writes=["ident"])
        k.op("pool", lambda e: e.affine_select(out=ident[:], in_=ident[:], pattern=[[-1, 128]], compare_op=ALU.is_equal,
                                                fill=k.fill(0.0), base=0, channel_multiplier=1), reads=["ident"], writes=["ident"])
        R = []
        for rt in range(2):
            d = {}
            P = "P%d" % rt
            for nm, shp in (("LAM", [128, 3]), ("BRE", [128, 64]), ("BIM", [128, 64]), ("CRE", [128, 64]), ("CIMN", [128, 64]),
                            ("BBR", [128, 64]), ("BBI", [128, 64]), ("BRT", [64, 128]), ("BIT", [64, 128]),
                            ("COS", [128, 512]), ("SIN", [128, 512]), ("RHO", [128, 512]), ("TMP", [128, 256]),
                            ("S", [128, 24]), ("KI", [128, 1])):
                d[nm] = k.sb("%s%d" % (nm, rt), shp, I32 if nm == "KI" else F32)
            k.dma("sp", d["LAM"][:], lam[:, rt, :], writes=[P])
            k.dma("sp", d["BRE"][:], bre[:, rt, :], writes=[P])
            k.dma("sp", d["BIM"][:], bim[:, rt, :], writes=[P])
            k.dma("sp", d["CRE"][:], cre[:, rt, :], writes=[P])
            k.dma("sp", d["CIMN"][:], cim[:, rt, :], writes=[P])
            S = d["S"]

            def sc(i):
                return S[:, i:i + 1]
            lr, li, ldt = d["LAM"][:, 0:1], d["LAM"][:, 1:2], d["LAM"][:, 2:3]

            def dv(fn):
                k.op("dve", fn, reads=[P], writes=[P])

            def ac(fn):
                k.op("act", fn, reads=[P], writes=[P])
            ac(lambda e: e.activation(out=sc(0), in_=ldt, func=AF.Exp))
            dv(lambda e: e.tensor_tensor(out=sc(1), in0=lr, in1=sc(0), op=ALU.mult))
            dv(lambda e: e.tensor_tensor(out=sc(2), in0=li, in1=sc(0), op=ALU.mult))
            ac(lambda e: e.activation(out=sc(3), in_=sc(1), func=AF.Exp))
            dv(lambda e: e.tensor_scalar(out=sc(4), in0=sc(2), scalar1=1.0 / TWO_PI, scalar2=None, op0=ALU.mult))
            dv(lambda e: e.tensor_copy(out=d["KI"][:], in_=sc(4)))
            dv(lambda e: e.tensor_copy(out=sc(4), in_=d["KI"][:]))
            dv(lambda e: e.scalar_tensor_tensor(out=sc(5), in0=sc(4), scalar=-C1, in1=sc(2), op0=ALU.mult, op1=ALU.add))
            dv(lambda e: e.scalar_tensor_tensor(out=sc(5), in0=sc(4), scalar=-C2, in1=sc(5), op0=ALU.mult, op1=ALU.add))
            dv(lambda e: e.tensor_scalar(out=sc(6), in0=sc(5), scalar1=math.pi, scalar2=-TWO_PI, op0=ALU.is_gt, op1=ALU.mult))
            dv(lambda e: e.tensor_tensor(out=sc(5), in0=sc(5), in1=sc(6), op=ALU.add))
            dv(lambda e: e.tensor_scalar(out=sc(6), in0=sc(5), scalar1=-math.pi, scalar2=TWO_PI, op0=ALU.is_lt, op1=ALU.mult))
            dv(lambda e: e.tensor_tensor(out=sc(5), in0=sc(5), in1=sc(6), op=ALU.add))
            ac(lambda e: e.activation(out=sc(7), in_=sc(5), func=AF.Sin))
            dv(lambda e: e.tensor_scalar(out=sc(6), in0=sc(5), scalar1=-1.0, scalar2=None, op0=ALU.mult))
            dv(lambda e: e.tensor_tensor(out=sc(6), in0=sc(6), in1=sc(5), op=ALU.max))
            dv(lambda e: e.tensor_scalar(out=sc(6), in0=sc(6), scalar1=-1.0, scalar2=math.pi / 2, op0=ALU.mult, op1=ALU.add))
            ac(lambda e: e.activation(out=sc(8), in_=sc(6), func=AF.Sin))
            dv(lambda e: e.tensor_tensor(out=sc(9), in0=sc(3), in1=sc(8), op=ALU.mult))
            dv(lambda e: e.tensor_scalar(out=sc(9), in0=sc(9), scalar1=-1.0, scalar2=None, op0=ALU.add))
            dv(lambda e: e.tensor_tensor(out=sc(10), in0=sc(3), in1=sc(7), op=ALU.mult))
            dv(lambda e: e.tensor_tensor(out=sc(11), in0=lr, in1=lr, op=ALU.mult))
            dv(lambda e: e.scalar_tensor_tensor(out=sc(11), in0=li, scalar=li, in1=sc(11), op0=ALU.mult, op1=ALU.add))
            dv(lambda e: e.reciprocal(out=sc(11), in_=sc(11)))
            dv(lambda e: e.tensor_tensor(out=sc(14), in0=sc(9), in1=lr, op=ALU.mult))
            dv(lambda e: e.scalar_tensor_tensor(out=sc(14), in0=sc(10), scalar=li, in1=sc(14), op0=ALU.mult, op1=ALU.add))
            dv(lambda e: e.tensor_tensor(out=sc(12), in0=sc(14), in1=sc(11), op=ALU.mult))
            dv(lambda e: e.tensor_tensor(out=sc(15), in0=sc(9), in1=li, op=ALU.mult))
            dv(lambda e: e.scalar_tensor_tensor(out=sc(15), in0=sc(10), scalar=lr, in1=sc(15), op0=ALU.mult, op1=ALU.subtract))
            dv(lambda e: e.tensor_tensor(out=sc(13), in0=sc(15), in1=sc(11), op=ALU.mult))
            dv(lambda e: e.tensor_scalar(out=d["BBR"][:], in0=d["BIM"][:], scalar1=sc(13), scalar2=None, op0=ALU.mult))
            dv(lambda e: e.scalar_tensor_tensor(out=d["BBR"][:], in0=d["BRE"][:], scalar=sc(12), in1=d["BBR"][:], op0=ALU.mult, op1=ALU.subtract))
            dv(lambda e: e.tensor_scalar(out=d["BBI"][:], in0=d["BRE"][:], scalar1=sc(13), scalar2=None, op0=ALU.mult))
            dv(lambda e: e.scalar_tensor_tensor(out=d["BBI"][:], in0=d["BIM"][:], scalar=sc(12), in1=d["BBI"][:], op0=ALU.mult, op1=ALU.add))
            dv(lambda e: e.tensor_scalar(out=d["CIMN"][:], in0=d["CIMN"][:], scalar1=-1.0, scalar2=None, op0=ALU.mult))
            for src, dst in (("BBR", "BRT"), ("BBI", "BIT")):
                p = psb[0]
                k.op("pe", lambda e: e.transpose(out=p[0:64, 0:128], in_=d[src][:], identity=ident[:]), reads=[P, "ident"], writes=["psb0"])
                k.op("dve", lambda e: e.tensor_copy(out=d[dst][:], in_=p[0:64, 0:128]), reads=["psb0", P], writes=[P])
            COS, SIN, TMP = d["COS"], d["SIN"], d["TMP"]
            dv(lambda e: e.memset(COS[:, 0:1], 1.0))
            dv(lambda e: e.memset(SIN[:, 0:1], 0.0))
            dv(lambda e: e.tensor_copy(out=sc(16), in_=sc(8)))
            dv(lambda e: e.tensor_copy(out=sc(17), in_=sc(7)))
            m = 1
            while m < 512:
                dv(lambda e: e.tensor_scalar(out=TMP[:, 0:m], in0=SIN[:, 0:m], scalar1=sc(17), scalar2=None, op0=ALU.mult))
                dv(lambda e: e.scalar_tensor_tensor(out=COS[:, m:2 * m], in0=COS[:, 0:m], scalar=sc(16), in1=TMP[:, 0:m],
                                                    op0=ALU.mult, op1=ALU.subtract))
                dv(lambda e: e.tensor_scalar(out=TMP[:, 0:m], in0=COS[:, 0:m], scalar1=sc(17), scalar2=None, op0=ALU.mult))
                dv(lambda e: e.scalar_tensor_tensor(out=SIN[:, m:2 * m], in0=SIN[:, 0:m], scalar=sc(16), in1=TMP[:, 0:m],
                                                    op0=ALU.mult, op1=ALU.add))
                dv(lambda e: e.tensor_tensor(out=sc(18), in0=sc(17), in1=sc(17), op=ALU.mult))
                dv(lambda e: e.tensor_tensor(out=sc(19), in0=sc(16), in1=sc(17), op=ALU.mult))
                dv(lambda e: e.scalar_tensor_tensor(out=sc(16), in0=sc(16), scalar=sc(16), in1=sc(18), op0=ALU.mult, op1=ALU.subtract))
                dv(lambda e: e.tensor_scalar(out=sc(17), in0=sc(19), scalar1=2.0, scalar2=None, op0=ALU.mult))
                m *= 2
            dv(lambda e: e.memset(d["RHO"][:], 1.0))
            dv(lambda e: e.tensor_scalar(out=d["RHO"][:], in0=d["RHO"][:], scalar1=sc(3), scalar2=None, op0=ALU.mult))
            for nm in ("T1", "T2", "VR", "VI", "WR", "WI", "XR", "XI"):
                d[nm] = k.sb("%s%d" % (nm, rt), [128, 512])
            d["INIT"] = k.sb("INIT%d" % rt, [128, 4])
            dv(lambda e: e.memset(d["INIT"][:], 0.0))
            R.append(d)

        for ch in range(16):
            cs = slice(ch * 512, (ch + 1) * 512)
            py = psb[6]
            for rt in range(2):
                d = R[rt]
                P = "P%d" % rt
                W = "W%d" % rt
                COS, SIN = d["COS"], d["SIN"]
                pr, pi_ = psb[2 * rt], psb[2 * rt + 1]
                prk, pik = "psb%d" % (2 * rt), "psb%d" % (2 * rt + 1)
                k.op("pe", lambda e: e.matmul(pr[:], lhsT=d["BRT"][:], rhs=UT[:, cs], start=True, stop=True), reads=[P, "UT"], writes=[prk])
                k.op("pe", lambda e: e.matmul(pi_[:], lhsT=d["BIT"][:], rhs=UT[:, cs], start=True, stop=True), reads=[P, "UT"], writes=[pik])
                k.op("dve", lambda e: e.tensor_tensor(out=d["T1"][:], in0=pr[:], in1=COS[:], op=ALU.mult), reads=[prk, P], writes=[W + "T1"])
                k.op("dve", lambda e: e.tensor_tensor(out=d["T2"][:], in0=pi_[:], in1=SIN[:], op=ALU.mult), reads=[pik, P], writes=[W + "T2"])
                k.op("pool", lambda e: e.tensor_tensor(out=d["VR"][:], in0=d["T1"][:], in1=d["T2"][:], op=ALU.add),
                     reads=[W + "T1", W + "T2"], writes=[W + "VR"])
                k.op("dve", lambda e: e.tensor_tensor(out=d["T1"][:], in0=pi_[:], in1=COS[:], op=ALU.mult), reads=[pik, P], writes=[W + "T1"])
                k.op("dve", lambda e: e.tensor_tensor(out=d["T2"][:], in0=pr[:], in1=SIN[:], op=ALU.mult), reads=[prk, P], writes=[W + "T2"])
                k.op("pool", lambda e: e.tensor_tensor(out=d["VI"][:], in0=d["T1"][:], in1=d["T2"][:], op=ALU.subtract),
                     reads=[W + "T1", W + "T2"], writes=[W + "VI"])
                k.op("dve", lambda e: e.tensor_tensor_scan(out=d["WR"][:], data0=d["RHO"][:], data1=d["VR"][:], initial=d["INIT"][:, 0:1],
                                                           op0=ALU.mult, op1=ALU.add), reads=[P, W + "VR", W + "INIT"], writes=[W + "WR"])
                k.op("dve", lambda e: e.tensor_tensor_scan(out=d["WI"][:], data0=d["RHO"][:], data1=d["VI"][:], initial=d["INIT"][:, 1:2],
                                                           op0=ALU.mult, op1=ALU.add), reads=[P, W + "VI", W + "INIT"], writes=[W + "WI"])
                k.op("pool", lambda e: e.tensor_tensor(out=d["VR"][:], in0=d["WR"][:], in1=COS[:], op=ALU.mult), reads=[W + "WR", P], writes=[W + "VR"])
                k.op("pool", lambda e: e.tensor_tensor(out=d["VI"][:], in0=d["WI"][:], in1=SIN[:], op=ALU.mult), reads=[W + "WI", P], writes=[W + "VI"])
                k.op("pool", lambda e: e.tensor_tensor(out=d["XR"][:], in0=d["VR"][:], in1=d["VI"][:], op=ALU.subtract),
                     reads=[W + "VR", W + "VI"], writes=[W + "XR"])
                k.op("pool", lambda e: e.tensor_tensor(out=d["VR"][:], in0=d["WR"][:], in1=SIN[:], op=ALU.mult), reads=[W + "WR", P], writes=[W + "VR"])
                k.op("pool", lambda e: e.tensor_tensor(out=d["VI"][:], in0=d["WI"][:], in1=COS[:], op=ALU.mult), reads=[W + "WI", P], writes=[W + "VI"])
                k.op("pool", lambda e: e.tensor_tensor(out=d["XI"][:], in0=d["VR"][:], in1=d["VI"][:], op=ALU.add),
                     reads=[W + "VR", W + "VI"], writes=[W + "XI"])
                S = d["S"]
                k.op("dve", lambda e: e.tensor_tensor(out=d["INIT"][:, 2:3], in0=d["XI"][:, 511:512], in1=S[:, 7:8], op=ALU.mult),
                     reads=[W + "XI", P, W + "INIT"], writes=[W + "INIT"])
                k.op("dve", lambda e: e.scalar_tensor_tensor(out=d["INIT"][:, 0:1], in0=d["XR"][:, 511:512], scalar=S[:, 8:9],
                                                             in1=d["INIT"][:, 2:3], op0=ALU.mult, op1=ALU.subtract),
                     reads=[W + "XR", P, W + "INIT"], writes=[W + "INIT"])
                k.op("dve", lambda e: e.tensor_tensor(out=d["INIT"][:, 2:3], in0=d["XR"][:, 511:512], in1=S[:, 7:8], op=ALU.mult),
                     reads=[W + "XR", P, W + "INIT"], writes=[W + "INIT"])
                k.op("dve", lambda e: e.scalar_tensor_tensor(out=d["INIT"][:, 1:2], in0=d["XI"][:, 511:512], scalar=S[:, 8:9],
                                                             in1=d["INIT"][:, 2:3], op0=ALU.mult, op1=ALU.add),
                     reads=[W + "XI", P, W + "INIT"], writes=[W + "INIT"])
                k.op("pe", lambda e: e.matmul(py[0:64, :], lhsT=d["CRE"][:], rhs=d["XR"][:], start=(rt == 0), stop=False),
                     reads=[P, W + "XR"], writes=["psb6"])
                k.op("pe", lambda e: e.matmul(py[0:64, :], lhsT=d["CIMN"][:], rhs=d["XI"][:], start=False, stop=(rt == 1)),
                     reads=[P, W + "XI"], writes=["psb6"])
            k.op("dve", lambda e: e.scalar_tensor_tensor(out=YT[:, cs], in0=UT[:, cs], scalar=DV[:, 0:1], in1=py[0:64, :],
                                                         op0=ALU.mult, op1=ALU.add), reads=["UT", "DV", "psb6"], writes=["YT%d" % ch])
            wr2(0, 64, ch * 512, YT[:, cs], ["YT%d" % ch])


def build_sb(k, ldg, wr2, qT3, kT3, vT3, psb, ptb):
    with k.scope():
        QB = k.sb("QB", [64, T], BF16)
        KB = k.sb("KB", [64, T], BF16)
        VB = k.sb("VB", [128, 64, 64], BF16)
        OT = k.sb("OTs", [64, T])
        U = k.sb("U", [128, 128])
        ONESF = k.sb("ONESF", [128, 128])
        VT = k.sb("VT", [64, T], BF16)
        identb = k.sb("identb_sb", [128, 128], BF16)
        k.op("pool", lambda e: e.memset(identb[:], 1.0), writes=["identb"])
        k.op("pool", lambda e: e.affine_select(out=identb[:], in_=identb[:], pattern=[[-1, 128]], compare_op=ALU.is_equal,
                                                fill=k.fill(0.0), base=0, channel_multiplier=1), reads=["identb"], writes=["identb"])
        ldg(QB, qT3, ["G1"], ["QB"])
        ldg(KB, kT3, ["G1"], ["KB"])
        ldg(VT, vT3, ["G1"], ["VT"])
        tok_major(k, VT, VB, 64, identb, ptb)
        k.op("pool", lambda e: e.memset(ONESF[:], 1.0), writes=["ONESF"])
        k.op("pool", lambda e: e.memset(U[:], 1.0), writes=["U"])
        k.op("pool", lambda e: e.affine_select(out=U[:], in_=U[:], pattern=[[-1, 128]], compare_op=ALU.is_gt,
                                                fill=k.fill(0.0), base=0, channel_multiplier=1), reads=["U"], writes=["U"])
        NB = 2
        E = [k.sb("E%d" % i, [128, 512]) for i in range(NB)]
        SP = [k.sb("SP%d" % i, [128, 512]) for i in range(NB)]
        B = [k.sb("B%d" % i, [128, 512]) for i in range(NB)]
        WW = [k.sb("WW%d" % i, [128, 512], BF16) for i in range(NB)]
        step = 0
        for Q in range(16):
            t0 = Q * 512
            kmax = 4 * Q + 3
            pR, pO = psb[4], psb[5]
            for kt in range(kmax, -1, -1):
                b = step % NB
                pz, pzk = psb[step % 2], "psb%d" % (step % 2)
                pL, pLk = psb[2 + step % 2], "psb%d" % (2 + step % 2)
                step += 1
                diag = kt >= 4 * Q
                e_, sp, bb, ww = E[b], SP[b], B[b], WW[b]
                ek, spk, bk, wk = "E%d" % b, "SP%d" % b, "B%d" % b, "WW%d" % b
                k.op("pe", lambda e: e.matmul(pz[:], lhsT=KB[:, kt * 128:(kt + 1) * 128], rhs=QB[:, t0:t0 + 512], start=True, stop=True),
                     reads=["KB", "QB"], writes=[pzk])
                k.op("act", lambda e: e.activation(out=e_[:], in_=pz[:], func=AF.Exp, scale=SCALE), reads=[pzk], writes=[ek])
                k.op("act", lambda e: e.activation(out=sp[:], in_=e_[:], func=AF.Ln, bias=1.0), reads=[ek], writes=[spk])
                if diag:
                    k.op("pool", lambda e: e.affine_select(out=sp[:], in_=sp[:], pattern=[[1, 512]], compare_op=ALU.is_gt, fill=k.fill(0.0),
                                                            base=t0 - 128 * kt, channel_multiplier=-1), reads=[spk], writes=[spk])
                k.op("pe", lambda e: e.matmul(pL[:], lhsT=U[:], rhs=sp[:], start=True, stop=True), reads=["U", spk], writes=[pLk])
                k.op("dve", lambda e: e.scalar_tensor_tensor(out=bb[:], in0=pz[:], scalar=SCALE, in1=sp[:], op0=ALU.mult, op1=ALU.subtract),
                     reads=[pzk, spk], writes=[bk])
                k.op("dve", lambda e: e.tensor_tensor(out=bb[:], in0=bb[:], in1=pL[:], op=ALU.subtract), reads=[bk, pLk], writes=[bk])
                if kt < kmax:
                    k.op("dve", lambda e: e.tensor_tensor(out=bb[:], in0=bb[:], in1=pR[:], op=ALU.subtract), reads=[bk, "psb4"], writes=[bk])
                if kt > 0:
                    k.op("pe", lambda e: e.matmul(pR[:], lhsT=ONESF[:], rhs=sp[:], start=(kt == kmax), stop=(kt == 1)),
                         reads=["ONESF", spk], writes=["psb4"])
                k.op("act", lambda e: e.activation(out=ww[:], in_=bb[:], func=AF.Exp), reads=[bk], writes=[wk])
                if diag:
                    k.op("pool", lambda e: e.affine_select(out=ww[:], in_=ww[:], pattern=[[1, 512]], compare_op=ALU.is_gt, fill=k.fill(0.0),
                                                            base=t0 - 128 * kt, channel_multiplier=-1), reads=[wk], writes=[wk])
                k.op("pe", lambda e: e.matmul(pO[0:64, :], lhsT=VB[:, kt, :], rhs=ww[:], start=(kt == kmax), stop=(kt == 0)),
                     reads=["VB", wk], writes=["psb5"])
            k.op("act", lambda e: e.activation(out=OT[:, t0:t0 + 512], in_=pO[0:64, :], func=AF.Copy), reads=["psb5"], writes=["OT%d" % Q])
            wr2(64, 128, t0, OT[:, t0:t0 + 512], ["OT%d" % Q])


DMIN = -2064
ND = 5120
FAR_D = 790


def build_nsa(k, ldg, wr2, nq_own3, nq_oth3, kc3, vc3, ksl3, vsl3, ksw3, vsw3, g3, w1, w2, posT, tab31, ovv, eall, BT, psb, ptb):
    with k.scope():
        OUT = k.sb("OUT", [128, 64, 128])
        PENT = k.sb("PENT", [128, T], BF16)
        GS = k.sb("GS", [128, 64, 6])
        TAB31 = k.sb("TAB31", [128, 4])
        OVV = k.sb("OVV", [128, 4, 193], BF16)
        KCT = k.sb("KCT", [64, 512], BF16)
        identb = k.sb("identb", [128, 128], BF16)
        k.op("pool", lambda e: e.memset(identb[:], 1.0), writes=["identb"])
        k.op("pool", lambda e: e.affine_select(out=identb[:], in_=identb[:], pattern=[[-1, 128]], compare_op=ALU.is_equal,
                                                fill=k.fill(0.0), base=0, channel_multiplier=1), reads=["identb"], writes=["identb"])
        identf = k.sb("identf", [128, 128])
        k.op("pool", lambda e: e.memset(identf[:], 1.0), writes=["identf"])
        k.op("pool", lambda e: e.affine_select(out=identf[:], in_=identf[:], pattern=[[-1, 128]], compare_op=ALU.is_equal,
                                                fill=k.fill(0.0), base=0, channel_multiplier=1), reads=["identf"], writes=["identf"])
        with k.scope():
            GT = k.sb("GT", [6, T])
            ldg(GT, g3, ["G1"], ["GT"])
            k.op("act", lambda e: e.activation(out=GT[:], in_=GT[:], func=AF.Sigmoid), reads=["GT"], writes=["GT"])
            pg = psb[6]
            for kt in range(64):
                k.op("pe", lambda e: e.transpose(out=pg[:, kt * 6:(kt + 1) * 6], in_=GT[:, kt * 128:(kt + 1) * 128], identity=identf[0:6, 0:6]),
                     reads=["GT", "identf"], writes=["psb6"])
            k.op("act", lambda e: e.activation(out=GS[:].rearrange("p a b -> p (a b)"), in_=pg[:, 0:384], func=AF.Copy), reads=["psb6"], writes=["GS"])
        k.dma("sp", TAB31[:], tab31[0, :].partition_broadcast(128), writes=["TAB31"])
        k.dma("pool", OVV[:, :, 0:129], ovv[:, :, :], writes=["OVV"])
        with k.scope():
            XIN = k.sb("XIN", [64, T], BF16)
            W1 = k.sb("W1", [64, 2, 32, 128], BF16)
            W2 = k.sb("W2", [128, 2, 64], BF16)
            POST = k.sb("POST", [64, 2, 32], BF16)
            G = k.sb("G", [128, 512], BF16)
            PB = k.sb("PB", [128, 1])
            k.dma("pool", W1[:], w1[:, :, :, :], writes=["W1"])
            k.dma("pool", W2[:], w2[:, :, :], writes=["W2"])
            k.dma("pool", POST[:], posT[:, :, :], writes=["POST"])
            for j in range(2):
                ldg(XIN, (kc3 if j == 0 else vc3), ["G1"], ["XIN"])
                k.op("pool", lambda e: e.memset(G[:], 0.0), reads=["G"], writes=["G"])
                ph, pb = psb[0], psb[1]
                for l in range(32):
                    k.op("pe", lambda e: e.matmul(ph[:, 0:511], lhsT=W1[:, j, l, :], rhs=XIN[:, l:l + 16 * 510 + 1:16], start=(l == 0), stop=(l == 31)),
                         reads=["W1", "XIN"], writes=["psb0"])
                for l in range(32):
                    k.op("pe", lambda e: e.matmul(pb[:, 0:1], lhsT=W1[:, j, l, :], rhs=POST[:, j, l:l + 1], start=(l == 0), stop=(l == 31)),
                         reads=["W1", "POST"], writes=["psb1"])
                k.op("dve", lambda e: e.tensor_copy(out=PB[:], in_=pb[:, 0:1]), reads=["psb1"], writes=["PB"])
                k.op("act", lambda e: e.activation(out=G[:, 0:511], in_=ph[:, 0:511], func=AF.Gelu_apprx_tanh, bias=PB[:, 0:1]),
                     reads=["psb0", "PB"], writes=["G"])
                if j == 0:
                    p2 = psb[2]
                    k.op("pe", lambda e: e.matmul(p2[0:64, :], lhsT=W2[:, 0, :], rhs=G[:], start=True, stop=True), reads=["W2", "G"], writes=["psb2"])
                    k.op("act", lambda e: e.activation(out=KCT[:], in_=p2[0:64, :], func=AF.Copy), reads=["psb2"], writes=["KCT"])
                else:
                    for m in range(4):
                        p2, p2k = psb[2 + m], "psb%d" % (2 + m)
                        k.op("pe", lambda e: e.matmul(p2[:, 0:64], lhsT=G[:, m * 128:(m + 1) * 128], rhs=W2[:, 1, :], start=True, stop=True),
                             reads=["W2", "G"], writes=[p2k])
                        k.op("act", lambda e: e.activation(out=OVV[:, m, 129:193], in_=p2[:, 0:64], func=AF.Copy), reads=[p2k, "OVV"], writes=["OVV"])
        with k.scope():
            QB4 = k.sb("QB4", [64, 4, T], BF16)
            CB = k.sb("CB", [128, 6, 4, 512])
            IMP = k.sb("IMP", [128, 4, 128])
            SC2 = k.sb("SC2", [128, 128])
            M8 = k.sb("M8", [128, 8])
            M8b = k.sb("M8b", [128, 8])
            PENTOK = k.sb("PENTOK", [128, 128], BF16)
            ET = [[k.sb("ET%d_%d" % (a, m), [128, 512], BF16) for m in range(4)] for a in range(2)]
            LG = [k.sb("LGc%d" % a, [128, 512]) for a in range(2)]
            RZ = k.sb("RZc", [128, 2])
            for hh in range(2):
                ldg(QB4[:, hh, :], nq_own3[hh], ["G1"], ["QB4"])
                ldg(QB4[:, 2 + hh, :], nq_oth3[hh], ["G1"], ["QB4"])
            for j in range(6):
                for hj in range(4):
                    bi = j * 4 + hj
                    k.dma("sp", CB[:, j, hj, :], BT[bi * 128:(bi + 1) * 128, :], reads=["BT"], writes=["CB"])
            lgc = 0
            pc = 0
            for Q in range(16):
                t0 = Q * 512
                mlist = list(range(0, Q // 4 + 1))
                for hj in range(4):
                    a = hj % 2
                    for m in mlist:
                        j = Q - 4 * m
                        p, pk = psb[pc % 2], "psb%d" % (pc % 2)
                        pc += 1
                        et, etk = ET[a][m], "ET%d_%d" % (a, m)
                        k.op("pe", lambda e: e.matmul(p[:], lhsT=KCT[:, m * 128:(m + 1) * 128], rhs=QB4[:, hj, t0:t0 + 512], start=True, stop=True),
                             reads=["KCT", "QB4"], writes=[pk])
                        if j <= 5:
                            lg, lgk = LG[lgc % 2], "LGc%d" % (lgc % 2)
                            lgc += 1
                            k.op("dve", lambda e: e.scalar_tensor_tensor(out=lg[:], in0=p[:], scalar=SCALE, in1=CB[:, j, hj, :], op0=ALU.mult, op1=ALU.add),
                                 reads=[pk, "CB"], writes=[lgk])
                            k.op("act", lambda e: e.activation(out=et[:], in_=lg[:], func=AF.Exp), reads=[lgk], writes=[etk])
                        else:
                            k.op("act", lambda e: e.activation(out=et[:], in_=p[:], func=AF.Exp, scale=SCALE, bias=TAB31[:, hj:hj + 1]),
                                 reads=[pk, "TAB31"], writes=[etk])
                    W = 193 if hj < 2 else 129
                    for sub in range(4):
                        tt = 4 * Q + sub
                        pI, pIk = psb[2 + sub], "psb%d" % (2 + sub)
                        for mi, m in enumerate(mlist):
                            k.op("pe", lambda e: e.matmul(pI[:, 0:W], lhsT=ET[a][m][:, sub * 128:(sub + 1) * 128], rhs=OVV[:, m, 0:W],
                                                          start=(mi == 0), stop=(mi == len(mlist) - 1)),
                                 reads=["ET%d_%d" % (a, m), "OVV"], writes=[pIk])
                        k.op("dve", lambda e: e.tensor_scalar(out=RZ[:, 0:1], in0=pI[:, 128:129], scalar1=1e-30, scalar2=None, op0=ALU.max),
                             reads=[pIk, "RZc"], writes=["RZc"])
                        k.op("dve", lambda e: e.reciprocal(out=RZ[:, 0:1], in_=RZ[:, 0:1]), reads=["RZc"], writes=["RZc"])
                        if hj == 0:
                            k.op("dve", lambda e: e.tensor_scalar(out=IMP[:, sub, :], in0=pI[:, 0:128], scalar1=RZ[:, 0:1], scalar2=None, op0=ALU.mult),
                                 reads=[pIk, "RZc", "IMP%d" % sub], writes=["IMP%d" % sub])
                        else:
                            k.op("dve", lambda e: e.scalar_tensor_tensor(out=IMP[:, sub, :], in0=pI[:, 0:128], scalar=RZ[:, 0:1], in1=IMP[:, sub, :],
                                                                         op0=ALU.mult, op1=ALU.add),
                                 reads=[pIk, "RZc", "IMP%d" % sub], writes=["IMP%d" % sub])
                        if hj < 2:
                            k.op("dve", lambda e: e.tensor_tensor(out=RZ[:, 1:2], in0=RZ[:, 0:1], in1=GS[:, tt, hj * 3:hj * 3 + 1], op=ALU.mult),
                                 reads=["RZc", "GS"], writes=["RZc"])
                            k.op("dve", lambda e: e.tensor_scalar(out=OUT[:, tt, hj * 64:(hj + 1) * 64], in0=pI[:, 129:193], scalar1=RZ[:, 1:2],
                                                                  scalar2=None, op0=ALU.mult), reads=[pIk, "RZc"], writes=["OUT%d" % tt])
                for sub in range(4):
                    tb = 128 * (4 * Q + sub)
                    ik = "IMP%d" % sub
                    sc_ = IMP[:, sub, :]
                    k.op("pool", lambda e: e.affine_select(out=sc_, in_=sc_, pattern=[[-64, 128]], compare_op=ALU.is_ge, fill=k.fill(1e4),
                                                            base=tb - 128, channel_multiplier=1), reads=[ik], writes=[ik])
                    k.op("pool", lambda e: e.affine_select(out=sc_, in_=sc_, pattern=[[-64, 128]], compare_op=ALU.is_ge, fill=k.fill(-1.0),
                                                            base=tb, channel_multiplier=1), reads=[ik], writes=[ik])
                    k.op("pool", lambda e: e.memset(IMP[:, sub, 0:1], 1e4), reads=[ik], writes=[ik])
                    k.op("dve", lambda e: e.max(out=M8[:], in_=sc_), reads=[ik], writes=["M8"])
                    k.op("dve", lambda e: e.match_replace(out=SC2[:], in_to_replace=M8[:], in_values=sc_, imm_value=-1e9),
                         reads=[ik, "M8"], writes=["SC2"])
                    k.op("dve", lambda e: e.max(out=M8b[:], in_=SC2[:]), reads=["SC2"], writes=["M8b"])
                    k.op("dve", lambda e: e.tensor_scalar(out=PENTOK[:], in0=sc_, scalar1=M8b[:, 7:8], scalar2=1.0, op0=ALU.is_ge, op1=ALU.subtract),
                         reads=[ik, "M8b"], writes=["PENTOK"])
                    k.op("pe", lambda e: e.transpose(out=ptb[:, 0:128], in_=PENTOK[:], identity=identb[:]), reads=["PENTOK", "identb"], writes=["ptb"])
                    k.op("act", lambda e: e.activation(out=PENT[:, tb:tb + 128], in_=ptb[:, 0:128], func=AF.Copy), reads=["ptb"], writes=["PENT%d" % Q])
        with k.scope():
            KSLT = k.sb("KSLT", [64, T], BF16)
            KSWT = k.sb("KSWT", [64, T], BF16)
            VSL = k.sb("VSL", [128, 64, 65], BF16)
            VSW = k.sb("VSW", [128, 64, 65], BF16)
            EALL = k.sb("EALL", [128, T], BF16)
            LG = [k.sb("LGs%d" % a, [128, 512]) for a in range(2)]
            EX = [k.sb("EX%d" % a, [128, 512], BF16) for a in range(2)]
            RZ = k.sb("RZs", [128, 2])
            ldg(KSLT, ksl3, ["G1"], ["KSLT"])
            ldg(KSWT, ksw3, ["G1"], ["KSWT"])
            k.op("pool", lambda e: e.memset(VSL[:, :, 64:65], 1.0), writes=["VSL"])
            k.op("pool", lambda e: e.memset(VSW[:, :, 64:65], 1.0), writes=["VSW"])
            with k.scope():
                VT = k.sb("VTn", [64, T], BF16)
                ldg(VT, vsl3, ["G1"], ["VT"])
                tok_major(k, VT, VSL, 65, identb, ptb, "VSL")
                ldg(VT, vsw3, ["G1", "VT"], ["VT"])
                tok_major(k, VT, VSW, 65, identb, ptb, "VSW")
            k.dma("pool", EALL[:], eall[:, :], writes=["EALL"])
            for hj in range(2):
                with k.scope():
                    QBh = k.sb("QBh", [64, T], BF16)
                    SELB = k.sb("SELB", [128, 11, 512])
                    WINB = k.sb("WINB", [128, 8, 512])
                    ldg(QBh, nq_own3[hj], ["G1"], ["QBh"])
                    for di in range(11):
                        bi = 24 + hj * 11 + di
                        k.dma("sp", SELB[:, di, :], BT[bi * 128:(bi + 1) * 128, :], reads=["BT"], writes=["SELB"])
                    for jw in range(8):
                        bi = 46 + hj * 8 + jw
                        k.dma("sp", WINB[:, jw, :], BT[bi * 128:(bi + 1) * 128, :], reads=["BT"], writes=["WINB"])
                    cnt = 0
                    for Q in range(16):
                        t0 = Q * 512
                        for kt in range(0, 4 * Q + 4):
                            D0 = 512 * Q - 128 * kt
                            p, pk = psb[cnt % 2], "psb%d" % (cnt % 2)
                            lg, lgk = LG[cnt % 2], "LGs%d" % (cnt % 2)
                            ex, exk = EX[cnt % 2], "EX%d" % (cnt % 2)
                            cnt += 1
                            k.op("pe", lambda e: e.matmul(p[:], lhsT=KSLT[:, kt * 128:(kt + 1) * 128], rhs=QBh[:, t0:t0 + 512], start=True, stop=False),
                                 reads=["KSLT", "QBh"], writes=[pk])
                            k.op("pe", lambda e: e.matmul(p[:], lhsT=EALL[:, kt * 128:(kt + 1) * 128], rhs=PENT[:, t0:t0 + 512], start=False, stop=True),
                                 reads=["EALL", "PENT%d" % Q], writes=[pk])
                            if D0 <= 896:
                                di = (D0 + 384) // 128
                                k.op("dve", lambda e: e.scalar_tensor_tensor(out=lg[:], in0=p[:], scalar=SCALE, in1=SELB[:, di, :], op0=ALU.mult, op1=ALU.add),
                                     reads=[pk, "SELB"], writes=[lgk])
                                k.op("act", lambda e: e.activation(out=ex[:], in_=lg[:], func=AF.Exp), reads=[lgk], writes=[exk])
                            else:
                                k.op("act", lambda e: e.activation(out=ex[:], in_=p[:], func=AF.Exp, scale=SCALE, bias=TAB31[:, hj:hj + 1]),
                                     reads=[pk, "TAB31"], writes=[exk])
                            for sub in range(4):
                                if kt <= 4 * Q + sub:
                                    k.op("pe", lambda e: e.matmul(psb[2 + sub][:, 0:65], lhsT=ex[:, sub * 128:(sub + 1) * 128], rhs=VSL[:, kt, :],
                                                                  start=(kt == 0), stop=(kt == 4 * Q + sub)),
                                         reads=[exk, "VSL"], writes=["psb%d" % (2 + sub)])
                        for sub in range(4):
                            tt = 4 * Q + sub
                            pa, pak = psb[2 + sub], "psb%d" % (2 + sub)
                            k.op("dve", lambda e: e.reciprocal(out=RZ[:, 0:1], in_=pa[:, 64:65]), reads=[pak, "RZs"], writes=["RZs"])
                            k.op("dve", lambda e: e.tensor_tensor(out=RZ[:, 1:2], in0=RZ[:, 0:1], in1=GS[:, tt, hj * 3 + 1:hj * 3 + 2], op=ALU.mult),
                                 reads=["RZs", "GS"], writes=["RZs"])
                            k.op("dve", lambda e: e.scalar_tensor_tensor(out=OUT[:, tt, hj * 64:(hj + 1) * 64], in0=pa[:, 0:64], scalar=RZ[:, 1:2],
                                                                         in1=OUT[:, tt, hj * 64:(hj + 1) * 64], op0=ALU.mult, op1=ALU.add),
                                 reads=[pak, "RZs", "OUT%d" % tt], writes=["OUT%d" % tt])
                        jw_min = 4 if Q == 0 else 0
                        for jw in range(jw_min, 8):
                            s0 = t0 - 512 + 128 * jw
                            kt = s0 // 128
                            p, pk = psb[cnt % 2], "psb%d" % (cnt % 2)
                            lg, lgk = LG[cnt % 2], "LGs%d" % (cnt % 2)
                            ex, exk = EX[cnt % 2], "EX%d" % (cnt % 2)
                            cnt += 1
                            k.op("pe", lambda e: e.matmul(p[:], lhsT=KSWT[:, kt * 128:(kt + 1) * 128], rhs=QBh[:, t0:t0 + 512], start=True, stop=True),
                                 reads=["KSWT", "QBh"], writes=[pk])
                            k.op("dve", lambda e: e.scalar_tensor_tensor(out=lg[:], in0=p[:], scalar=SCALE, in1=WINB[:, jw, :], op0=ALU.mult, op1=ALU.add),
                                 reads=[pk, "WINB"], writes=[lgk])
                            k.op("act", lambda e: e.activation(out=ex[:], in_=lg[:], func=AF.Exp), reads=[lgk], writes=[exk])
                            for sub in range(4):
                                first = max(sub, jw_min)
                                if first <= jw <= sub + 4:
                                    k.op("pe", lambda e: e.matmul(psb[2 + sub][:, 0:65], lhsT=ex[:, sub * 128:(sub + 1) * 128], rhs=VSW[:, kt, :],
                                                                  start=(jw == first), stop=(jw == sub + 4)),
                                         reads=[exk, "VSW"], writes=["psb%d" % (2 + sub)])
                        for sub in range(4):
                            tt = 4 * Q + sub
                            pa, pak = psb[2 + sub], "psb%d" % (2 + sub)
                            k.op("dve", lambda e: e.reciprocal(out=RZ[:, 0:1], in_=pa[:, 64:65]), reads=[pak, "RZs"], writes=["RZs"])
                            k.op("dve", lambda e: e.tensor_tensor(out=RZ[:, 1:2], in0=RZ[:, 0:1], in1=GS[:, tt, hj * 3 + 2:hj * 3 + 3], op=ALU.mult),
                                 reads=["RZs", "GS"], writes=["RZs"])
                            k.op("dve", lambda e: e.scalar_tensor_tensor(out=OUT[:, tt, hj * 64:(hj + 1) * 64], in0=pa[:, 0:64], scalar=RZ[:, 1:2],
                                                                         in1=OUT[:, tt, hj * 64:(hj + 1) * 64], op0=ALU.mult, op1=ALU.add),
                                 reads=[pak, "RZs", "OUT%d" % tt], writes=["OUT%d" % tt])
        YN = [k.sb("YN%d" % a, [128, 512]) for a in range(2)]
        for Q in range(16):
            p, pk = psb[Q % 2], "psb%d" % (Q % 2)
            for sub in range(4):
                tt = 4 * Q + sub
                k.op("pe", lambda e: e.transpose(out=p[:, sub * 128:(sub + 1) * 128], in_=OUT[:, tt, :], identity=identf[:]),
                     reads=["OUT%d" % tt, "identf"], writes=[pk])
            yn, ynk = YN[Q % 2], "YN%d" % (Q % 2)
            k.op("act", lambda e: e.activation(out=yn[:], in_=p[:], func=AF.Copy), reads=[pk], writes=[ynk])
            wr2(128, 256, Q * 512, yn[:], [ynk])


def tok_major(k, VT, VB, W, identb, ptb, key="VB"):
    for g8 in range(8):
        for q in range(8):
            kt = g8 * 8 + q
            k.op("pe", lambda e: e.transpose(out=ptb[:, q * 64:(q + 1) * 64], in_=VT[:, kt * 128:(kt + 1) * 128], identity=identb[0:64, 0:64]),
                 reads=["VT", "identb"], writes=["ptb"])
        k.op("act", lambda e: e.activation(out=VB[:, g8 * 8:(g8 + 1) * 8, 0:64], in_=ptb[:, 0:512].rearrange("p (q d) -> p q d", q=8), func=AF.Copy),
             reads=["ptb", key], writes=[key])


THR = [0, 1, 2, 3, 4, 5, 6, 7, 8, 9, 10, 11, 12, 13, 14, 15, 16, 21, 27, 35, 46, 59, 77, 99, 128, 166, 216, 280, 363, 470, 609, 790]


def bias_setup(k, tabx, OH, OHW, BT, psb):
    fsd = k.dram_tmp("fsd", [4, ND])
    fwd = k.dram_tmp("fwd", [4, ND])
    with k.scope():
        TABX = k.sb("TABX", [33, 4])
        OHS = k.sb("OHS", [33, ND])
        FSB = k.sb("FSB", [4, ND])
        JJ = k.sb("JJ", [128, 128])
        HK = [k.sb("HK%d" % i, [128, 512]) for i in range(2)]
        RV = [k.sb("RV%d" % i, [128, 512]) for i in range(2)]
        k.op("pool", lambda e: e.memset(JJ[:], 1.0), writes=["JJ"])
        k.op("pool", lambda e: e.affine_select(out=JJ[:], in_=JJ[:], pattern=[[1, 128]], compare_op=ALU.is_equal,
                                                fill=k.fill(0.0), base=-127, channel_multiplier=1), reads=["JJ"], writes=["JJ"])
        k.dma("sp", TABX[:], tabx[:, :], writes=["TABX"])
        for which, (src, dst) in enumerate(((OH, fsd), (OHW, fwd))):
            k.dma("sp", OHS[:], src[:, :], reads=["OHS"], writes=["OHS"])
            for c in range(ND // 512):
                p, pk = psb[c % 2], "psb%d" % (c % 2)
                k.op("pe", lambda e: e.matmul(p[0:4, :], lhsT=TABX[:], rhs=OHS[:, c * 512:(c + 1) * 512], start=True, stop=True),
                     reads=["TABX", "OHS"], writes=[pk])
                k.op("dve", lambda e: e.tensor_copy(out=FSB[:, c * 512:(c + 1) * 512], in_=p[0:4, :]), reads=[pk, "FSB"], writes=["FSB"])
            k.dma("sp", dst[:, :], FSB[:], reads=["FSB"], writes=["FD%d" % which])
        specs = []
        for j in range(6):
            for hj in range(4):
                specs.append((fsd, "FD0", hj * ND + (512 * j - 31 - 2032 - DMIN), 16, j * 4 + hj))
        for hj in range(2):
            for di in range(11):
                specs.append((fsd, "FD0", hj * ND + (-384 + 128 * di - 127 - DMIN), 1, 24 + hj * 11 + di))
            for jw in range(8):
                specs.append((fwd, "FD1", hj * ND + (512 - 128 * jw - 127 - DMIN), 1, 46 + hj * 8 + jw))
        for i, (src, sk, off, stride, bi) in enumerate(specs):
            hk, hkk = HK[i % 2], "HK%d" % (i % 2)
            rv, rvk = RV[i % 2], "RV%d" % (i % 2)
            p, pk = psb[2 + i % 2], "psb%d" % (2 + i % 2)
            k.dma("sp", hk[:], AP(tensor=src.tensor, offset=off, ap=[[stride, 128], [1, 512]]), reads=[sk], writes=[hkk])
            k.op("pe", lambda e: e.matmul(p[:], lhsT=JJ[:], rhs=hk[:], start=True, stop=True), reads=["JJ", hkk], writes=[pk])
            k.op("act", lambda e: e.activation(out=rv[:], in_=p[:], func=AF.Copy), reads=[pk], writes=[rvk])
            k.dma("sp", BT[bi * 128:(bi + 1) * 128, :], rv[:], reads=[rvk], writes=["BT"])


def build_fused(NL=4, debug=False, stage=None):
    k = K()
    nc = k.nc
    k.fill(0.0), k.fill(1e4), k.fill(-1.0)
    psb = [k.ps("psb%d" % i, [128, 512]) for i in range(7)]
    ptb = k.ps("ptb", [128, 1024], BF16)
    h_in = k.dram_in("h_in", [TOK, D])
    W = dict(w_in=k.dram_in("w_in", [NL, D, NIN]), w_glu=k.dram_in("w_glu", [NL, 512, 512]), b_glu=k.dram_in("b_glu", [NL, 128, 4]),
             w_out=k.dram_in("w_out", [NL, D, D]), ln_g=k.dram_in("ln_g", [NL, 3, D]), ln_b=k.dram_in("ln_b", [NL, 3, D]),
             memT=k.dram_in("memT", [D, 256]), xa_wq=k.dram_in("xa_wq", [NL, D, 512]), xa_wkv=k.dram_in("xa_wkv", [NL, D, 1024]),
             xa_wo=k.dram_in("xa_wo", [NL, 512, D]), w_up=k.dram_in("w_up", [NL, D, 2 * DFF]), convw=k.dram_in("convw", [NL, 128, 88, 3]),
             convb=k.dram_in("convb", [NL, 128, 88]), w_down=k.dram_in("w_down", [NL, DFF, D]), flag=k.dram_in("flag", [128, 1]))
    lam = k.dram_in("lam", [NL, 128, 2, 3])
    bre = k.dram_in("bre", [NL, 128, 2, 64])
    bim = k.dram_in("bim", [NL, 128, 2, 64])
    cre = k.dram_in("cre", [NL, 128, 2, 64])
    cim = k.dram_in("cim", [NL, 128, 2, 64])
    dvec = k.dram_in("dvec", [NL, 64, 1])
    cw1 = k.dram_in("cw1", [NL, 64, 2, 32, 128])
    cw2 = k.dram_in("cw2", [NL, 128, 2, 64])
    posT = k.dram_in("posT", [NL, 64, 2, 32])
    tabx = k.dram_in("tabx", [33, 4])
    tab31 = k.dram_in("tab31", [1, 4])
    OH = k.dram_in("OH", [33, ND])
    OHW = k.dram_in("OHW", [33, ND])
    ovv = k.dram_in("ovv", [128, 4, 129])
    eall = k.dram_in("eall", [128, T])
    h_out = k.dram_out("h_out", [1024, D])
    B1 = k.dram_tmp("B1", [4912, 1024])
    G1 = k.dram_tmp("G1", [8 * 4912, 1024])
    B2 = k.dram_tmp("B2", [256, 8 * TOK])
    G2 = k.dram_tmp("G2", [8 * 256, 8 * TOK])
    gidx = k.dram_in("gidx", [128, 145], U32)
    IDX = k.sb("IDX", [128, 145], U32)
    k.dma("sp", IDX[:], gidx[:, :], writes=["IDX"])
    B2v = B2.rearrange("f (w t) -> f w t", w=8)
    G1r = G1

    def ldg(tile, i, reads, writes):
        n = tile.shape[0]
        for r in range(8):
            k.gather(tile[:, r * 1024:(r + 1) * 1024], G1r[:, :], IDX[0:n, i * 8 + r:i * 8 + r + 1], reads=list(reads) + ["IDX"], writes=writes)

    def wr2(r0, r1, t0, src, reads):
        w = t0 // 1024
        off = 128 + (t0 % 1024)
        k.dma("sp", B2v[r0:r1, w, off:off + 512], src, reads=reads, writes=["B2"])
        if t0 % 1024 == 512 and w < 7:
            k.dma("sp", B2v[r0:r1, w + 1, 0:128], src[:, 384:512], reads=reads, writes=["B2"])
    HSAVE = k.dram_tmp("HSAVE", [TOK, D])
    BT = k.dram_tmp("BT", [62 * 128, 512])
    bias_setup(k, tabx, OH, OHW, BT, psb)
    with k.scope():
        ZZ = k.sb("ZZ", [128, 256])
        k.op("pool", lambda e: e.memset(ZZ[:], 0.0), writes=["ZZ"])
        k.dma("sp", B2v[0:128, 0, 0:128], ZZ[:, 0:128], reads=["ZZ"], writes=["B2"])
        k.dma("sp", B2v[128:256, 0, 0:128], ZZ[:, 0:128], reads=["ZZ"], writes=["B2"])
    with k.scope():
        H = k.sb("H0s", [128, NT, D])
        HT = k.sb("HT0s", [128, 16, TOK], BF16)
        ident = k.sb("ident0", [128, 128])
        k.op("pool", lambda e: e.memset(ident[:], 1.0), writes=["ident"])
        k.op("pool", lambda e: e.affine_select(out=ident[:], in_=ident[:], pattern=[[-1, 128]], compare_op=ALU.is_equal,
                                                fill=k.fill(0.0), base=0, channel_multiplier=1), reads=["ident"], writes=["ident"])
        pc = 0
        for i in range(1, NT):
            k.dma("sp", H[:, i, :], h_in[i * 128:(i + 1) * 128, :], writes=["H%d" % i])
            for kb in range(4):
                p, pk = psb[pc % 7], "psb%d" % (pc % 7)
                pc += 1
                for q in range(4):
                    kk = kb * 4 + q
                    k.op("pe", lambda e: e.transpose(out=p[:, q * 128:(q + 1) * 128], in_=H[:, i, kk * 128:(kk + 1) * 128],
                                                     identity=ident[:]), reads=["H%d" % i, "ident"], writes=[pk])
                k.op("act", lambda e: e.activation(out=HT[:, kb * 4:(kb + 1) * 4, i * 128:(i + 1) * 128],
                                                   in_=p[:].rearrange("p (a b) -> p a b", a=4), func=AF.Copy), reads=[pk], writes=["HT%d" % i])

        def ht_keys(c0, c1):
            return ["HT%d" % i for i in range(c0 // 128, (c1 - 1) // 128 + 1)]
        k.dma("sp", B1[4656:4912, :].rearrange("(p a) t -> p (a t)", a=2), H[:, NT - 1, :], reads=["H%d" % (NT - 1)], writes=["B1"])
        proj_phase(k, HT, ht_keys, W["w_in"][0], B1, psb)
    k.collective("AllGather", B1, G1, reads=["B1"], writes=["G1"])
    if stage == 1:
        UTd = k.sb("UTd", [64, T])
        ldg(UTd, 0, ["G1"], ["UTd"])
        o1 = k.dram_out("dbg_ut", [64, T])
        k.dma("sp", o1[:, :], UTd[:], reads=["UTd"], is_output=True)
        HHd = k.sb("HHd", [128, D])
        k.gather(HHd[:], G1.rearrange("(a b) t -> a (b t)", b=2), IDX[:, 144:145], reads=["G1", "IDX"], writes=["HHd"])
        o2 = k.dram_out("dbg_hh", [128, D])
        k.dma("sp", o2[:, :], HHd[:], reads=["HHd"], is_output=True)
        k.finish()
        return k.nc
    dbg = {}
    for l in range(NL):
        build_s5(k, ldg, wr2, 0, lam[l], bre[l], bim[l], cre[l], cim[l], dvec[l], psb)
        build_sb(k, ldg, wr2, 1, 2, 3, psb, ptb)
        build_nsa(k, ldg, wr2, [4, 5], [6, 7], 8, 9, 10, 11, 12, 13, 14, cw1[l], cw2[l], posT[l], tab31, ovv, eall, BT, psb, ptb)
        if debug and l == 0:
            dbg["y"] = k.dram_out("dbg_y", [256, 8 * TOK])
            k.dma("sp", dbg["y"][:, :], B2[:, :], reads=["B2"], is_output=True)
            dbg["p"] = k.dram_out("dbg_p", [4912, 1024])
            k.dma("sp", dbg["p"][:, :], B1[:, :], reads=["B1"], is_output=True)
        if stage == 2:
            k.finish()
            return k.nc
        k.collective("AllGather", B2, G2, reads=["B2"], writes=["G2"])
        if stage == 3:
            CAd = k.sb("CAd", [128, TOK], BF16)
            CAf = k.sb("CAf", [128, TOK])
            k.gather(CAd[:], G2.rearrange("r (w t) -> (r w) t", w=8), IDX[:, 136:137], reads=["G2", "IDX"], writes=["CAd"])
            k.op("dve", lambda e: e.tensor_copy(out=CAf[:], in_=CAd[:]), reads=["CAd"], writes=["CAf"])
            o3 = k.dram_out("dbg_ca", [128, TOK])
            k.dma("sp", o3[:, :], CAf[:], reads=["CAf"], is_output=True)
            k.finish()
            return k.nc
        with k.scope():
            dense_layer(k, l, NL, W, h_in, G1, G2, B1, HSAVE, h_out, IDX, psb, stage=stage)
        if l < NL - 1:
            k.collective("AllGather", B1, G1, reads=["B1"], writes=["G1"])
    k.finish()
    return k.nc


def nsa_consts():
    d = DMIN + np.arange(ND)
    bucket = np.zeros(ND, np.int64)
    for kk in range(1, 32):
        bucket += (d >= THR[kk])
    OH = np.zeros((33, ND), np.float32)
    OHW = np.zeros((33, ND), np.float32)
    valid = d >= 0
    OH[bucket[valid], np.nonzero(valid)[0]] = 1.0
    OH[32, ~valid] = 1.0
    vw = (d >= 0) & (d < 512)
    OHW[bucket[vw], np.nonzero(vw)[0]] = 1.0
    OHW[32, ~vw] = 1.0
    n = np.arange(512)
    jj = np.arange(128)
    ov = ((16 * n[:, None] < 64 * (jj[None, :] + 1)) & (16 * n[:, None] + 31 >= 64 * jj[None, :]) & (n[:, None] <= 510)).astype(np.float32)
    ovv = np.concatenate([ov, (n[:, None] <= 510).astype(np.float32)], 1)
    ovv = np.ascontiguousarray(ovv.reshape(4, 128, 129).transpose(1, 0, 2))
    eall = np.zeros((128, 8192), np.float32)
    eall[np.arange(8192) // 64, np.arange(8192)] = 240000.0
    return dict(OH=OH, OHW=OHW, ovv=ovv, eall=eall)


def gather_indices(c):
    g, oth = c // 2, c ^ 1
    p = np.arange(128)
    idx = np.zeros((128, 145), np.int64)
    bases = [64 * c, 512 + 64 * c, 1024 + 64 * c, 1536 + 64 * c, 2048 + 128 * c, 2048 + 128 * c + 64, 2048 + 128 * oth, 2048 + 128 * oth + 64,
             3072 + 64 * g, 3328 + 64 * g, 3584 + 64 * g, 3840 + 64 * g, 4096 + 64 * g, 4352 + 64 * g, 4608 + 6 * c]
    for i, b in enumerate(bases):
        n = 6 if i == 14 else 64
        for r in range(8):
            idx[:n, i * 8 + r] = r * 4912 + b + p[:n]
    for kc in range(4):
        rank = 2 * kc + p // 64
        for tc in range(3):
            idx[:, 120 + kc * 3 + tc] = ((rank * 256 + p % 64) * 8 + c) * 3 + tc
        idx[:, 132 + kc] = (rank * 256 + 64 + p % 64) * 8 + c
    for r in range(8):
        idx[:, 136 + r] = (r * 256 + 128 + p) * 8 + c
    idx[:, 144] = (((c + 7) % 8) * 4912 + 4656) // 2 + p
    return idx.astype(np.uint32)


def halo_rows(a, c):
    if c == 0:
        return np.concatenate([np.zeros((128, a.shape[1]), a.dtype), a[0:1024]], 0)
    return a[1024 * c - 128:1024 * c + 1024]


def fused_in_maps(inp, NL):
    consts = nsa_consts()
    L = slice(0, NL)
    shared = {
        "w_in": inp["w_in"][L], "w_glu": inp["s5_w_glu"][L],
        "b_glu": np.ascontiguousarray(inp["s5_b_glu"][L].reshape(NL, 4, 128).transpose(0, 2, 1)),
        "w_out": inp["w_out"][L], "ln_g": inp["ln_g"][L], "ln_b": inp["ln_b"][L],
        "memT": np.ascontiguousarray(inp["mem"][0].T), "xa_wq": inp["xa_wq"][L], "xa_wkv": inp["xa_wkv"][L], "xa_wo": inp["xa_wo"][L],
        "w_up": inp["ffn_w_up"][L],
        "convw": np.ascontiguousarray(inp["ffn_conv_w"][L].reshape(NL, 3, 88, 128).transpose(0, 3, 2, 1)),
        "convb": np.ascontiguousarray(inp["ffn_conv_b"][L].reshape(NL, 88, 128).transpose(0, 2, 1)),
        "w_down": inp["ffn_w_down"][L],
        "cw1": np.ascontiguousarray(inp["nsa_cmp_w1"][L].reshape(NL, 2, 32, 64, 128).transpose(0, 3, 1, 2, 4)),
        "cw2": np.ascontiguousarray(inp["nsa_cmp_w2"][L].transpose(0, 2, 1, 3)),
        "posT": np.ascontiguousarray(inp["nsa_cmp_pos"][L].transpose(0, 3, 1, 2)),
    }
    shared.update(consts)
    maps = []
    x = inp["x"][0]
    for c in range(8):
        m = dict(shared)
        m["h_in"] = np.ascontiguousarray(halo_rows(x, c))
        m["flag"] = np.full((128, 1), 0.0 if c == 0 else 1.0, np.float32)
        lam = np.zeros((NL, 128, 2, 3), np.float32)
        bre = np.zeros((NL, 128, 2, 64), np.float32)
        bim = np.zeros_like(bre)
        cre = np.zeros_like(bre)
        cim = np.zeros_like(bre)
        for rt in range(2):
            for half in range(2):
                gl = 2 * rt + half
                g = 4 * c + gl
                rows = slice(half * 64, half * 64 + 64)
                lam[:, rows, rt, 0] = inp["s5_lambda_re"][L, g]
                lam[:, rows, rt, 1] = inp["s5_lambda_im"][L, g]
                lam[:, rows, rt, 2] = inp["s5_log_dt"][L, g][:, None]
                bre[:, rows, rt, gl * 16:gl * 16 + 16] = inp["s5_b_re"][L, g]
                bim[:, rows, rt, gl * 16:gl * 16 + 16] = inp["s5_b_im"][L, g]
                cre[:, rows, rt, gl * 16:gl * 16 + 16] = inp["s5_c_re"][L, g].transpose(0, 2, 1)
                cim[:, rows, rt, gl * 16:gl * 16 + 16] = inp["s5_c_im"][L, g].transpose(0, 2, 1)
        m.update(lam=lam, bre=bre, bim=bim, cre=cre, cim=cim)
        m["dvec"] = np.ascontiguousarray(inp["s5_d"][L, 64 * c:64 * c + 64].reshape(NL, 64, 1))
        heads = [2 * c, 2 * c + 1, 2 * (c ^ 1), 2 * (c ^ 1) + 1]
        tab = inp["rel_bias"][:, heads]
        m["tabx"] = np.ascontiguousarray(np.concatenate([tab, np.full((1, 4), -30000.0, np.float32)], 0))
        m["tab31"] = np.ascontiguousarray(tab[31:32, :])
        m["gidx"] = gather_indices(c)
        maps.append(m)
    return maps


_CACHE = {}


def kernel(**inputs):
    inp = {kk: np.ascontiguousarray(np.asarray(v, dtype=np.float32)) for kk, v in inputs.items()}
    if "nc" not in _CACHE:
        _CACHE["nc"] = build_fused(4)
    res = run_bass_kernel_spmd(_CACHE["nc"], fused_in_maps(inp, 4), core_ids=list(range(8)))
    h = np.concatenate([r["h_out"] for r in res.results], 0)
    return h[None].astype(np.float32)
```

```python
import numpy as np
from contextlib import ExitStack
import concourse.bass as bass
import concourse.mybir as mybir
from concourse.bass_utils import run_bass_kernel_spmd

F32 = mybir.dt.float32
BF16 = mybir.dt.bfloat16
I32 = mybir.dt.int32
AF = mybir.ActivationFunctionType
ALU = mybir.AluOpType
AX = mybir.AxisListType

NDMA_SEMS = 24
GEN_MAX = 16000


class K:
    def __init__(self):
        self.nc = bass.Bass("TRN2", target_bir_lowering=False)
        nc = self.nc
        self.es = ExitStack()
        self.es0 = self.es
        self.eng = {"pe": nc.tensor, "dve": nc.vector, "act": nc.scalar, "pool": nc.gpsimd, "sp": nc.sync}
        self.sem = {}
        self.cnt = {}
        self.gen = {e: 0 for e in self.eng}
        ngen = {"pe": 10, "dve": 6, "act": 6, "pool": 3, "sp": 1}
        for e in self.eng:
            for g in range(ngen[e]):
                self.sem[(e, g)] = self.es.enter_context(nc.semaphore("s_%s_%d" % (e, g)))
                self.cnt[(e, g)] = 0
        self.csems = [self.es.enter_context(nc.semaphore("s_cc%d" % i)) for i in range(10)]
        self.ccnt = 0
        self.waited = {}
        self.dsem = [self.es.enter_context(nc.semaphore("d%d" % i)) for i in range(NDMA_SEMS)]
        self.dcnt = [0] * NDMA_SEMS
        self.dnext = 0
        self.lastw = {}
        self.readers = {}
        self.ninstr = 0
        self.out_tokens = []

    def sb(self, name, shape, dt=F32):
        self.nalloc = getattr(self, "nalloc", 0) + 1
        return self.es.enter_context(self.nc.sbuf_tensor("%s_%d" % (name, self.nalloc), list(shape), dt))

    def ps(self, name, shape, dt=F32):
        return self.es.enter_context(self.nc.psum_tensor(name, list(shape), dt))

    def dram_in(self, name, shape, dt=F32):
        return self.nc.dram_tensor(name, list(shape), dt, kind="ExternalInput").ap()

    def dram_out(self, name, shape, dt=F32):
        return self.nc.dram_tensor(name, list(shape), dt, kind="ExternalOutput").ap()

    def dram_tmp(self, name, shape, dt=F32):
        return self.nc.dram_tensor(name, list(shape), dt).ap()

    def fill(self, v):
        if not hasattr(self, "_fills"):
            self._fills = {}
        if v not in self._fills:
            self._fills[v] = self.nc.gpsimd.to_reg(float(v))
        return self._fills[v]

    def _wait(self, e, tok):
        kind, src, val = tok
        k = (e, kind, src)
        if self.waited.get(k, 0) >= val:
            return
        self.waited[k] = val
        sem = self.sem[src] if kind == "e" else (self.csems[src] if kind == "c" else self.dsem[src])
        self.eng[e].wait_ge(sem, val)

    def _deps(self, e, reads, writes):
        toks = []
        for r in reads:
            if r in self.lastw:
                toks.append(self.lastw[r])
        for w in writes:
            if w in self.lastw:
                toks.append(self.lastw[w])
            toks.extend(self.readers.get(w, ()))
        for t in toks:
            if t[0] == "e" and t[1][0] == e and e == "pe":
                continue
            self._wait(e, t)

    def _commit(self, tok, reads, writes):
        for w in writes:
            self.lastw[w] = tok
            self.readers[w] = []
        for r in reads:
            if r not in writes:
                self.readers.setdefault(r, []).append(tok)
                if len(self.readers[r]) > 64:
                    self.readers[r] = self._compact(self.readers[r])

    @staticmethod
    def _compact(toks):
        best = {}
        for t in toks:
            k = (t[0], t[1])
            if k not in best or best[k][2] < t[2]:
                best[k] = t
        return list(best.values())

    def op(self, e, fn, reads=(), writes=()):
        self._deps(e, reads, writes)
        ins = fn(self.eng[e])
        g = self.gen[e]
        if self.cnt[(e, g)] >= GEN_MAX:
            g += 1
            self.gen[e] = g
            assert (e, g) in self.sem, "out of pre-allocated semaphore generations for " + e
        self.cnt[(e, g)] += 1
        ins.then_inc(self.sem[(e, g)], 1)
        tok = ("e", (e, g), self.cnt[(e, g)])
        self._commit(tok, reads, writes)
        self.ninstr += 1
        return tok

    def dma(self, e, out, in_, reads=(), writes=(), is_output=False, **kw):
        k = self.dnext
        self.dnext = (self.dnext + 1) % NDMA_SEMS
        if self.dcnt[k] > 0:
            self._wait(e, ("d", k, self.dcnt[k]))
        self._deps(e, reads, writes)
        ins = self.eng[e].dma_start(out=out, in_=in_, **kw)
        self.dcnt[k] += 16
        ins.then_inc(self.dsem[k], 16)
        tok = ("d", k, self.dcnt[k])
        self._commit(tok, reads, writes)
        self.ninstr += 1
        if is_output:
            self.out_tokens.append(tok)
        return tok

    def finish(self):
        for k in range(NDMA_SEMS):
            if self.dcnt[k] > 0:
                self._wait("sp", ("d", k, self.dcnt[k]))
        for e in self.eng:
            g = self.gen[e]
            if e != "sp" and self.cnt[(e, g)] > 0:
                self._wait("sp", ("e", (e, g), self.cnt[(e, g)]))
            elif e != "sp" and g > 0:
                self._wait("sp", ("e", (e, g - 1), self.cnt[(e, g - 1)]))
        if self.ccnt > 0:
            self._wait("sp", ("c", self.ccnt - 1, 1))
        self.es.close()
        return self.nc


def _barrier(self):
    for e in self.eng:
        for s in self.eng:
            g = self.gen[s]
            if s != e and self.cnt[(s, g)] > 0:
                self._wait(e, ("e", (s, g), self.cnt[(s, g)]))
            elif s != e and g > 0:
                self._wait(e, ("e", (s, g - 1), self.cnt[(s, g - 1)]))
        if self.ccnt > 0:
            self._wait(e, ("c", self.ccnt - 1, 1))
        for q in range(NDMA_SEMS):
            if self.dcnt[q] > 0:
                self._wait(e, ("d", q, self.dcnt[q]))


class _Scope:
    def __init__(self, k):
        self.k = k

    def __enter__(self):
        self.saved = self.k.es
        self.k.es = ExitStack()
        return self

    def __exit__(self, *a):
        self.k.barrier()
        self.k.es.close()
        self.k.es = self.saved
        return False


K.barrier = _barrier
K.scope = lambda self: _Scope(self)


def _collective(self, kind, src, dst, reads=(), writes=()):
    self._deps("pool", reads, writes)
    ins = self.nc.gpsimd.collective_compute(kind, ALU.bypass, replica_groups=[list(range(8))], ins=[src.opt()], outs=[dst.opt()])
    sem = self.csems[self.ccnt]
    ins.then_inc(sem)
    tok = ("c", self.ccnt, 1)
    self.ccnt += 1
    self._commit(tok, reads, writes)
    return tok


K.collective = _collective


def _gather(self, out, in_, idx, reads=(), writes=()):
    q = self.dnext
    self.dnext = (self.dnext + 1) % NDMA_SEMS
    if self.dcnt[q] > 0:
        self._wait("pool", ("d", q, self.dcnt[q]))
    self._deps("pool", reads, writes)
    ins = self.nc.gpsimd.indirect_dma_start(out=out, out_offset=None, in_=in_, in_offset=bass.IndirectOffsetOnAxis(ap=idx, axis=0))
    self.dcnt[q] += 16
    ins.then_inc(self.dsem[q], 16)
    tok = ("d", q, self.dcnt[q])
    self._commit(tok, reads, writes)
    self.ninstr += 1
    return tok


K.gather = _gather
U32 = mybir.dt.uint32

from concourse.bass_types import AP
import math

ALPHA = (2.0 * 4) ** 0.25
LN_EPS = 1e-5
D = 2048
DFF = 5632
NIN = 4656
NT = 9
TOK = NT * 128
XA_SCALE = 128 ** -0.5


def dense_layer(k, l, NL, W, h_in, G1, G2, B1, HSAVE, h_out, IDX, psb, stage=None):
    nc = k.nc
    w_glu, b_glu, w_out, ln_g, ln_b = W["w_glu"][l], W["b_glu"][l], W["w_out"][l], W["ln_g"][l], W["ln_b"][l]
    memT, xa_wq, xa_wkv, xa_wo = W["memT"], W["xa_wq"][l], W["xa_wkv"][l], W["xa_wo"][l]
    w_up, convw, convb, w_down, flag = W["w_up"][l], W["convw"][l], W["convb"][l], W["w_down"][l], W["flag"]
    do_chain = True
    do_proj = l < NL - 1
    G2a = G2.rearrange("r (w t) -> (r w) t", w=8)
    G2b = G2.rearrange("r (w t) -> (r w) t", w=24)
    H = k.sb("H", [128, NT, D])
    HT = k.sb("HT", [128, 16, TOK], BF16)
    WB = [k.sb("WB%d" % i, [128, 16, 256], BF16) for i in range(2)]
    ident = k.sb("ident", [128, 128])
    st = {"p": 0, "w": 0}

    def ps():
        i = st["p"]
        st["p"] = (i + 1) % 7
        return psb[i], "psb%d" % i

    def wb():
        i = st["w"]
        st["w"] = (i + 1) % 2
        return WB[i], "WB%d" % i

    k.op("pool", lambda e: e.memset(ident[:], 1.0), writes=["ident"])
    k.op("pool", lambda e: e.affine_select(out=ident[:], in_=ident[:], pattern=[[-1, 128]], compare_op=ALU.is_equal,
                                            fill=k.fill(0.0), base=0, channel_multiplier=1), reads=["ident"], writes=["ident"])
    if l == 0:
        for i in range(NT):
            k.dma("sp", H[:, i, :], h_in[i * 128:(i + 1) * 128, :], writes=["H%d" % i])
    else:
        for i in range(1, NT):
            k.dma("sp", H[:, i, :], HSAVE[i * 128:(i + 1) * 128, :], reads=["HSAVE"], writes=["H%d" % i])
        k.gather(H[:, 0, :], G1.rearrange("(a b) t -> a (b t)", b=2), IDX[:, 144:145], reads=["G1", "IDX"], writes=["H0"])

    def ht_keys(c0, c1):
        return ["HT%d" % i for i in range(c0 // 128, (c1 - 1) // 128 + 1)]

    def transposes(i):
        for kb in range(4):
            p, pk = ps()
            for q in range(4):
                kk = kb * 4 + q
                k.op("pe", lambda e: e.transpose(out=p[:, q * 128:(q + 1) * 128], in_=H[:, i, kk * 128:(kk + 1) * 128],
                                                 identity=ident[:]), reads=["H%d" % i, "ident"], writes=[pk])
            k.op("act", lambda e: e.activation(out=HT[:, kb * 4:(kb + 1) * 4, i * 128:(i + 1) * 128],
                                               in_=p[:].rearrange("p (a b) -> p a b", a=4), func=AF.Copy),
                 reads=[pk], writes=["HT%d" % i])

    def layer_norm(i, GB, stat, mv, sd):
        x = H[:, i, :]
        hk = "H%d" % i
        for c in range(4):
            k.op("dve", lambda e: e.bn_stats(out=stat[:, c, :], in_=H[:, i, c * 512:(c + 1) * 512]), reads=[hk], writes=["stat"])
        k.op("dve", lambda e: e.bn_aggr(out=mv[:], in_=stat[:].rearrange("p a b -> p (a b)")), reads=["stat"], writes=["mv"])
        k.op("dve", lambda e: e.tensor_scalar(out=sd[:], in0=mv[:, 1:2], scalar1=LN_EPS, scalar2=None, op0=ALU.add),
             reads=["mv"], writes=["sd"])
        k.op("act", lambda e: e.activation(out=sd[:], in_=sd[:], func=AF.Sqrt), reads=["sd"], writes=["sd"])
        k.op("dve", lambda e: e.reciprocal(out=sd[:], in_=sd[:]), reads=["sd"], writes=["sd"])
        k.op("dve", lambda e: e.tensor_scalar(out=x, in0=x, scalar1=mv[:, 0:1], scalar2=sd[:, 0:1], op0=ALU.subtract, op1=ALU.mult),
             reads=[hk, "mv", "sd"], writes=[hk])
        k.op("pool", lambda e: e.tensor_tensor(out=x, in0=x, in1=GB[:, 0, :], op=ALU.mult), reads=[hk, "GB"], writes=[hk])
        k.op("pool", lambda e: e.tensor_tensor(out=x, in0=x, in1=GB[:, 1, :], op=ALU.add), reads=[hk, "GB"], writes=[hk])

    def load_gb(GB, which):
        k.dma("sp", GB[:, 0, :], ln_g[which, :].partition_broadcast(128), writes=["GB"])
        k.dma("sp", GB[:, 1, :], ln_b[which, :].partition_broadcast(128), writes=["GB"])

    def resid(i, c0, c1, p, pk, first=True):
        x = H[:, i, c0:c1]
        if first:
            k.op("dve", lambda e: e.scalar_tensor_tensor(out=x, in0=x, scalar=ALPHA, in1=p, op0=ALU.mult, op1=ALU.add),
                 reads=["H%d" % i, pk], writes=["H%d" % i])
        else:
            k.op("dve", lambda e: e.tensor_tensor(out=x, in0=x, in1=p, op=ALU.add), reads=["H%d" % i, pk], writes=["H%d" % i])

    if do_chain:
        with k.scope():
            CATT = k.sb("CATT", [128, 16, TOK], BF16)
            Y5 = k.sb("Y5", [128, 4, 384])
            Z = k.sb("Z", [128, 4, 384])
            ZB = k.sb("ZB", [128, 4, 384], BF16)
            SG = k.sb("SG", [128, 384])
            WGLU = k.sb("WGLU", [128, 4, 512], BF16)
            BGLU = k.sb("BGLU", [128, 4])
            GB = k.sb("GB", [128, 2, D])
            stat = k.sb("stat", [128, 4, 6])
            mv = k.sb("mv", [128, 2])
            sd = k.sb("sd", [128, 1])
            load_gb(GB, 0)
            k.dma("pool", WGLU[:], w_glu.rearrange("(c p) n -> p c n", p=128), writes=["WGLU"])
            k.dma("sp", BGLU[:], b_glu[:, :], writes=["BGLU"])
            for kc in range(4):
                k.gather(CATT[:, 4 + kc, :], G2a, IDX[:, 132 + kc:133 + kc], reads=["G2", "IDX"], writes=["CA_sb"])
            for r in range(8):
                k.gather(CATT[:, 8 + r, :], G2a, IDX[:, 136 + r:137 + r], reads=["G2", "IDX"], writes=["CA_nsa"])
            for tc in range(3):
                cs = slice(tc * 384, (tc + 1) * 384)
                for kc in range(4):
                    k.gather(Y5[:, kc, :], G2b, IDX[:, 120 + kc * 3 + tc:121 + kc * 3 + tc], reads=["G2", "IDX"], writes=["Y5"])
                k.op("act", lambda e: e.activation(out=Z[:], in_=Y5[:], func=AF.Gelu_apprx_tanh), reads=["Y5"], writes=["Z"])
                k.op("pool", lambda e: e.tensor_copy(out=ZB[:], in_=Z[:]), reads=["Z"], writes=["ZB"])
                for co in range(4):
                    p, pk = ps()
                    for ci in range(4):
                        k.op("pe", lambda e: e.matmul(p[:, 0:384], lhsT=WGLU[:, ci, co * 128:(co + 1) * 128], rhs=ZB[:, ci, :],
                                                      start=(ci == 0), stop=(ci == 3)), reads=["WGLU", "ZB"], writes=[pk])
                    k.op("act", lambda e: e.activation(out=SG[:], in_=p[:, 0:384], func=AF.Sigmoid, bias=BGLU[:, co:co + 1]),
                         reads=[pk, "BGLU"], writes=["SG"])
                    k.op("dve", lambda e: e.tensor_tensor(out=CATT[:, co, cs], in0=Z[:, co, :], in1=SG[:], op=ALU.mult),
                         reads=["Z", "SG"], writes=["CA_s5_%d" % tc])
            for n in range(8):
                w, wk = wb()
                k.dma("pool", w[:], w_out[:, n * 256:(n + 1) * 256].rearrange("(k p) n -> p k n", p=128), writes=[wk])
                for i in range(NT):
                    p, pk = ps()
                    for kk in range(16):
                        k.op("pe", lambda e: e.matmul(p[:, 0:256], lhsT=CATT[:, kk, i * 128:(i + 1) * 128], rhs=w[:, kk, :],
                                                      start=(kk == 0), stop=(kk == 15)),
                             reads=[wk, "CA_s5_%d" % (i // 3), "CA_sb", "CA_nsa"], writes=[pk])
                    resid(i, n * 256, (n + 1) * 256, p[:, 0:256], pk)
            for i in range(NT):
                layer_norm(i, GB, stat, mv, sd)
                transposes(i)

        if stage == 4:
            for i in range(1, NT):
                k.dma("sp", h_out[(i - 1) * 128:i * 128, :], H[:, i, :], reads=["H%d" % i], is_output=True)
            return
        with k.scope():
            MEMT = k.sb("MEMT", [128, 16, 256], BF16)
            KT = k.sb("KT", [128, 4, 256], BF16)
            V = k.sb("V", [128, 2, 512], BF16)
            QT = k.sb("QT", [128, 384], BF16)
            PT = k.sb("PT", [128, 2, 384], BF16)
            RZ = k.sb("RZ", [128, 384])
            OT = k.sb("OT", [128, 4, TOK], BF16)
            WO = k.sb("WO", [128, 4, D], BF16)
            ONES = k.sb("ONES", [128, 128], BF16)
            GB = k.sb("GB", [128, 2, D])
            stat = k.sb("stat", [128, 4, 6])
            mv = k.sb("mv", [128, 2])
            sd = k.sb("sd", [128, 1])
            load_gb(GB, 1)
            k.op("pool", lambda e: e.memset(ONES[:], 1.0), writes=["ONES"])
            k.dma("pool", MEMT[:], memT.rearrange("(k p) m -> p k m", p=128), writes=["MEMT"])
            k.dma("pool", WO[:], xa_wo.rearrange("(h p) n -> p h n", p=128), writes=["WO"])
            for c in range(4):
                w, wk = wb()
                k.dma("pool", w[:], xa_wkv[:, c * 256:(c + 1) * 256].rearrange("(k p) n -> p k n", p=128), writes=[wk])
                if c < 2:
                    for hh in range(2):
                        p, pk = ps()
                        for kk in range(16):
                            k.op("pe", lambda e: e.matmul(p[:, 0:256], lhsT=w[:, kk, hh * 128:(hh + 1) * 128], rhs=MEMT[:, kk, :],
                                                          start=(kk == 0), stop=(kk == 15)), reads=[wk, "MEMT"], writes=[pk])
                        k.op("act", lambda e: e.activation(out=KT[:, 2 * c + hh, :], in_=p[:, 0:256], func=AF.Copy),
                             reads=[pk], writes=["KT"])
                else:
                    for mt in range(2):
                        p, pk = ps()
                        for kk in range(16):
                            k.op("pe", lambda e: e.matmul(p[:, 0:256], lhsT=MEMT[:, kk, mt * 128:(mt + 1) * 128], rhs=w[:, kk, :],
                                                          start=(kk == 0), stop=(kk == 15)), reads=[wk, "MEMT"], writes=[pk])
                        k.op("act", lambda e: e.activation(out=V[:, mt, (c - 2) * 256:(c - 1) * 256], in_=p[:, 0:256], func=AF.Copy),
                             reads=[pk], writes=["V"])
            for c in range(2):
                w, wk = wb()
                k.dma("pool", w[:], xa_wq[:, c * 256:(c + 1) * 256].rearrange("(k p) n -> p k n", p=128), writes=[wk])
                for tc in range(3):
                    cs = slice(tc * 384, (tc + 1) * 384)
                    for hh in range(2):
                        h = 2 * c + hh
                        p, pk = ps()
                        for kk in range(16):
                            k.op("pe", lambda e: e.matmul(p[:, 0:384], lhsT=w[:, kk, hh * 128:(hh + 1) * 128], rhs=HT[:, kk, cs],
                                                          start=(kk == 0), stop=(kk == 15)),
                                 reads=[wk] + ht_keys(tc * 384, tc * 384 + 384), writes=[pk])
                        k.op("act", lambda e: e.activation(out=QT[:], in_=p[:, 0:384], func=AF.Copy), reads=[pk], writes=["QT"])
                        for mt in range(2):
                            p2, pk2 = ps()
                            k.op("pe", lambda e: e.matmul(p2[:, 0:384], lhsT=KT[:, h, mt * 128:(mt + 1) * 128], rhs=QT[:],
                                                          start=True, stop=True), reads=["KT", "QT"], writes=[pk2])
                            k.op("act", lambda e: e.activation(out=PT[:, mt, :], in_=p2[:, 0:384], func=AF.Exp, scale=XA_SCALE),
                                 reads=[pk2], writes=["PT%d" % mt])
                        po, pko = ps()
                        pz, pkz = ps()
                        for mt in range(2):
                            k.op("pe", lambda e: e.matmul(po[:, 0:384], lhsT=V[:, mt, h * 128:(h + 1) * 128], rhs=PT[:, mt, :],
                                                          start=(mt == 0), stop=(mt == 1)), reads=["V", "PT%d" % mt], writes=[pko])
                        for mt in range(2):
                            k.op("pe", lambda e: e.matmul(pz[:, 0:384], lhsT=ONES[:], rhs=PT[:, mt, :],
                                                          start=(mt == 0), stop=(mt == 1)), reads=["ONES", "PT%d" % mt], writes=[pkz])
                        k.op("dve", lambda e: e.reciprocal(out=RZ[:], in_=pz[:, 0:384]), reads=[pkz], writes=["RZ"])
                        k.op("dve", lambda e: e.tensor_tensor(out=OT[:, h, cs], in0=po[:, 0:384], in1=RZ[:], op=ALU.mult),
                             reads=[pko, "RZ"], writes=["OT%d_%d" % (h, tc)])
            for i in range(NT):
                for n in range(4):
                    p, pk = ps()
                    for h in range(4):
                        k.op("pe", lambda e: e.matmul(p[:], lhsT=OT[:, h, i * 128:(i + 1) * 128], rhs=WO[:, h, n * 512:(n + 1) * 512],
                                                      start=(h == 0), stop=(h == 3)),
                             reads=["WO", "OT%d_%d" % (h, i // 3)], writes=[pk])
                    resid(i, n * 512, (n + 1) * 512, p[:], pk)
                layer_norm(i, GB, stat, mv, sd)
                transposes(i)

        if stage == 5:
            for i in range(1, NT):
                k.dma("sp", h_out[(i - 1) * 128:i * 128, :], H[:, i, :], reads=["H%d" % i], is_output=True)
            return
        with k.scope():
            ACTT = k.sb("ACTT", [128, 11, 1024], BF16)
            WA = [k.sb("WA%d" % i, [128, 16, 128], BF16) for i in range(2)]
            WG = [k.sb("WG%d" % i, [128, 16, 128], BF16) for i in range(2)]
            WD = [k.sb("WD%d" % i, [128, 11, 256], BF16) for i in range(2)]
            CA = [k.sb("CA%d" % i, [128, 344]) for i in range(2)]
            CG = [k.sb("CG%d" % i, [128, 344]) for i in range(2)]
            CW = k.sb("CW", [128, 88, 3])
            CB = k.sb("CB", [128, 88])
            FL = k.sb("FL", [128, 1])
            GB = k.sb("GB", [128, 2, D])
            stat = k.sb("stat", [128, 4, 6])
            mv = k.sb("mv", [128, 2])
            sd = k.sb("sd", [128, 1])
            load_gb(GB, 2)
            k.dma("sp", CW[:], convw[:, :, :], writes=["CW"])
            k.dma("sp", CB[:], convb[:, :], writes=["CB"])
            k.dma("sp", FL[:], flag[:, :], writes=["FL"])
            pieces = [(0, 342), (342, 342), (684, 340)]
            cnt = 0
            wdc = 0
            for g in range(4):
                for jj in range(11):
                    j = g * 11 + jj
                    wa, wak = WA[cnt % 2], "WA%d" % (cnt % 2)
                    wg, wgk = WG[cnt % 2], "WG%d" % (cnt % 2)
                    k.dma("pool", wa[:], w_up[:, j * 128:(j + 1) * 128].rearrange("(k p) n -> p k n", p=128), writes=[wak])
                    k.dma("pool", wg[:], w_up[:, DFF + j * 128:DFF + (j + 1) * 128].rearrange("(k p) n -> p k n", p=128), writes=[wgk])
                    for pi, (t0, n) in enumerate(pieces):
                        c0 = 126 + t0
                        hk = ht_keys(c0, c0 + n + 2)
                        ca, cak = CA[cnt % 2], "CA%d" % (cnt % 2)
                        cg, cgk = CG[cnt % 2], "CG%d" % (cnt % 2)
                        cnt += 1
                        for (wt, wtk, ch, buf, bk) in ((wa, wak, j, ca, cak), (wg, wgk, 44 + j, cg, cgk)):
                            p, pk = ps()
                            for kk in range(16):
                                k.op("pe", lambda e: e.matmul(p[:, 0:n + 2], lhsT=wt[:, kk, :], rhs=HT[:, kk, c0:c0 + n + 2],
                                                              start=(kk == 0), stop=(kk == 15)), reads=[wtk] + hk, writes=[pk])
                            if pi == 0:
                                k.op("dve", lambda e: e.tensor_scalar(out=p[:, 0:2], in0=p[:, 0:2], scalar1=FL[:, 0:1], scalar2=None,
                                                                      op0=ALU.mult), reads=[pk, "FL"], writes=[pk])
                            k.op("act", lambda e: e.activation(out=buf[:, 0:n], in_=p[:, 2:n + 2], func=AF.Identity,
                                                               scale=CW[:, ch, 2:3], bias=CB[:, ch:ch + 1]),
                                 reads=[pk, "CW", "CB"], writes=[bk])
                            k.op("dve", lambda e: e.scalar_tensor_tensor(out=buf[:, 0:n], in0=p[:, 1:n + 1], scalar=CW[:, ch, 1:2],
                                                                         in1=buf[:, 0:n], op0=ALU.mult, op1=ALU.add),
                                 reads=[pk, "CW", bk], writes=[bk])
                            k.op("dve", lambda e: e.scalar_tensor_tensor(out=buf[:, 0:n], in0=p[:, 0:n], scalar=CW[:, ch, 0:1],
                                                                         in1=buf[:, 0:n], op0=ALU.mult, op1=ALU.add),
                                 reads=[pk, "CW", bk], writes=[bk])
                        k.op("act", lambda e: e.activation(out=cg[:, 0:n], in_=cg[:, 0:n], func=AF.Gelu_apprx_tanh),
                             reads=[cgk], writes=[cgk])
                        k.op("pool", lambda e: e.tensor_tensor(out=ACTT[:, jj, t0:t0 + n], in0=ca[:, 0:n], in1=cg[:, 0:n], op=ALU.mult),
                             reads=[cak, cgk], writes=["AT%d" % jj])
                for n8 in range(8):
                    wd, wdk = WD[wdc % 2], "WD%d" % (wdc % 2)
                    wdc += 1
                    k.dma("pool", wd[:], w_down[g * 1408:(g + 1) * 1408, n8 * 256:(n8 + 1) * 256].rearrange("(j p) n -> p j n", p=128),
                          writes=[wdk])
                    for i in range(1, NT):
                        p, pk = ps()
                        for jj in range(11):
                            k.op("pe", lambda e: e.matmul(p[:, 0:256], lhsT=ACTT[:, jj, (i - 1) * 128:i * 128], rhs=wd[:, jj, :],
                                                          start=(jj == 0), stop=(jj == 10)), reads=[wdk, "AT%d" % jj], writes=[pk])
                        resid(i, n8 * 256, (n8 + 1) * 256, p[:, 0:256], pk, first=(g == 0))
            for i in range(1, NT):
                layer_norm(i, GB, stat, mv, sd)
                if not do_proj:
                    k.dma("sp", h_out[(i - 1) * 128:i * 128, :], H[:, i, :], reads=["H%d" % i], is_output=True)
                else:
                    k.dma("sp", HSAVE[i * 128:(i + 1) * 128, :], H[:, i, :], reads=["H%d" % i], writes=["HSAVE"])
                    transposes(i)
            if do_proj:
                k.dma("sp", B1[4656:4912, :].rearrange("(p a) t -> p (a t)", a=2), H[:, NT - 1, :], reads=["H%d" % (NT - 1), "G1"], writes=["B1"])
    if do_proj:
        proj_phase(k, HT, ht_keys, W["w_in"][l + 1], B1, psb)


def proj_phase(k, HT, ht_keys, w_in, B1, psb):
    with k.scope():
        WP = [k.sb("WP%d" % i, [128, 16, 128], BF16) for i in range(2)]
        STG = [k.sb("STG%d" % i, [128, 1024]) for i in range(2)]
        pc = 0
        for n in range(37):
            c0 = n * 128
            cw = min(128, NIN - c0)
            w, wk = WP[n % 2], "WP%d" % (n % 2)
            s, sk = STG[n % 2], "STG%d" % (n % 2)
            k.dma("pool", w[:, :, 0:cw], w_in[:, c0:c0 + cw].rearrange("(k p) n -> p k n", p=128), writes=[wk])
            for half in range(2):
                p, pk = psb[pc % 7], "psb%d" % (pc % 7)
                pc += 1
                t0 = 128 + half * 512
                for kk in range(16):
                    k.op("pe", lambda e: e.matmul(p[0:cw, :], lhsT=w[:, kk, 0:cw], rhs=HT[:, kk, t0:t0 + 512],
                                                  start=(kk == 0), stop=(kk == 15)), reads=[wk] + ht_keys(t0, t0 + 512), writes=[pk])
                k.op("act", lambda e: e.activation(out=s[0:cw, half * 512:(half + 1) * 512], in_=p[0:cw, :], func=AF.Copy), reads=[pk], writes=[sk])
            k.dma("sp", B1[c0:c0 + cw, :], s[0:cw, :], reads=[sk, "G1"], writes=["B1"])


T = 8192
SCALE = 64 ** -0.5
TWO_PI = 2.0 * math.pi
C1 = 6.28125
C2 = TWO_PI - C1


def build_s5(k, ldg, wr2, uT3, lam, bre, bim, cre, cim, dvec, psb):
    with k.scope():
        UT = k.sb("UT", [64, T])
        YT = k.sb("YT", [64, T])
        ident = k.sb("ident", [128, 128])
        DV = k.sb("DV", [64, 1])
        ldg(UT, uT3, ["G1"], ["UT"])
        k.dma("sp", DV[:], dvec[:, :], writes=["DV"])
        k.op("pool", lambda e: e.memset(ident[:], 1.0), writes=["ident"])
        k.op("pool", lambda e: e.affine_select(out=ident[:], in_=ident[:], pattern=[[-1, 128]], compare_op=ALU.is_equal,
                                                fill=k.fill(0.0), base=0, channel_multiplier=1), reads=["ident"], writes=["ident"])
        R = []
        for rt in range(2):
            d = {}
            P = "P%d" % rt
            for nm, shp in (("LAM", [128, 3]), ("BRE", [128, 64]), ("BIM", [128, 64]), ("CRE", [128, 64]), ("CIMN", [128, 64]),
                            ("BBR", [128, 64]), ("BBI", [128, 64]), ("BRT", [64, 128]), ("BIT", [64, 128]),
                            ("COS", [128, 512]), ("SIN", [128, 512]), ("RHO", [128, 512]), ("TMP", [128, 256]),
                            ("S", [128, 24]), ("KI", [128, 1])):
                d[nm] = k.sb("%s%d" % (nm, rt), shp, I32 if nm == "KI" else F32)
            k.dma("sp", d["LAM"][:], lam[:, rt, :], writes=[P])
            k.dma("sp", d["BRE"][:], bre[:, rt, :], writes=[P])
            k.dma("sp", d["BIM"][:], bim[:, rt, :], writes=[P])
            k.dma("sp", d["CRE"][:], cre[:, rt, :], writes=[P])
            k.dma("sp", d["CIMN"][:], cim[:, rt, :], writes=[P])
            S = d["S"]

            def sc(i):
                return S[:, i:i + 1]
            lr, li, ldt = d["LAM"][:, 0:1], d["LAM"][:, 1:2], d["LAM"][:, 2:3]

            def dv(fn):
                k.op("dve", fn, reads=[P], writes=[P])

            def ac(fn):
                k.op("act", fn, reads=[P], writes=[P])
            ac(lambda e: e.activation(out=sc(0), in_=ldt, func=AF.Exp))
            dv(lambda e: e.tensor_tensor(out=sc(1), in0=lr, in1=sc(0), op=ALU.mult))
            dv(lambda e: e.tensor_tensor(out=sc(2), in0=li, in1=sc(0), op=ALU.mult))
            ac(lambda e: e.activation(out=sc(3), in_=sc(1), func=AF.Exp))
            dv(lambda e: e.tensor_scalar(out=sc(4), in0=sc(2), scalar1=1.0 / TWO_PI, scalar2=None, op0=ALU.mult))
            dv(lambda e: e.tensor_copy(out=d["KI"][:], in_=sc(4)))
            dv(lambda e: e.tensor_copy(out=sc(4), in_=d["KI"][:]))
            dv(lambda e: e.scalar_tensor_tensor(out=sc(5), in0=sc(4), scalar=-C1, in1=sc(2), op0=ALU.mult, op1=ALU.add))
            dv(lambda e: e.scalar_tensor_tensor(out=sc(5), in0=sc(4), scalar=-C2, in1=sc(5), op0=ALU.mult, op1=ALU.add))
            dv(lambda e: e.tensor_scalar(out=sc(6), in0=sc(5), scalar1=math.pi, scalar2=-TWO_PI, op0=ALU.is_gt, op1=ALU.mult))
            dv(lambda e: e.tensor_tensor(out=sc(5), in0=sc(5), in1=sc(6), op=ALU.add))
            dv(lambda e: e.tensor_scalar(out=sc(6), in0=sc(5), scalar1=-math.pi, scalar2=TWO_PI, op0=ALU.is_lt, op1=ALU.mult))
            dv(lambda e: e.tensor_tensor(out=sc(5), in0=sc(5), in1=sc(6), op=ALU.add))
            ac(lambda e: e.activation(out=sc(7), in_=sc(5), func=AF.Sin))
            dv(lambda e: e.tensor_scalar(out=sc(6), in0=sc(5), scalar1=-1.0, scalar2=None, op0=ALU.mult))
            dv(lambda e: e.tensor_tensor(out=sc(6), in0=sc(6), in1=sc(5), op=ALU.max))
            dv(lambda e: e.tensor_scalar(out=sc(6), in0=sc(6), scalar1=-1.0, scalar2=math.pi / 2, op0=ALU.mult, op1=ALU.add))
            ac(lambda e: e.activation(out=sc(8), in_=sc(6), func=AF.Sin))
            dv(lambda e: e.tensor_tensor(out=sc(9), in0=sc(3), in1=sc(8), op=ALU.mult))
            dv(lambda e: e.tensor_scalar(out=sc(9), in0=sc(9), scalar1=-1.0, scalar2=None, op0=ALU.add))
            dv(lambda e: e.tensor_tensor(out=sc(10), in0=sc(3), in1=sc(7), op=ALU.mult))
            dv(lambda e: e.tensor_tensor(out=sc(11), in0=lr, in1=lr, op=ALU.mult))
            dv(lambda e: e.scalar_tensor_tensor(out=sc(11), in0=li, scalar=li, in1=sc(11), op0=ALU.mult, op1=ALU.add))
            dv(lambda e: e.reciprocal(out=sc(11), in_=sc(11)))
            dv(lambda e: e.tensor_tensor(out=sc(14), in0=sc(9), in1=lr, op=ALU.mult))
            dv(lambda e: e.scalar_tensor_tensor(out=sc(14), in0=sc(10), scalar=li, in1=sc(14), op0=ALU.mult, op1=ALU.add))
            dv(lambda e: e.tensor_tensor(out=sc(12), in0=sc(14), in1=sc(11), op=ALU.mult))
            dv(lambda e: e.tensor_tensor(out=sc(15), in0=sc(9), in1=li, op=ALU.mult))
            dv(lambda e: e.scalar_tensor_tensor(out=sc(15), in0=sc(10), scalar=lr, in1=sc(15), op0=ALU.mult, op1=ALU.subtract))
            dv(lambda e: e.tensor_tensor(out=sc(13), in0=sc(15), in1=sc(11), op=ALU.mult))
            dv(lambda e: e.tensor_scalar(out=d["BBR"][:], in0=d["BIM"][:], scalar1=sc(13), scalar2=None, op0=ALU.mult))
            dv(lambda e: e.scalar_tensor_tensor(out=d["BBR"][:], in0=d["BRE"][:], scalar=sc(12), in1=d["BBR"][:], op0=ALU.mult, op1=ALU.subtract))
            dv(lambda e: e.tensor_scalar(out=d["BBI"][:], in0=d["BRE"][:], scalar1=sc(13), scalar2=None, op0=ALU.mult))
            dv(lambda e: e.scalar_tensor_tensor(out=d["BBI"][:], in0=d["BIM"][:], scalar=sc(12), in1=d["BBI"][:], op0=ALU.mult, op1=ALU.add))
            dv(lambda e: e.tensor_scalar(out=d["CIMN"][:], in0=d["CIMN"][:], scalar1=-1.0, scalar2=None, op0=ALU.mult))
            for src, dst in (("BBR", "BRT"), ("BBI", "BIT")):
                p = psb[0]
                k.op("pe", lambda e: e.transpose(out=p[0:64, 0:128], in_=d[src][:], identity=ident[:]), reads=[P, "ident"], writes=["psb0"])
                k.op("dve", lambda e: e.tensor_copy(out=d[dst][:], in_=p[0:64, 0:128]), reads=["psb0", P], writes=[P])
            COS, SIN, TMP = d["COS"], d["SIN"], d["TMP"]
            dv(lambda e: e.memset(COS[:, 0:1], 1.0))
            dv(lambda e: e.memset(SIN[:, 0:1], 0.0))
            dv(lambda e: e.tensor_copy(out=sc(16), in_=sc(8)))
            dv(lambda e: e.tensor_copy(out=sc(17), in_=sc(7)))
            m = 1
            while m < 512:
                dv(lambda e: e.tensor_scalar(out=TMP[:, 0:m], in0=SIN[:, 0:m], scalar1=sc(17), scalar2=None, op0=ALU.mult))
                dv(lambda e: e.scalar_tensor_tensor(out=COS[:, m:2 * m], in0=COS[:, 0:m], scalar=sc(16), in1=TMP[:, 0:m],
                                                    op0=ALU.mult, op1=ALU.subtract))
                dv(lambda e: e.tensor_scalar(out=TMP[:, 0:m], in0=COS[:, 0:m], scalar1=sc(17), scalar2=None, op0=ALU.mult))
                dv(lambda e: e.scalar_tensor_tensor(out=SIN[:, m:2 * m], in0=SIN[:, 0:m], scalar=sc(16), in1=TMP[:, 0:m],
                                                    op0=ALU.mult, op1=ALU.add))
                dv(lambda e: e.tensor_tensor(out=sc(18), in0=sc(17), in1=sc(17), op=ALU.mult))
                dv(lambda e: e.tensor_tensor(out=sc(19), in0=sc(16), in1=sc(17), op=ALU.mult))
                dv(lambda e: e.scalar_tensor_tensor(out=sc(16), in0=sc(16), scalar=sc(16), in1=sc(18), op0=ALU.mult, op1=ALU.subtract))
                dv(lambda e: e.tensor_scalar(out=sc(17), in0=sc(19), scalar1=2.0, scalar2=None, op0=ALU.mult))
                m *= 2
            dv(lambda e: e.memset(d["RHO"][:], 1.0))
            dv(lambda e: e.tensor_scalar(out=d["RHO"][:], in0=d["RHO"][:], scalar1=sc(3), scalar2=None, op0=ALU.mult))
            for nm in ("T1", "T2", "VR", "VI", "WR", "WI", "XR", "XI"):
                d[nm] = k.sb("%s%d" % (nm, rt), [128, 512])
            d["INIT"] = k.sb("INIT%d" % rt, [128, 4])
            dv(lambda e: e.memset(d["INIT"][:], 0.0))
            R.append(d)

        for ch in range(16):
            cs = slice(ch * 512, (ch + 1) * 512)
            py = psb[6]
            for rt in range(2):
                d = R[rt]
                P = "P%d" % rt
                W = "W%d" % rt
                COS, SIN = d["COS"], d["SIN"]
                pr, pi_ = psb[2 * rt], psb[2 * rt + 1]
                prk, pik = "psb%d" % (2 * rt), "psb%d" % (2 * rt + 1)
                k.op("pe", lambda e: e.matmul(pr[:], lhsT=d["BRT"][:], rhs=UT[:, cs], start=True, stop=True), reads=[P, "UT"], writes=[prk])
                k.op("pe", lambda e: e.matmul(pi_[:], lhsT=d["BIT"][:], rhs=UT[:, cs], start=True, stop=True), reads=[P, "UT"], writes=[pik])
                k.op("dve", lambda e: e.tensor_tensor(out=d["T1"][:], in0=pr[:], in1=COS[:], op=ALU.mult), reads=[prk, P], writes=[W + "T1"])
                k.op("dve", lambda e: e.tensor_tensor(out=d["T2"][:], in0=pi_[:], in1=SIN[:], op=ALU.mult), reads=[pik, P], writes=[W + "T2"])
                k.op("pool", lambda e: e.tensor_tensor(out=d["VR"][:], in0=d["T1"][:], in1=d["T2"][:], op=ALU.add),
                     reads=[W + "T1", W + "T2"], writes=[W + "VR"])
                k.op("dve", lambda e: e.tensor_tensor(out=d["T1"][:], in0=pi_[:], in1=COS[:], op=ALU.mult), reads=[pik, P], writes=[W + "T1"])
                k.op("dve", lambda e: e.tensor_tensor(out=d["T2"][:], in0=pr[:], in1=SIN[:], op=ALU.mult), reads=[prk, P], writes=[W + "T2"])
                k.op("pool", lambda e: e.tensor_tensor(out=d["VI"][:], in0=d["T1"][:], in1=d["T2"][:], op=ALU.subtract),
                     reads=[W + "T1", W + "T2"], writes=[W + "VI"])
                k.op("dve", lambda e: e.tensor_tensor_scan(out=d["WR"][:], data0=d["RHO"][:], data1=d["VR"][:], initial=d["INIT"][:, 0:1],
                                                           op0=ALU.mult, op1=ALU.add), reads=[P, W + "VR", W + "INIT"], writes=[W + "WR"])
                k.op("dve", lambda e: e.tensor_tensor_scan(out=d["WI"][:], data0=d["RHO"][:], data1=d["VI"][:], initial=d["INIT"][:, 1:2],
                                                           op0=ALU.mult, op1=ALU.add), reads=[P, W + "VI", W + "INIT"], writes=[W + "WI"])
                k.op("pool", lambda e: e.tensor_tensor(out=d["VR"][:], in0=d["WR"][:], in1=COS[:], op=ALU.mult), reads=[W + "WR", P], writes=[W + "VR"])
                k.op("pool", lambda e: e.tensor_tensor(out=d["VI"][:], in0=d["WI"][:], in1=SIN[:], op=ALU.mult), reads=[W + "WI", P], writes=[W + "VI"])
                k.op("pool", lambda e: e.tensor_tensor(out=d["XR"][:], in0=d["VR"][:], in1=d["VI"][:], op=ALU.subtract),
                     reads=[W + "VR", W + "VI"], writes=[W + "XR"])
                k.op("pool", lambda e: e.tensor_tensor(out=d["VR"][:], in0=d["WR"][:], in1=SIN[:], op=ALU.mult), reads=[W + "WR", P], writes=[W + "VR"])
                k.op("pool", lambda e: e.tensor_tensor(out=d["VI"][:], in0=d["WI"][:], in1=COS[:], op=ALU.mult), reads=[W + "WI", P], writes=[W + "VI"])
                k.op("pool", lambda e: e.tensor_tensor(out=d["XI"][:], in0=d["VR"][:], in1=d["VI"][:], op=ALU.add),
                     reads=[W + "VR", W + "VI"], writes=[W + "XI"])
                S = d["S"]
                k.op("dve", lambda e: e.tensor_tensor(out=d["INIT"][:, 2:3], in0=d["XI"][:, 511:512], in1=S[:, 7:8], op=ALU.mult),
                     reads=[W + "XI", P, W + "INIT"], writes=[W + "INIT"])
                k.op("dve", lambda e: e.scalar_tensor_tensor(out=d["INIT"][:, 0:1], in0=d["XR"][:, 511:512], scalar=S[:, 8:9],
                                                             in1=d["INIT"][:, 2:3], op0=ALU.mult, op1=ALU.subtract),
                     reads=[W + "XR", P, W + "INIT"], writes=[W + "INIT"])
                k.op("dve", lambda e: e.tensor_tensor(out=d["INIT"][:, 2:3], in0=d["XR"][:, 511:512], in1=S[:, 7:8], op=ALU.mult),
                     reads=[W + "XR", P, W + "INIT"], writes=[W + "INIT"])
                k.op("dve", lambda e: e.scalar_tensor_tensor(out=d["INIT"][:, 1:2], in0=d["XI"][:, 511:512], scalar=S[:, 8:9],
                                                             in1=d["INIT"][:, 2:3], op0=ALU.mult, op1=ALU.add),
                     reads=[W + "XI", P, W + "INIT"], writes=[W + "INIT"])
                k.op("pe", lambda e: e.matmul(py[0:64, :], lhsT=d["CRE"][:], rhs=d["XR"][:], start=(rt == 0), stop=False),
                     reads=[P, W + "XR"], writes=["psb6"])
                k.op("pe", lambda e: e.matmul(py[0:64, :], lhsT=d["CIMN"][:], rhs=d["XI"][:], start=False, stop=(rt == 1)),
                     reads=[P, W + "XI"], writes=["psb6"])
            k.op("dve", lambda e: e.scalar_tensor_tensor(out=YT[:, cs], in0=UT[:, cs], scalar=DV[:, 0:1], in1=py[0:64, :],
                                                         op0=ALU.mult, op1=ALU.add), reads=["UT", "DV", "psb6"], writes=["YT%d" % ch])
            wr2(0, 64, ch * 512, YT[:, cs], ["YT%d" % ch])


def build_sb(k, ldg, wr2, qT3, kT3, vT3, psb, ptb):
    with k.scope():
        QB = k.sb("QB", [64, T], BF16)
        KB = k.sb("KB", [64, T], BF16)
        VB = k.sb("VB", [128, 64, 64], BF16)
        U = k.sb("U", [128, 128])
        ONESF = k.sb("ONESF", [128, 128])
        VT = k.sb("VT", [64, T], BF16)
        identb = k.sb("identb_sb", [128, 128], BF16)
        k.op("pool", lambda e: e.memset(identb[:], 1.0), writes=["identb"])
        k.op("pool", lambda e: e.affine_select(out=identb[:], in_=identb[:], pattern=[[-1, 128]], compare_op=ALU.is_equal,
                                                fill=k.fill(0.0), base=0, channel_multiplier=1), reads=["identb"], writes=["identb"])
        ldg(QB, qT3, ["G1"], ["QB"])
        ldg(KB, kT3, ["G1"], ["KB"])
        ldg(VT, vT3, ["G1"], ["VT"])
        tok_major(k, VT, VB, 64, identb, ptb)
        k.op("pool", lambda e: e.memset(ONESF[:], 1.0), writes=["ONESF"])
        k.op("pool", lambda e: e.memset(U[:], 1.0), writes=["U"])
        k.op("pool", lambda e: e.affine_select(out=U[:], in_=U[:], pattern=[[-1, 128]], compare_op=ALU.is_gt,
                                                fill=k.fill(0.0), base=0, channel_multiplier=1), reads=["U"], writes=["U"])
        UC = k.sb("UC", [128, 128])
        k.op("pool", lambda e: e.memset(UC[:], 1.0), writes=["UC"])
        k.op("pool", lambda e: e.affine_select(out=UC[:], in_=UC[:], pattern=[[1, 128]], compare_op=ALU.is_ge,
                                                fill=k.fill(0.0), base=0, channel_multiplier=-1), reads=["UC"], writes=["UC"])
        NB = 2
        E = [[k.sb("E%d_%d" % (s, i), [128, 512]) for i in range(NB)] for s in range(2)]
        SP = [[k.sb("SP%d_%d" % (s, i), [128, 512]) for i in range(NB)] for s in range(2)]
        B = [[k.sb("B%d_%d" % (s, i), [128, 512]) for i in range(NB)] for s in range(2)]
        WW = [[k.sb("WW%d_%d" % (s, i), [128, 512], BF16) for i in range(NB)] for s in range(2)]
        OTs = [k.sb("OTs%d" % s, [64, 512]) for s in range(2)]
        steps = [[], []]
        for Q in range(15, -1, -1):
            s = Q % 2
            kmax = 4 * Q + 3
            for kt in range(kmax, -1, -1):
                steps[s].append((Q, kt, kt == kmax, kt == 0))
        cnts = [0, 0]

        def ctx(s, Q, kt, first, last):
            b = cnts[s] % NB
            cnts[s] += 1
            return dict(s=s, Q=Q, kt=kt, first=first, last=last, t0=Q * 512, b=b, diag=(kt >= 4 * Q))

        def ph1(c):
            s, b, kt, t0 = c["s"], c["b"], c["kt"], c["t0"]
            pz, pzk = psb[s], "psb%d" % s
            e_, sp = E[s][b], SP[s][b]
            ek, spk = "E%d_%d" % (s, b), "SP%d_%d" % (s, b)
            k.op("pe", lambda e: e.matmul(pz[:], lhsT=KB[:, kt * 128:(kt + 1) * 128], rhs=QB[:, t0:t0 + 512], start=True, stop=True),
                 reads=["KB", "QB"], writes=[pzk])
            k.op("act", lambda e: e.activation(out=e_[:], in_=pz[:], func=AF.Exp, scale=SCALE), reads=[pzk], writes=[ek])
            k.op("act", lambda e: e.activation(out=sp[:], in_=e_[:], func=AF.Ln, bias=1.0), reads=[ek], writes=[spk])
            if c["diag"]:
                k.op("pool", lambda e: e.affine_select(out=sp[:], in_=sp[:], pattern=[[1, 512]], compare_op=ALU.is_gt, fill=k.fill(0.0),
                                                        base=t0 - 128 * kt, channel_multiplier=-1), reads=[spk], writes=[spk])

        def ph2(c):
            s, b = c["s"], c["b"]
            pz, pzk = psb[s], "psb%d" % s
            pRL, prk = psb[2 + s], "psb%d" % (2 + s)
            sp, bb = SP[s][b], B[s][b]
            spk, bk = "SP%d_%d" % (s, b), "B%d_%d" % (s, b)
            k.op("pe", lambda e: e.matmul(pRL[:], lhsT=U[:], rhs=sp[:], start=c["first"], stop=False), reads=["U", spk], writes=[prk])
            k.op("dve", lambda e: e.scalar_tensor_tensor(out=bb[:], in0=pz[:], scalar=SCALE, in1=sp[:], op0=ALU.mult, op1=ALU.subtract),
                 reads=[pzk, spk], writes=[bk])
            k.op("dve", lambda e: e.tensor_tensor(out=bb[:], in0=bb[:], in1=pRL[:], op=ALU.subtract), reads=[bk, prk], writes=[bk])

        def ph3(c):
            s, b, kt, t0 = c["s"], c["b"], c["kt"], c["t0"]
            pRL, prk = psb[2 + s], "psb%d" % (2 + s)
            sp, bb, ww = SP[s][b], B[s][b], WW[s][b]
            spk, bk, wk = "SP%d_%d" % (s, b), "B%d_%d" % (s, b), "WW%d_%d" % (s, b)
            if not c["last"]:
                k.op("pe", lambda e: e.matmul(pRL[:], lhsT=UC[:], rhs=sp[:], start=False, stop=(kt == 1)), reads=["UC", spk], writes=[prk])
            k.op("act", lambda e: e.activation(out=ww[:], in_=bb[:], func=AF.Exp), reads=[bk], writes=[wk])
            if c["diag"]:
                k.op("pool", lambda e: e.affine_select(out=ww[:], in_=ww[:], pattern=[[1, 512]], compare_op=ALU.is_gt, fill=k.fill(0.0),
                                                        base=t0 - 128 * kt, channel_multiplier=-1), reads=[wk], writes=[wk])

        def ph4(c):
            s, b, kt, t0 = c["s"], c["b"], c["kt"], c["t0"]
            pO, pok = psb[4 + s], "psb%d" % (4 + s)
            ww, wk = WW[s][b], "WW%d_%d" % (s, b)
            k.op("pe", lambda e: e.matmul(pO[0:64, :], lhsT=VB[:, kt, :], rhs=ww[:], start=c["first"], stop=c["last"]), reads=["VB", wk], writes=[pok])
            if c["last"]:
                k.op("act", lambda e: e.activation(out=OTs[s][:], in_=pO[0:64, :], func=AF.Copy), reads=[pok], writes=["OTs%d" % s])
                wr2(64, 128, t0, OTs[s][:], ["OTs%d" % s])

        for i in range(max(len(steps[0]), len(steps[1]))):
            cs = [ctx(s, *steps[s][i]) for s in range(2) if i < len(steps[s])]
            for ph in (ph1, ph2, ph3, ph4):
                for c in cs:
                    ph(c)


DMIN = -2064
ND = 5120
FAR_D = 790


def build_nsa(k, ldg, wr2, nq_own3, nq_oth3, kc3, vc3, ksl3, vsl3, ksw3, vsw3, g3, w1, w2, posT, tab31, ovv, eall, BT, psb, ptb):
    with k.scope():
        OUT = k.sb("OUT", [128, 64, 128])
        PENT = k.sb("PENT", [128, T], BF16)
        GS = k.sb("GS", [128, 64, 6])
        TAB31 = k.sb("TAB31", [128, 4])
        OVV = k.sb("OVV", [128, 4, 193], BF16)
        KCT = k.sb("KCT", [64, 512], BF16)
        identb = k.sb("identb", [128, 128], BF16)
        k.op("pool", lambda e: e.memset(identb[:], 1.0), writes=["identb"])
        k.op("pool", lambda e: e.affine_select(out=identb[:], in_=identb[:], pattern=[[-1, 128]], compare_op=ALU.is_equal,
                                                fill=k.fill(0.0), base=0, channel_multiplier=1), reads=["identb"], writes=["identb"])
        identf = k.sb("identf", [128, 128])
        k.op("pool", lambda e: e.memset(identf[:], 1.0), writes=["identf"])
        k.op("pool", lambda e: e.affine_select(out=identf[:], in_=identf[:], pattern=[[-1, 128]], compare_op=ALU.is_equal,
                                                fill=k.fill(0.0), base=0, channel_multiplier=1), reads=["identf"], writes=["identf"])
        with k.scope():
            GT = k.sb("GT", [6, T])
            ldg(GT, g3, ["G1"], ["GT"])
            k.op("act", lambda e: e.activation(out=GT[:], in_=GT[:], func=AF.Sigmoid), reads=["GT"], writes=["GT"])
            pg = psb[6]
            for kt in range(64):
                k.op("pe", lambda e: e.transpose(out=pg[:, kt * 6:(kt + 1) * 6], in_=GT[:, kt * 128:(kt + 1) * 128], identity=identf[0:6, 0:6]),
                     reads=["GT", "identf"], writes=["psb6"])
            k.op("act", lambda e: e.activation(out=GS[:].rearrange("p a b -> p (a b)"), in_=pg[:, 0:384], func=AF.Copy), reads=["psb6"], writes=["GS"])
        k.dma("sp", TAB31[:], tab31[0, :].partition_broadcast(128), writes=["TAB31"])
        k.dma("pool", OVV[:, :, 0:129], ovv[:, :, :], writes=["OVV"])
        with k.scope():
            XIN = k.sb("XIN", [64, T], BF16)
            W1 = k.sb("W1", [64, 2, 32, 128], BF16)
            W2 = k.sb("W2", [128, 2, 64], BF16)
            POST = k.sb("POST", [64, 2, 32], BF16)
            G = k.sb("G", [128, 512], BF16)
            PB = k.sb("PB", [128, 1])
            k.dma("pool", W1[:], w1[:, :, :, :], writes=["W1"])
            k.dma("pool", W2[:], w2[:, :, :], writes=["W2"])
            k.dma("pool", POST[:], posT[:, :, :], writes=["POST"])
            for j in range(2):
                ldg(XIN, (kc3 if j == 0 else vc3), ["G1"], ["XIN"])
                k.op("pool", lambda e: e.memset(G[:], 0.0), reads=["G"], writes=["G"])
                ph, pb = psb[0], psb[1]
                for l in range(32):
                    k.op("pe", lambda e: e.matmul(ph[:, 0:511], lhsT=W1[:, j, l, :], rhs=XIN[:, l:l + 16 * 510 + 1:16], start=(l == 0), stop=(l == 31)),
                         reads=["W1", "XIN"], writes=["psb0"])
                for l in range(32):
                    k.op("pe", lambda e: e.matmul(pb[:, 0:1], lhsT=W1[:, j, l, :], rhs=POST[:, j, l:l + 1], start=(l == 0), stop=(l == 31)),
                         reads=["W1", "POST"], writes=["psb1"])
                k.op("dve", lambda e: e.tensor_copy(out=PB[:], in_=pb[:, 0:1]), reads=["psb1"], writes=["PB"])
                k.op("act", lambda e: e.activation(out=G[:, 0:511], in_=ph[:, 0:511], func=AF.Gelu_apprx_tanh, bias=PB[:, 0:1]),
                     reads=["psb0", "PB"], writes=["G"])
                if j == 0:
                    p2 = psb[2]
                    k.op("pe", lambda e: e.matmul(p2[0:64, :], lhsT=W2[:, 0, :], rhs=G[:], start=True, stop=True), reads=["W2", "G"], writes=["psb2"])
                    k.op("act", lambda e: e.activation(out=KCT[:], in_=p2[0:64, :], func=AF.Copy), reads=["psb2"], writes=["KCT"])
                else:
                    for m in range(4):
                        p2, p2k = psb[2 + m], "psb%d" % (2 + m)
                        k.op("pe", lambda e: e.matmul(p2[:, 0:64], lhsT=G[:, m * 128:(m + 1) * 128], rhs=W2[:, 1, :], start=True, stop=True),
                             reads=["W2", "G"], writes=[p2k])
                        k.op("act", lambda e: e.activation(out=OVV[:, m, 129:193], in_=p2[:, 0:64], func=AF.Copy), reads=[p2k, "OVV"], writes=["OVV"])
        with k.scope():
            QB4 = k.sb("QB4", [64, 4, T], BF16)
            CB = k.sb("CB", [128, 6, 4, 512])
            IMP = k.sb("IMP", [128, 4, 128])
            SC2 = k.sb("SC2", [128, 128])
            M8 = k.sb("M8", [128, 8])
            M8b = k.sb("M8b", [128, 8])
            PENTOK = k.sb("PENTOK", [128, 128], BF16)
            ET = [[k.sb("ET%d_%d" % (a, m), [128, 512], BF16) for m in range(4)] for a in range(2)]
            LG = [k.sb("LGc%d" % a, [128, 512]) for a in range(2)]
            RZ = k.sb("RZc", [128, 2])
            for hh in range(2):
                ldg(QB4[:, hh, :], nq_own3[hh], ["G1"], ["QB4"])
                ldg(QB4[:, 2 + hh, :], nq_oth3[hh], ["G1"], ["QB4"])
            for j in range(6):
                for hj in range(4):
                    bi = j * 4 + hj
                    k.dma("sp", CB[:, j, hj, :], BT[bi * 128:(bi + 1) * 128, :], reads=["BT"], writes=["CB"])
            lgc = 0
            pc = 0
            for Q in range(16):
                t0 = Q * 512
                mlist = list(range(0, Q // 4 + 1))
                for hj in range(4):
                    a = hj % 2
                    for m in mlist:
                        j = Q - 4 * m
                        p, pk = psb[pc % 2], "psb%d" % (pc % 2)
                        pc += 1
                        et, etk = ET[a][m], "ET%d_%d" % (a, m)
                        k.op("pe", lambda e: e.matmul(p[:], lhsT=KCT[:, m * 128:(m + 1) * 128], rhs=QB4[:, hj, t0:t0 + 512], start=True, stop=True),
                             reads=["KCT", "QB4"], writes=[pk])
                        if j <= 5:
                            lg, lgk = LG[lgc % 2], "LGc%d" % (lgc % 2)
                            lgc += 1
                            k.op("dve", lambda e: e.scalar_tensor_tensor(out=lg[:], in0=p[:], scalar=SCALE, in1=CB[:, j, hj, :], op0=ALU.mult, op1=ALU.add),
                                 reads=[pk, "CB"], writes=[lgk])
                            k.op("act", lambda e: e.activation(out=et[:], in_=lg[:], func=AF.Exp), reads=[lgk], writes=[etk])
                        else:
                            k.op("act", lambda e: e.activation(out=et[:], in_=p[:], func=AF.Exp, scale=SCALE, bias=TAB31[:, hj:hj + 1]),
                                 reads=[pk, "TAB31"], writes=[etk])
                    W = 193 if hj < 2 else 129
                    for sub in range(4):
                        tt = 4 * Q + sub
                        pI, pIk = psb[2 + sub], "psb%d" % (2 + sub)
                        for mi, m in enumerate(mlist):
                            k.op("pe", lambda e: e.matmul(pI[:, 0:W], lhsT=ET[a][m][:, sub * 128:(sub + 1) * 128], rhs=OVV[:, m, 0:W],
                                                          start=(mi == 0), stop=(mi == len(mlist) - 1)),
                                 reads=["ET%d_%d" % (a, m), "OVV"], writes=[pIk])
                        k.op("dve", lambda e: e.tensor_scalar(out=RZ[:, 0:1], in0=pI[:, 128:129], scalar1=1e-30, scalar2=None, op0=ALU.max),
                             reads=[pIk, "RZc"], writes=["RZc"])
                        k.op("dve", lambda e: e.reciprocal(out=RZ[:, 0:1], in_=RZ[:, 0:1]), reads=["RZc"], writes=["RZc"])
                        if hj == 0:
                            k.op("dve", lambda e: e.tensor_scalar(out=IMP[:, sub, :], in0=pI[:, 0:128], scalar1=RZ[:, 0:1], scalar2=None, op0=ALU.mult),
                                 reads=[pIk, "RZc", "IMP%d" % sub], writes=["IMP%d" % sub])
                        else:
                            k.op("dve", lambda e: e.scalar_tensor_tensor(out=IMP[:, sub, :], in0=pI[:, 0:128], scalar=RZ[:, 0:1], in1=IMP[:, sub, :],
                                                                         op0=ALU.mult, op1=ALU.add),
                                 reads=[pIk, "RZc", "IMP%d" % sub], writes=["IMP%d" % sub])
                        if hj < 2:
                            k.op("dve", lambda e: e.tensor_tensor(out=RZ[:, 1:2], in0=RZ[:, 0:1], in1=GS[:, tt, hj * 3:hj * 3 + 1], op=ALU.mult),
                                 reads=["RZc", "GS"], writes=["RZc"])
                            k.op("dve", lambda e: e.tensor_scalar(out=OUT[:, tt, hj * 64:(hj + 1) * 64], in0=pI[:, 129:193], scalar1=RZ[:, 1:2],
                                                                  scalar2=None, op0=ALU.mult), reads=[pIk, "RZc"], writes=["OUT%d" % tt])
                for sub in range(4):
                    tb = 128 * (4 * Q + sub)
                    ik = "IMP%d" % sub
                    sc_ = IMP[:, sub, :]
                    k.op("pool", lambda e: e.affine_select(out=sc_, in_=sc_, pattern=[[-64, 128]], compare_op=ALU.is_ge, fill=k.fill(1e4),
                                                            base=tb - 128, channel_multiplier=1), reads=[ik], writes=[ik])
                    k.op("pool", lambda e: e.affine_select(out=sc_, in_=sc_, pattern=[[-64, 128]], compare_op=ALU.is_ge, fill=k.fill(-1.0),
                                                            base=tb, channel_multiplier=1), reads=[ik], writes=[ik])
                    k.op("pool", lambda e: e.memset(IMP[:, sub, 0:1], 1e4), reads=[ik], writes=[ik])
                    k.op("dve", lambda e: e.max(out=M8[:], in_=sc_), reads=[ik], writes=["M8"])
                    k.op("dve", lambda e: e.match_replace(out=SC2[:], in_to_replace=M8[:], in_values=sc_, imm_value=-1e9),
                         reads=[ik, "M8"], writes=["SC2"])
                    k.op("dve", lambda e: e.max(out=M8b[:], in_=SC2[:]), reads=["SC2"], writes=["M8b"])
                    k.op("dve", lambda e: e.tensor_scalar(out=PENTOK[:], in0=sc_, scalar1=M8b[:, 7:8], scalar2=1.0, op0=ALU.is_ge, op1=ALU.subtract),
                         reads=[ik, "M8b"], writes=["PENTOK"])
                    k.op("pe", lambda e: e.transpose(out=ptb[:, 0:128], in_=PENTOK[:], identity=identb[:]), reads=["PENTOK", "identb"], writes=["ptb"])
                    k.op("act", lambda e: e.activation(out=PENT[:, tb:tb + 128], in_=ptb[:, 0:128], func=AF.Copy), reads=["ptb"], writes=["PENT%d" % Q])
        with k.scope():
            KSLT = k.sb("KSLT", [64, T], BF16)
            KSWT = k.sb("KSWT", [64, T], BF16)
            VSL = k.sb("VSL", [128, 64, 65], BF16)
            VSW = k.sb("VSW", [128, 64, 65], BF16)
            EALL = k.sb("EALL", [128, T], BF16)
            LG = [k.sb("LGs%d" % a, [128, 512]) for a in range(2)]
            EX = [k.sb("EX%d" % a, [128, 512], BF16) for a in range(2)]
            RZ = k.sb("RZs", [128, 2])
            ldg(KSLT, ksl3, ["G1"], ["KSLT"])
            ldg(KSWT, ksw3, ["G1"], ["KSWT"])
            k.op("pool", lambda e: e.memset(VSL[:, :, 64:65], 1.0), writes=["VSL"])
            k.op("pool", lambda e: e.memset(VSW[:, :, 64:65], 1.0), writes=["VSW"])
            with k.scope():
                VT = k.sb("VTn", [64, T], BF16)
                ldg(VT, vsl3, ["G1"], ["VT"])
                tok_major(k, VT, VSL, 65, identb, ptb, "VSL")
                ldg(VT, vsw3, ["G1", "VT"], ["VT"])
                tok_major(k, VT, VSW, 65, identb, ptb, "VSW")
            k.dma("pool", EALL[:], eall[:, :], writes=["EALL"])
            for hj in range(2):
                with k.scope():
                    QBh = k.sb("QBh", [64, T], BF16)
                    SELB = k.sb("SELB", [128, 11, 512])
                    WINB = k.sb("WINB", [128, 8, 512])
                    ldg(QBh, nq_own3[hj], ["G1"], ["QBh"])
                    for di in range(11):
                        bi = 24 + hj * 11 + di
                        k.dma("sp", SELB[:, di, :], BT[bi * 128:(bi + 1) * 128, :], reads=["BT"], writes=["SELB"])
                    for jw in range(8):
                        bi = 46 + hj * 8 + jw
                        k.dma("sp", WINB[:, jw, :], BT[bi * 128:(bi + 1) * 128, :], reads=["BT"], writes=["WINB"])
                    cnt = 0
                    for Q in range(16):
                        t0 = Q * 512
                        pend = []
                        for kt in range(0, 4 * Q + 4):
                            D0 = 512 * Q - 128 * kt
                            p, pk = psb[cnt % 2], "psb%d" % (cnt % 2)
                            lg, lgk = LG[cnt % 2], "LGs%d" % (cnt % 2)
                            ex, exk = EX[cnt % 2], "EX%d" % (cnt % 2)
                            cnt += 1
                            k.op("pe", lambda e: e.matmul(p[:], lhsT=KSLT[:, kt * 128:(kt + 1) * 128], rhs=QBh[:, t0:t0 + 512], start=True, stop=False),
                                 reads=["KSLT", "QBh"], writes=[pk])
                            k.op("pe", lambda e: e.matmul(p[:], lhsT=EALL[:, kt * 128:(kt + 1) * 128], rhs=PENT[:, t0:t0 + 512], start=False, stop=True),
                                 reads=["EALL", "PENT%d" % Q], writes=[pk])
                            while pend:
                                pend.pop(0)()
                            if D0 <= 896:
                                di = (D0 + 384) // 128
                                k.op("dve", lambda e: e.scalar_tensor_tensor(out=lg[:], in0=p[:], scalar=SCALE, in1=SELB[:, di, :], op0=ALU.mult, op1=ALU.add),
                                     reads=[pk, "SELB"], writes=[lgk])
                                k.op("act", lambda e: e.activation(out=ex[:], in_=lg[:], func=AF.Exp), reads=[lgk], writes=[exk])
                            else:
                                k.op("act", lambda e: e.activation(out=ex[:], in_=p[:], func=AF.Exp, scale=SCALE, bias=TAB31[:, hj:hj + 1]),
                                     reads=[pk, "TAB31"], writes=[exk])
                            def pv_sel(kt=kt, ex=ex, exk=exk, Q=Q):
                                for sub in range(4):
                                    if kt <= 4 * Q + sub:
                                        k.op("pe", lambda e: e.matmul(psb[2 + sub][:, 0:65], lhsT=ex[:, sub * 128:(sub + 1) * 128], rhs=VSL[:, kt, :],
                                                                      start=(kt == 0), stop=(kt == 4 * Q + sub)),
                                             reads=[exk, "VSL"], writes=["psb%d" % (2 + sub)])
                            pv_sel()
                        while pend:
                            pend.pop(0)()
                        for sub in range(4):
                            tt = 4 * Q + sub
                            pa, pak = psb[2 + sub], "psb%d" % (2 + sub)
                            k.op("dve", lambda e: e.reciprocal(out=RZ[:, 0:1], in_=pa[:, 64:65]), reads=[pak, "RZs"], writes=["RZs"])
                            k.op("dve", lambda e: e.tensor_tensor(out=RZ[:, 1:2], in0=RZ[:, 0:1], in1=GS[:, tt, hj * 3 + 1:hj * 3 + 2], op=ALU.mult),
                                 reads=["RZs", "GS"], writes=["RZs"])
                            k.op("dve", lambda e: e.scalar_tensor_tensor(out=OUT[:, tt, hj * 64:(hj + 1) * 64], in0=pa[:, 0:64], scalar=RZ[:, 1:2],
                                                                         in1=OUT[:, tt, hj * 64:(hj + 1) * 64], op0=ALU.mult, op1=ALU.add),
                                 reads=[pak, "RZs", "OUT%d" % tt], writes=["OUT%d" % tt])
                        jw_min = 4 if Q == 0 else 0
                        for jw in range(jw_min, 8):
                            s0 = t0 - 512 + 128 * jw
                            kt = s0 // 128
                            p, pk = psb[cnt % 2], "psb%d" % (cnt % 2)
                            lg, lgk = LG[cnt % 2], "LGs%d" % (cnt % 2)
                            ex, exk = EX[cnt % 2], "EX%d" % (cnt % 2)
                            cnt += 1
                            k.op("pe", lambda e: e.matmul(p[:], lhsT=KSWT[:, kt * 128:(kt + 1) * 128], rhs=QBh[:, t0:t0 + 512], start=True, stop=True),
                                 reads=["KSWT", "QBh"], writes=[pk])
                            while len(pend) > 0:
                                pend.pop(0)()
                            k.op("dve", lambda e: e.scalar_tensor_tensor(out=lg[:], in0=p[:], scalar=SCALE, in1=WINB[:, jw, :], op0=ALU.mult, op1=ALU.add),
                                 reads=[pk, "WINB"], writes=[lgk])
                            k.op("act", lambda e: e.activation(out=ex[:], in_=lg[:], func=AF.Exp), reads=[lgk], writes=[exk])
                            def pv_win(kt=kt, ex=ex, exk=exk, jw=jw, jw_min=jw_min):
                                for sub in range(4):
                                    first = max(sub, jw_min)
                                    if first <= jw <= sub + 4:
                                        k.op("pe", lambda e: e.matmul(psb[2 + sub][:, 0:65], lhsT=ex[:, sub * 128:(sub + 1) * 128], rhs=VSW[:, kt, :],
                                                                      start=(jw == first), stop=(jw == sub + 4)),
                                             reads=[exk, "VSW"], writes=["psb%d" % (2 + sub)])
                            pv_win()
                        while pend:
                            pend.pop(0)()
                        for sub in range(4):
                            tt = 4 * Q + sub
                            pa, pak = psb[2 + sub], "psb%d" % (2 + sub)
                            k.op("dve", lambda e: e.reciprocal(out=RZ[:, 0:1], in_=pa[:, 64:65]), reads=[pak, "RZs"], writes=["RZs"])
                            k.op("dve", lambda e: e.tensor_tensor(out=RZ[:, 1:2], in0=RZ[:, 0:1], in1=GS[:, tt, hj * 3 + 2:hj * 3 + 3], op=ALU.mult),
                                 reads=["RZs", "GS"], writes=["RZs"])
                            k.op("dve", lambda e: e.scalar_tensor_tensor(out=OUT[:, tt, hj * 64:(hj + 1) * 64], in0=pa[:, 0:64], scalar=RZ[:, 1:2],
                                                                         in1=OUT[:, tt, hj * 64:(hj + 1) * 64], op0=ALU.mult, op1=ALU.add),
                                 reads=[pak, "RZs", "OUT%d" % tt], writes=["OUT%d" % tt])
        YN = [k.sb("YN%d" % a, [128, 512]) for a in range(2)]
        for Q in range(16):
            p, pk = psb[Q % 2], "psb%d" % (Q % 2)
            for sub in range(4):
                tt = 4 * Q + sub
                k.op("pe", lambda e: e.transpose(out=p[:, sub * 128:(sub + 1) * 128], in_=OUT[:, tt, :], identity=identf[:]),
                     reads=["OUT%d" % tt, "identf"], writes=[pk])
            yn, ynk = YN[Q % 2], "YN%d" % (Q % 2)
            k.op("act", lambda e: e.activation(out=yn[:], in_=p[:], func=AF.Copy), reads=[pk], writes=[ynk])
            wr2(128, 256, Q * 512, yn[:], [ynk])


def tok_major(k, VT, VB, W, identb, ptb, key="VB"):
    for g8 in range(8):
        for q in range(8):
            kt = g8 * 8 + q
            k.op("pe", lambda e: e.transpose(out=ptb[:, q * 64:(q + 1) * 64], in_=VT[:, kt * 128:(kt + 1) * 128], identity=identb[0:64, 0:64]),
                 reads=["VT", "identb"], writes=["ptb"])
        k.op("act", lambda e: e.activation(out=VB[:, g8 * 8:(g8 + 1) * 8, 0:64], in_=ptb[:, 0:512].rearrange("p (q d) -> p q d", q=8), func=AF.Copy),
             reads=["ptb", key], writes=[key])


THR = [0, 1, 2, 3, 4, 5, 6, 7, 8, 9, 10, 11, 12, 13, 14, 15, 16, 21, 27, 35, 46, 59, 77, 99, 128, 166, 216, 280, 363, 470, 609, 790]


def bias_setup(k, tabx, OH, OHW, BT, psb):
    fsd = k.dram_tmp("fsd", [4, ND])
    fwd = k.dram_tmp("fwd", [4, ND])
    with k.scope():
        TABX = k.sb("TABX", [33, 4])
        OHS = k.sb("OHS", [33, ND])
        FSB = k.sb("FSB", [4, ND])
        JJ = k.sb("JJ", [128, 128])
        HK = [k.sb("HK%d" % i, [128, 512]) for i in range(2)]
        RV = [k.sb("RV%d" % i, [128, 512]) for i in range(2)]
        k.op("pool", lambda e: e.memset(JJ[:], 1.0), writes=["JJ"])
        k.op("pool", lambda e: e.affine_select(out=JJ[:], in_=JJ[:], pattern=[[1, 128]], compare_op=ALU.is_equal,
                                                fill=k.fill(0.0), base=-127, channel_multiplier=1), reads=["JJ"], writes=["JJ"])
        k.dma("sp", TABX[:], tabx[:, :], writes=["TABX"])
        for which, (src, dst) in enumerate(((OH, fsd), (OHW, fwd))):
            k.dma("sp", OHS[:], src[:, :], reads=["OHS"], writes=["OHS"])
            for c in range(ND // 512):
                p, pk = psb[c % 2], "psb%d" % (c % 2)
                k.op("pe", lambda e: e.matmul(p[0:4, :], lhsT=TABX[:], rhs=OHS[:, c * 512:(c + 1) * 512], start=True, stop=True),
                     reads=["TABX", "OHS"], writes=[pk])
                k.op("dve", lambda e: e.tensor_copy(out=FSB[:, c * 512:(c + 1) * 512], in_=p[0:4, :]), reads=[pk, "FSB"], writes=["FSB"])
            k.dma("sp", dst[:, :], FSB[:], reads=["FSB"], writes=["FD%d" % which])
        specs = []
        for j in range(6):
            for hj in range(4):
                specs.append((fsd, "FD0", hj * ND + (512 * j - 31 - 2032 - DMIN), 16, j * 4 + hj))
        for hj in range(2):
            for di in range(11):
                specs.append((fsd, "FD0", hj * ND + (-384 + 128 * di - 127 - DMIN), 1, 24 + hj * 11 + di))
            for jw in range(8):
                specs.append((fwd, "FD1", hj * ND + (512 - 128 * jw - 127 - DMIN), 1, 46 + hj * 8 + jw))
        for i, (src, sk, off, stride, bi) in enumerate(specs):
            hk, hkk = HK[i % 2], "HK%d" % (i % 2)
            rv, rvk = RV[i % 2], "RV%d" % (i % 2)
            p, pk = psb[2 + i % 2], "psb%d" % (2 + i % 2)
            k.dma("sp", hk[:], AP(tensor=src.tensor, offset=off, ap=[[stride, 128], [1, 512]]), reads=[sk], writes=[hkk])
            k.op("pe", lambda e: e.matmul(p[:], lhsT=JJ[:], rhs=hk[:], start=True, stop=True), reads=["JJ", hkk], writes=[pk])
            k.op("act", lambda e: e.activation(out=rv[:], in_=p[:], func=AF.Copy), reads=[pk], writes=[rvk])
            k.dma("sp", BT[bi * 128:(bi + 1) * 128, :], rv[:], reads=[rvk], writes=["BT"])


def build_fused(NL=4, debug=False, stage=None):
    k = K()
    nc = k.nc
    k.fill(0.0), k.fill(1e4), k.fill(-1.0)
    psb = [k.ps("psb%d" % i, [128, 512]) for i in range(7)]
    ptb = k.ps("ptb", [128, 1024], BF16)
    h_in = k.dram_in("h_in", [TOK, D])
    W = dict(w_in=k.dram_in("w_in", [NL, D, NIN]), w_glu=k.dram_in("w_glu", [NL, 512, 512]), b_glu=k.dram_in("b_glu", [NL, 128, 4]),
             w_out=k.dram_in("w_out", [NL, D, D]), ln_g=k.dram_in("ln_g", [NL, 3, D]), ln_b=k.dram_in("ln_b", [NL, 3, D]),
             memT=k.dram_in("memT", [D, 256]), xa_wq=k.dram_in("xa_wq", [NL, D, 512]), xa_wkv=k.dram_in("xa_wkv", [NL, D, 1024]),
             xa_wo=k.dram_in("xa_wo", [NL, 512, D]), w_up=k.dram_in("w_up", [NL, D, 2 * DFF]), convw=k.dram_in("convw", [NL, 128, 88, 3]),
             convb=k.dram_in("convb", [NL, 128, 88]), w_down=k.dram_in("w_down", [NL, DFF, D]), flag=k.dram_in("flag", [128, 1]))
    lam = k.dram_in("lam", [NL, 128, 2, 3])
    bre = k.dram_in("bre", [NL, 128, 2, 64])
    bim = k.dram_in("bim", [NL, 128, 2, 64])
    cre = k.dram_in("cre", [NL, 128, 2, 64])
    cim = k.dram_in("cim", [NL, 128, 2, 64])
    dvec = k.dram_in("dvec", [NL, 64, 1])
    cw1 = k.dram_in("cw1", [NL, 64, 2, 32, 128])
    cw2 = k.dram_in("cw2", [NL, 128, 2, 64])
    posT = k.dram_in("posT", [NL, 64, 2, 32])
    tabx = k.dram_in("tabx", [33, 4])
    tab31 = k.dram_in("tab31", [1, 4])
    OH = k.dram_in("OH", [33, ND])
    OHW = k.dram_in("OHW", [33, ND])
    ovv = k.dram_in("ovv", [128, 4, 129])
    eall = k.dram_in("eall", [128, T])
    h_out = k.dram_out("h_out", [1024, D])
    B1 = k.dram_tmp("B1", [4912, 1024])
    G1 = k.dram_tmp("G1", [8 * 4912, 1024])
    B2 = k.dram_tmp("B2", [256, 8 * TOK])
    G2 = k.dram_tmp("G2", [8 * 256, 8 * TOK])
    gidx = k.dram_in("gidx", [128, 145], U32)
    IDX = k.sb("IDX", [128, 145], U32)
    k.dma("sp", IDX[:], gidx[:, :], writes=["IDX"])
    B2v = B2.rearrange("f (w t) -> f w t", w=8)
    G1r = G1

    def ldg(tile, i, reads, writes):
        n = tile.shape[0]
        for r in range(8):
            k.gather(tile[:, r * 1024:(r + 1) * 1024], G1r[:, :], IDX[0:n, i * 8 + r:i * 8 + r + 1], reads=list(reads) + ["IDX"], writes=writes)

    def wr2(r0, r1, t0, src, reads):
        w = t0 // 1024
        off = 128 + (t0 % 1024)
        k.dma("sp", B2v[r0:r1, w, off:off + 512], src, reads=reads, writes=["B2"])
        if t0 % 1024 == 512 and w < 7:
            k.dma("sp", B2v[r0:r1, w + 1, 0:128], src[:, 384:512], reads=reads, writes=["B2"])
    HSAVE = k.dram_tmp("HSAVE", [TOK, D])
    BT = k.dram_tmp("BT", [62 * 128, 512])
    bias_setup(k, tabx, OH, OHW, BT, psb)
    with k.scope():
        ZZ = k.sb("ZZ", [128, 256])
        k.op("pool", lambda e: e.memset(ZZ[:], 0.0), writes=["ZZ"])
        k.dma("sp", B2v[0:128, 0, 0:128], ZZ[:, 0:128], reads=["ZZ"], writes=["B2"])
        k.dma("sp", B2v[128:256, 0, 0:128], ZZ[:, 0:128], reads=["ZZ"], writes=["B2"])
    with k.scope():
        H = k.sb("H0s", [128, NT, D])
        HT = k.sb("HT0s", [128, 16, TOK], BF16)
        ident = k.sb("ident0", [128, 128])
        k.op("pool", lambda e: e.memset(ident[:], 1.0), writes=["ident"])
        k.op("pool", lambda e: e.affine_select(out=ident[:], in_=ident[:], pattern=[[-1, 128]], compare_op=ALU.is_equal,
                                                fill=k.fill(0.0), base=0, channel_multiplier=1), reads=["ident"], writes=["ident"])
        pc = 0
        for i in range(1, NT):
            k.dma("sp", H[:, i, :], h_in[i * 128:(i + 1) * 128, :], writes=["H%d" % i])
            for kb in range(4):
                p, pk = psb[pc % 7], "psb%d" % (pc % 7)
                pc += 1
                for q in range(4):
                    kk = kb * 4 + q
                    k.op("pe", lambda e: e.transpose(out=p[:, q * 128:(q + 1) * 128], in_=H[:, i, kk * 128:(kk + 1) * 128],
                                                     identity=ident[:]), reads=["H%d" % i, "ident"], writes=[pk])
                k.op("act", lambda e: e.activation(out=HT[:, kb * 4:(kb + 1) * 4, i * 128:(i + 1) * 128],
                                                   in_=p[:].rearrange("p (a b) -> p a b", a=4), func=AF.Copy), reads=[pk], writes=["HT%d" % i])

        def ht_keys(c0, c1):
            return ["HT%d" % i for i in range(c0 // 128, (c1 - 1) // 128 + 1)]
        k.dma("sp", B1[4656:4912, :].rearrange("(p a) t -> p (a t)", a=2), H[:, NT - 1, :], reads=["H%d" % (NT - 1)], writes=["B1"])
        proj_phase(k, HT, ht_keys, W["w_in"][0], B1, psb)
    k.collective("AllGather", B1, G1, reads=["B1"], writes=["G1"])
    if stage == 1:
        UTd = k.sb("UTd", [64, T])
        ldg(UTd, 0, ["G1"], ["UTd"])
        o1 = k.dram_out("dbg_ut", [64, T])
        k.dma("sp", o1[:, :], UTd[:], reads=["UTd"], is_output=True)
        HHd = k.sb("HHd", [128, D])
        k.gather(HHd[:], G1.rearrange("(a b) t -> a (b t)", b=2), IDX[:, 144:145], reads=["G1", "IDX"], writes=["HHd"])
        o2 = k.dram_out("dbg_hh", [128, D])
        k.dma("sp", o2[:, :], HHd[:], reads=["HHd"], is_output=True)
        k.finish()
        return k.nc
    dbg = {}
    for l in range(NL):
        build_s5(k, ldg, wr2, 0, lam[l], bre[l], bim[l], cre[l], cim[l], dvec[l], psb)
        build_sb(k, ldg, wr2, 1, 2, 3, psb, ptb)
        build_nsa(k, ldg, wr2, [4, 5], [6, 7], 8, 9, 10, 11, 12, 13, 14, cw1[l], cw2[l], posT[l], tab31, ovv, eall, BT, psb, ptb)
        if debug and l == 0:
            dbg["y"] = k.dram_out("dbg_y", [256, 8 * TOK])
            k.dma("sp", dbg["y"][:, :], B2[:, :], reads=["B2"], is_output=True)
            dbg["p"] = k.dram_out("dbg_p", [4912, 1024])
            k.dma("sp", dbg["p"][:, :], B1[:, :], reads=["B1"], is_output=True)
        if stage == 2:
            k.finish()
            return k.nc
        k.collective("AllGather", B2, G2, reads=["B2"], writes=["G2"])
        if stage == 3:
            CAd = k.sb("CAd", [128, TOK], BF16)
            CAf = k.sb("CAf", [128, TOK])
            k.gather(CAd[:], G2.rearrange("r (w t) -> (r w) t", w=8), IDX[:, 136:137], reads=["G2", "IDX"], writes=["CAd"])
            k.op("dve", lambda e: e.tensor_copy(out=CAf[:], in_=CAd[:]), reads=["CAd"], writes=["CAf"])
            o3 = k.dram_out("dbg_ca", [128, TOK])
            k.dma("sp", o3[:, :], CAf[:], reads=["CAf"], is_output=True)
            k.finish()
            return k.nc
        with k.scope():
            dense_layer(k, l, NL, W, h_in, G1, G2, B1, HSAVE, h_out, IDX, psb, stage=stage)
        if l < NL - 1:
            k.collective("AllGather", B1, G1, reads=["B1"], writes=["G1"])
    k.finish()
    return k.nc


def nsa_consts():
    d = DMIN + np.arange(ND)
    bucket = np.zeros(ND, np.int64)
    for kk in range(1, 32):
        bucket += (d >= THR[kk])
    OH = np.zeros((33, ND), np.float32)
    OHW = np.zeros((33, ND), np.float32)
    valid = d >= 0
    OH[bucket[valid], np.nonzero(valid)[0]] = 1.0
    OH[32, ~valid] = 1.0
    vw = (d >= 0) & (d < 512)
    OHW[bucket[vw], np.nonzero(vw)[0]] = 1.0
    OHW[32, ~vw] = 1.0
    n = np.arange(512)
    jj = np.arange(128)
    ov = ((16 * n[:, None] < 64 * (jj[None, :] + 1)) & (16 * n[:, None] + 31 >= 64 * jj[None, :]) & (n[:, None] <= 510)).astype(np.float32)
    ovv = np.concatenate([ov, (n[:, None] <= 510).astype(np.float32)], 1)
    ovv = np.ascontiguousarray(ovv.reshape(4, 128, 129).transpose(1, 0, 2))
    eall = np.zeros((128, 8192), np.float32)
    eall[np.arange(8192) // 64, np.arange(8192)] = 240000.0
    return dict(OH=OH, OHW=OHW, ovv=ovv, eall=eall)


def gather_indices(c):
    g, oth = c // 2, c ^ 1
    p = np.arange(128)
    idx = np.zeros((128, 145), np.int64)
    bases = [64 * c, 512 + 64 * c, 1024 + 64 * c, 1536 + 64 * c, 2048 + 128 * c, 2048 + 128 * c + 64, 2048 + 128 * oth, 2048 + 128 * oth + 64,
             3072 + 64 * g, 3328 + 64 * g, 3584 + 64 * g, 3840 + 64 * g, 4096 + 64 * g, 4352 + 64 * g, 4608 + 6 * c]
    for i, b in enumerate(bases):
        n = 6 if i == 14 else 64
        for r in range(8):
            idx[:n, i * 8 + r] = r * 4912 + b + p[:n]
    for kc in range(4):
        rank = 2 * kc + p // 64
        for tc in range(3):
            idx[:, 120 + kc * 3 + tc] = ((rank * 256 + p % 64) * 8 + c) * 3 + tc
        idx[:, 132 + kc] = (rank * 256 + 64 + p % 64) * 8 + c
    for r in range(8):
        idx[:, 136 + r] = (r * 256 + 128 + p) * 8 + c
    idx[:, 144] = (((c + 7) % 8) * 4912 + 4656) // 2 + p
    return idx.astype(np.uint32)


def halo_rows(a, c):
    if c == 0:
        return np.concatenate([np.zeros((128, a.shape[1]), a.dtype), a[0:1024]], 0)
    return a[1024 * c - 128:1024 * c + 1024]


def fused_in_maps(inp, NL):
    consts = nsa_consts()
    L = slice(0, NL)
    shared = {
        "w_in": inp["w_in"][L], "w_glu": inp["s5_w_glu"][L],
        "b_glu": np.ascontiguousarray(inp["s5_b_glu"][L].reshape(NL, 4, 128).transpose(0, 2, 1)),
        "w_out": inp["w_out"][L], "ln_g": inp["ln_g"][L], "ln_b": inp["ln_b"][L],
        "memT": np.ascontiguousarray(inp["mem"][0].T), "xa_wq": inp["xa_wq"][L], "xa_wkv": inp["xa_wkv"][L], "xa_wo": inp["xa_wo"][L],
        "w_up": inp["ffn_w_up"][L],
        "convw": np.ascontiguousarray(inp["ffn_conv_w"][L].reshape(NL, 3, 88, 128).transpose(0, 3, 2, 1)),
        "convb": np.ascontiguousarray(inp["ffn_conv_b"][L].reshape(NL, 88, 128).transpose(0, 2, 1)),
        "w_down": inp["ffn_w_down"][L],
        "cw1": np.ascontiguousarray(inp["nsa_cmp_w1"][L].reshape(NL, 2, 32, 64, 128).transpose(0, 3, 1, 2, 4)),
        "cw2": np.ascontiguousarray(inp["nsa_cmp_w2"][L].transpose(0, 2, 1, 3)),
        "posT": np.ascontiguousarray(inp["nsa_cmp_pos"][L].transpose(0, 3, 1, 2)),
    }
    shared.update(consts)
    maps = []
    x = inp["x"][0]
    for c in range(8):
        m = dict(shared)
        m["h_in"] = np.ascontiguousarray(halo_rows(x, c))
        m["flag"] = np.full((128, 1), 0.0 if c == 0 else 1.0, np.float32)
        lam = np.zeros((NL, 128, 2, 3), np.float32)
        bre = np.zeros((NL, 128, 2, 64), np.float32)
        bim = np.zeros_like(bre)
        cre = np.zeros_like(bre)
        cim = np.zeros_like(bre)
        for rt in range(2):
            for half in range(2):
                gl = 2 * rt + half
                g = 4 * c + gl
                rows = slice(half * 64, half * 64 + 64)
                lam[:, rows, rt, 0] = inp["s5_lambda_re"][L, g]
                lam[:, rows, rt, 1] = inp["s5_lambda_im"][L, g]
                lam[:, rows, rt, 2] = inp["s5_log_dt"][L, g][:, None]
                bre[:, rows, rt, gl * 16:gl * 16 + 16] = inp["s5_b_re"][L, g]
                bim[:, rows, rt, gl * 16:gl * 16 + 16] = inp["s5_b_im"][L, g]
                cre[:, rows, rt, gl * 16:gl * 16 + 16] = inp["s5_c_re"][L, g].transpose(0, 2, 1)
                cim[:, rows, rt, gl * 16:gl * 16 + 16] = inp["s5_c_im"][L, g].transpose(0, 2, 1)
        m.update(lam=lam, bre=bre, bim=bim, cre=cre, cim=cim)
        m["dvec"] = np.ascontiguousarray(inp["s5_d"][L, 64 * c:64 * c + 64].reshape(NL, 64, 1))
        heads = [2 * c, 2 * c + 1, 2 * (c ^ 1), 2 * (c ^ 1) + 1]
        tab = inp["rel_bias"][:, heads]
        m["tabx"] = np.ascontiguousarray(np.concatenate([tab, np.full((1, 4), -30000.0, np.float32)], 0))
        m["tab31"] = np.ascontiguousarray(tab[31:32, :])
        m["gidx"] = gather_indices(c)
        maps.append(m)
    return maps


_CACHE = {}


def kernel(**inputs):
    inp = {kk: np.ascontiguousarray(np.asarray(v, dtype=np.float32)) for kk, v in inputs.items()}
    if "nc" not in _CACHE:
        _CACHE["nc"] = build_fused(4)
    res = run_bass_kernel_spmd(_CACHE["nc"], fused_in_maps(inp, 4), core_ids=list(range(8)))
    h = np.concatenate([r["h_out"] for r in res.results], 0)
    return h[None].astype(np.float32)
```
